# Optimizing a Trainium2 kernel written in Bass

```python
import math
import jax, jax.numpy as jnp
from jax import lax
import numpy as np

D_MODEL = 2048
BATCH = 32
SEQ = 256
DEPTH = 1
DEC_BATCH = 4
DEC_SEQ = 4096
PAST_LEN = 512

GRID_W = 64
NA_HEADS = 16
NA_HEAD_DIM = 64
NA_WIDTH = NA_HEADS * NA_HEAD_DIM
WIN_R = 8
WIN_C = 16
S5_GROUP_CH = 16
S5_WIDTH = 1024
S5_GROUPS = S5_WIDTH // S5_GROUP_CH
S5_STATE = 64
D_FF = 5632
N_MOD = 9
Q_BLOCK = 128
EPS = 1e-6
IN_COLS = 3 * NA_WIDTH + S5_WIDTH + 2 * D_MODEL

kernel_name = 'hybrid_natten_s5_macaron_prefix_step'


def rms_norm(x, g):
    x32 = x.astype(jnp.float32)
    y = x32 * lax.rsqrt(jnp.mean(x32 * x32, axis=-1, keepdims=True) + EPS)
    return (y * g.astype(jnp.float32)).astype(x.dtype)


def modulate(h, shift, scale):
    return h * (1 + scale) + shift


def swiglu(h, w_in, w_out):
    g, u = jnp.split(h @ w_in, 2, axis=-1)
    return (jax.nn.silu(g) * u) @ w_out


def split_heads(t):
    b, l, _ = t.shape
    return t.reshape(b, l, NA_HEADS, NA_HEAD_DIM).transpose(0, 2, 1, 3)


def merge_heads(t):
    b, h, l, d = t.shape
    return t.transpose(0, 2, 1, 3).reshape(b, l, h * d)


def context_attention(q, k, v):
    b, h, l, d = q.shape
    nb = l // Q_BLOCK
    scale = NA_HEAD_DIM ** -0.5
    qb = jnp.moveaxis(q.reshape(b, h, nb, Q_BLOCK, d), 2, 0)

    def block(q_blk):
        s = jnp.einsum('bhqd,bhkd->bhqk', q_blk, k).astype(jnp.float32) * scale
        p = jax.nn.softmax(s, axis=-1).astype(v.dtype)
        return jnp.einsum('bhqk,bhkd->bhqd', p, v)

    o = lax.map(block, qb)
    return jnp.moveaxis(o, 0, 2).reshape(b, h, l, d)


def neighbourhood_attention(q, k, v, ck, cv, rpb):
    b, h, t, d = q.shape
    rows = t // GRID_W
    kr = min(WIN_R, rows)
    n_keys = kr * WIN_C
    scale = NA_HEAD_DIM ** -0.5
    r = jnp.arange(rows)
    col = jnp.arange(GRID_W)
    key_rows = jnp.clip(r - kr // 2, 0, rows - kr)[:, None] + jnp.arange(kr)[None, :]
    key_cols = jnp.clip(col - WIN_C // 2, 0, GRID_W - WIN_C)[:, None] + jnp.arange(WIN_C)[None, :]
    idx = (key_rows[:, None, :, None] * GRID_W + key_cols[None, :, None, :]).reshape(rows, GRID_W, n_keys)
    dr = key_rows - r[:, None] + (WIN_R - 1)
    dc = key_cols - col[:, None] + (WIN_C - 1)
    bias = rpb[:, dr[:, None, :, None], dc[None, :, None, :]]
    bias = jnp.moveaxis(bias.reshape(h, rows, GRID_W, n_keys), 1, 0)
    q_rows = jnp.moveaxis(q.reshape(b, h, rows, GRID_W, d), 2, 0)

    def row_block(args):
        q_r, idx_r, bias_r = args
        k_w = jnp.take(k, idx_r, axis=2)
        v_w = jnp.take(v, idx_r, axis=2)
        s_loc = jnp.einsum('bhwd,bhwnd->bhwn', q_r, k_w).astype(jnp.float32) * scale + bias_r.astype(jnp.float32)
        s_ctx = jnp.einsum('bhwd,bhcd->bhwc', q_r, ck).astype(jnp.float32) * scale
        p = jax.nn.softmax(jnp.concatenate([s_loc, s_ctx], axis=-1), axis=-1).astype(v.dtype)
        return (jnp.einsum('bhwn,bhwnd->bhwd', p[..., :n_keys], v_w)
                + jnp.einsum('bhwc,bhcd->bhwd', p[..., n_keys:], cv))

    o = lax.map(row_block, (q_rows, idx, bias))
    return jnp.moveaxis(o, 0, 2).reshape(b, h, t, d)


def s5_discretize(lam_re, lam_im, log_dt, b_re, b_im):
    f32 = jnp.float32
    lam_re, lam_im = lam_re.astype(f32), lam_im.astype(f32)
    dt = jnp.exp(log_dt.astype(f32))[..., None]
    ldr, ldi = lam_re * dt, lam_im * dt
    ea = jnp.exp(ldr)
    a_re, a_im = ea * jnp.cos(ldi), ea * jnp.sin(ldi)
    mag2 = lam_re * lam_re + lam_im * lam_im
    f_re = ((a_re - 1) * lam_re + a_im * lam_im) / mag2
    f_im = (a_im * lam_re - (a_re - 1) * lam_im) / mag2
    b_re, b_im = b_re.astype(f32), b_im.astype(f32)
    bb_re = f_re[..., None] * b_re - f_im[..., None] * b_im
    bb_im = f_re[..., None] * b_im + f_im[..., None] * b_re
    return ldr, ldi, bb_re, bb_im


def ssm_scan(ug, ldr, ldi, bb_re, bb_im, h0, reverse):
    l = ug.shape[1]
    bu_re = jnp.einsum('blgh,gph->blgp', ug, bb_re)
    bu_im = jnp.einsum('blgh,gph->blgp', ug, bb_im)
    ea = jnp.exp(ldr)
    a_re = jnp.broadcast_to(ea * jnp.cos(ldi), bu_re.shape)
    a_im = jnp.broadcast_to(ea * jnp.sin(ldi), bu_re.shape)

    def combine(e1, e2):
        a1r, a1i, b1r, b1i = e1
        a2r, a2i, b2r, b2i = e2
        return (a2r * a1r - a2i * a1i, a2r * a1i + a2i * a1r,
                a2r * b1r - a2i * b1i + b2r, a2r * b1i + a2i * b1r + b2i)

    _, _, s_re, s_im = lax.associative_scan(combine, (a_re, a_im, bu_re, bu_im), reverse=reverse, axis=1)
    if h0 is not None:
        n = jnp.arange(l, 0, -1, dtype=jnp.float32) if reverse else jnp.arange(1, l + 1, dtype=jnp.float32)
        mag = jnp.exp(ldr[None] * n[:, None, None])
        ang = ldi[None] * n[:, None, None]
        p_re, p_im = mag * jnp.cos(ang), mag * jnp.sin(ang)
        h0r, h0i = h0[0][:, None], h0[1][:, None]
        s_re = s_re + p_re * h0r - p_im * h0i
        s_im = s_im + p_re * h0i + p_im * h0r
    last = 0 if reverse else -1
    return s_re, s_im, s_re[:, last], s_im[:, last]


def s5_branch(u, lam_re, lam_im, log_dt, b_re, b_im, c_re, c_im, d_skip, w_glu, h0):
    f32 = jnp.float32
    b, l, _ = u.shape
    ug = u.astype(f32).reshape(b, l, S5_GROUPS, S5_GROUP_CH)
    ldr, ldi, bb_re, bb_im = s5_discretize(lam_re, lam_im, log_dt, b_re, b_im)
    y = d_skip.astype(f32).reshape(S5_GROUPS, S5_GROUP_CH) * ug
    finals = []
    for dirn in range(2):
        h0_d = None if h0 is None else (h0[:, dirn, 0].astype(f32), h0[:, dirn, 1].astype(f32))
        s_re, s_im, f_re, f_im = ssm_scan(ug, ldr[dirn], ldi[dirn], bb_re[dirn], bb_im[dirn], h0_d, dirn == 1)
        y = (y + jnp.einsum('blgp,ghp->blgh', s_re, c_re[dirn].astype(f32))
             - jnp.einsum('blgp,ghp->blgh', s_im, c_im[dirn].astype(f32)))
        if h0 is None:
            finals.append(jnp.stack([f_re, f_im], axis=1))
    y = jax.nn.gelu(y.reshape(b, l, S5_WIDTH)).astype(u.dtype)
    y = y * jax.nn.sigmoid(y @ w_glu)
    state = jnp.stack(finals, axis=1) if h0 is None else None
    return y, state


def trunk_layer(x, cond, ctx, p):
    mods = jax.nn.silu(cond) @ p['w_ada'] + p['b_ada']
    sh1, sc1, g1, sh2, sc2, g2, sh3, sc3, g3 = [m[:, None, :] for m in jnp.split(mods, N_MOD, axis=-1)]
    h = modulate(rms_norm(x, p['norm_g'][0]), sh1, sc1)
    x = x + 0.5 * g1 * swiglu(h, p['ffn_in'][0], p['ffn_out'][0])

    h = modulate(rms_norm(x, p['norm_g'][1]), sh2, sc2)
    proj = h @ p['w_in']
    q, k, v, u, ga, gb = jnp.split(
        proj, [NA_WIDTH, 2 * NA_WIDTH, 3 * NA_WIDTH, 3 * NA_WIDTH + S5_WIDTH,
               3 * NA_WIDTH + S5_WIDTH + D_MODEL], axis=-1)
    q, k, v = split_heads(q), split_heads(k), split_heads(v)
    if ctx is None:
        ya = context_attention(q, k, v)
        h0 = None
    else:
        ck, cv, h0 = ctx
        ya = neighbourhood_attention(q, k, v, ck, cv, p['rpb'])
    yb, s_final = s5_branch(u, p['lam_re'], p['lam_im'], p['log_dt'], p['b_re'], p['b_im'],
                            p['c_re'], p['c_im'], p['d'], p['w_glu'], h0)
    merged = (jax.nn.sigmoid(ga) * (merge_heads(ya) @ p['w_up_a'])
              + jax.nn.sigmoid(gb) * (yb @ p['w_up_b']))
    x = x + g2 * (merged @ p['w_out'])

    h = modulate(rms_norm(x, p['norm_g'][2]), sh3, sc3)
    x = x + 0.5 * g3 * swiglu(h, p['ffn_in'][1], p['ffn_out'][1])
    ctx_out = (k, v, s_final) if ctx is None else None
    return x, ctx_out


def setup_inputs(seed: int = 0) -> dict:
    key = jax.random.key(seed)
    ks = jax.random.split(key, 32)
    f32 = jnp.float32

    def nrm(k, shape, scale=1.0):
        return jax.random.normal(k, shape, f32) * scale

    lam_im = (jnp.pi * jnp.arange(S5_STATE, dtype=f32))[None, None, None, :] + nrm(ks[14], (DEPTH, 2, S5_GROUPS, S5_STATE), 0.01)
    return {
        'x_prompt': nrm(ks[0], (BATCH, SEQ, D_MODEL)),
        'x_sample': nrm(ks[1], (DEC_BATCH, DEC_SEQ, D_MODEL)),
        'cache_k': nrm(ks[2], (DEC_BATCH, DEPTH, NA_HEADS, PAST_LEN, NA_HEAD_DIM)),
        'cache_v': nrm(ks[3], (DEC_BATCH, DEPTH, NA_HEADS, PAST_LEN, NA_HEAD_DIM)),
        'state_ssm': nrm(ks[4], (DEC_BATCH, DEPTH, 2, 2, S5_GROUPS, S5_STATE), 0.1),
        'c': nrm(ks[5], (DEC_BATCH, D_MODEL)),
        'c_ctx': nrm(ks[6], (D_MODEL,)),
        'w_ada': nrm(ks[7], (DEPTH, D_MODEL, N_MOD * D_MODEL), 0.5 * D_MODEL ** -0.5),
        'b_ada': nrm(ks[8], (DEPTH, N_MOD * D_MODEL), 0.01),
        'norm_g': 1.0 + nrm(ks[9], (DEPTH, 3, D_MODEL), 0.02),
        'ffn_in': nrm(ks[10], (DEPTH, 2, D_MODEL, 2 * D_FF), D_MODEL ** -0.5),
        'ffn_out': nrm(ks[11], (DEPTH, 2, D_FF, D_MODEL), D_FF ** -0.5),
        'w_in': nrm(ks[12], (DEPTH, D_MODEL, IN_COLS), D_MODEL ** -0.5),
        'rpb': nrm(ks[13], (DEPTH, NA_HEADS, 2 * WIN_R - 1, 2 * WIN_C - 1), 0.1),
        's5_lam_re': -0.5 + nrm(ks[15], (DEPTH, 2, S5_GROUPS, S5_STATE), 0.01),
        's5_lam_im': lam_im,
        's5_log_dt': jax.random.uniform(ks[16], (DEPTH, 2, S5_GROUPS), f32, math.log(1e-3), math.log(1e-1)),
        's5_b_re': nrm(ks[17], (DEPTH, 2, S5_GROUPS, S5_STATE, S5_GROUP_CH), (2 * S5_GROUP_CH) ** -0.5),
        's5_b_im': nrm(ks[18], (DEPTH, 2, S5_GROUPS, S5_STATE, S5_GROUP_CH), (2 * S5_GROUP_CH) ** -0.5),
        's5_c_re': nrm(ks[19], (DEPTH, 2, S5_GROUPS, S5_GROUP_CH, S5_STATE), S5_STATE ** -0.5),
        's5_c_im': nrm(ks[20], (DEPTH, 2, S5_GROUPS, S5_GROUP_CH, S5_STATE), S5_STATE ** -0.5),
        's5_d': nrm(ks[21], (DEPTH, S5_WIDTH)),
        'w_glu': nrm(ks[22], (DEPTH, S5_WIDTH, S5_WIDTH), S5_WIDTH ** -0.5),
        'w_up_a': nrm(ks[23], (DEPTH, NA_WIDTH, D_MODEL), NA_WIDTH ** -0.5),
        'w_up_b': nrm(ks[24], (DEPTH, S5_WIDTH, D_MODEL), S5_WIDTH ** -0.5),
        'w_out': nrm(ks[25], (DEPTH, D_MODEL, D_MODEL), D_MODEL ** -0.5),
        'final_g': 1.0 + nrm(ks[26], (D_MODEL,), 0.02),
    }


def reference(x_prompt, x_sample, cache_k, cache_v, state_ssm, c, c_ctx, w_ada, b_ada, norm_g,
              ffn_in, ffn_out, w_in, rpb, s5_lam_re, s5_lam_im, s5_log_dt, s5_b_re, s5_b_im,
              s5_c_re, s5_c_im, s5_d, w_glu, w_up_a, w_up_b, w_out, final_g):
    xp, xs = x_prompt, x_sample
    new_k, new_v, new_s = [], [], []
    for l in range(DEPTH):
        p = {
            'w_ada': w_ada[l], 'b_ada': b_ada[l], 'norm_g': norm_g[l],
            'ffn_in': ffn_in[l], 'ffn_out': ffn_out[l], 'w_in': w_in[l], 'rpb': rpb[l],
            'lam_re': s5_lam_re[l], 'lam_im': s5_lam_im[l], 'log_dt': s5_log_dt[l],
            'b_re': s5_b_re[l], 'b_im': s5_b_im[l], 'c_re': s5_c_re[l], 'c_im': s5_c_im[l],
            'd': s5_d[l], 'w_glu': w_glu[l], 'w_up_a': w_up_a[l], 'w_up_b': w_up_b[l], 'w_out': w_out[l],
        }
        xp, (k_l, v_l, s_l) = trunk_layer(xp, c_ctx[None, :], None, p)
        new_k.append(k_l)
        new_v.append(v_l)
        new_s.append(s_l)
        xs, _ = trunk_layer(xs, c, (cache_k[:, l], cache_v[:, l], state_ssm[:, l]), p)
    y_prompt = rms_norm(xp, final_g)
    y_sample = rms_norm(xs, final_g)
    new_cache_k = jnp.stack(new_k, axis=1)
    new_cache_v = jnp.stack(new_v, axis=1)
    new_state_ssm = jnp.stack(new_s, axis=1)
    return (y_prompt, y_sample, new_cache_k, new_cache_v, new_state_ssm)
```

```python
import contextlib
import math
import numpy as np
import concourse.bass as bass
import concourse.mybir as mybir
from concourse.bass_utils import run_bass_kernel_spmd

F32 = mybir.dt.float32
BF16 = mybir.dt.bfloat16
ALU = mybir.AluOpType
AF = mybir.ActivationFunctionType

T = 512
NEG = -30000.0
EPS = 1e-6
D = 2048
DFF = 5632
NSLOT = 22
ARENA = 52800
STAGE = 99
KSQ = 'act'


class Prog:
    ENGS = ("pe", "dve", "act", "pool", "sp")

    def __init__(self, nc):
        self.nc = nc
        self.ops = []
        self.last_w = {}
        self.readers = {}
        self.pending = {}
        self.last_eng = {}
        self.dma_since = []

    def _add(self, eng, fn, reads, writes, dma_key=None):
        idx = len(self.ops)
        deps = set()
        for r in reads:
            w = self.last_w.get(r)
            if w is not None:
                deps.add(w)
        for r in writes:
            w = self.last_w.get(r)
            if w is not None:
                deps.add(w)
            rl = self.readers.get(r)
            if rl:
                deps.update(rl.values())
        rkey = eng if dma_key is None else ("dma", idx)
        for r in reads:
            self.readers.setdefault(r, {})[rkey] = idx
        for r in writes:
            self.last_w[r] = idx
            self.readers[r] = {}
        pb = self.pending.pop(eng, None)
        if pb:
            deps.update(pb)
        deps.discard(idx)
        self.ops.append(dict(eng=eng, fn=fn, deps=deps, dma=dma_key))
        self.last_eng[eng] = idx
        if dma_key is not None:
            self.dma_since.append(idx)
        return idx

    def op(self, eng, fn, reads=(), writes=()):
        return self._add(eng, fn, tuple(reads), tuple(writes))

    def dma(self, queue, fn, reads=(), writes=(), key=None):
        return self._add(queue, fn, tuple(reads), tuple(writes), dma_key=key)

    def barrier(self):
        b = set(self.last_eng.values()) | set(self.dma_since)
        self.dma_since = []
        for e in self.ENGS:
            self.pending[e] = set(b) | self.pending.get(e, set())

    def emit(self):
        nc = self.nc
        ops = self.ops
        needed = set()
        for i, o in enumerate(ops):
            nd = set()
            for d in o["deps"]:
                od = ops[d]
                if od["dma"] is None and od["eng"] == "pe" and o["eng"] == "pe" and o["dma"] is None:
                    continue
                nd.add(d)
            o["deps"] = nd
            needed |= nd
        cnt = {e: 0 for e in self.ENGS}
        dcnt = {}
        for i, o in enumerate(ops):
            if o["dma"] is not None:
                k = o["dma"]
                dcnt[k] = dcnt.get(k, 0) + 16
                o["sig"] = ("d:" + str(k), dcnt[k])
            elif i in needed:
                cnt[o["eng"]] += 1
                o["sig"] = ("e:" + o["eng"], cnt[o["eng"]])
            else:
                o["sig"] = None
        semnames = ["e:" + e for e in self.ENGS] + ["d:" + str(k) for k in dcnt]
        self.stats = (len(ops), dict(cnt), len(semnames), max(dcnt.values()) if dcnt else 0)
        per_eng = {e: [] for e in self.ENGS}
        for i, o in enumerate(ops):
            per_eng[o["eng"]].append(i)
        with contextlib.ExitStack() as st:
            sems = {}
            for j, n in enumerate(semnames):
                sems[n] = st.enter_context(nc.semaphore("s%d" % j))
            block = st.enter_context(nc.Block())

            def make(e):
                def body(eng):
                    known = {}
                    for i in per_eng[e]:
                        o = ops[i]
                        w = {}
                        for d in o["deps"]:
                            s, v = ops[d]["sig"]
                            if v > w.get(s, 0):
                                w[s] = v
                        for s, v in w.items():
                            if known.get(s, 0) >= v:
                                continue
                            known[s] = v
                            eng.wait_ge(sems[s], v)
                        ins = o["fn"](eng)
                        if o["sig"] is not None:
                            s, v = o["sig"]
                            ins.then_inc(sems[s], 16 if o["dma"] is not None else 1)
                    if e == "sp":
                        for k, v in dcnt.items():
                            if known.get("d:" + str(k), 0) < v:
                                eng.wait_ge(sems["d:" + str(k)], v)
                return body

            block.tensor(make("pe"))
            block.vector(make("dve"))
            block.scalar(make("act"))
            block.gpsimd(make("pool"))
            block.sync(make("sp"))


class Arena:
    def __init__(self, ap, size):
        self.ap, self.size, self.off = ap, size, 0

    def f32(self, n):
        v = self.ap[:, self.off:self.off + n]
        self.off += n
        assert self.off <= self.size, ("arena overflow", self.off)
        return v

    def bf(self, n):
        nf = (n + 1) // 2
        return self.f32(nf).bitcast(BF16)

    def reset(self, to):
        self.off = to


def build_program():
    nc = bass.Bass("TRN2", target_bir_lowering=False)

    def din(name, shape, dt=F32):
        return nc.dram_tensor(name, list(shape), dt, kind="ExternalInput").ap()

    def dout(name, shape, dt=F32):
        return nc.dram_tensor(name, list(shape), dt, kind="ExternalOutput").ap()

    def dscr(name, shape, dt=F32):
        return nc.dram_tensor(name, list(shape), dt, kind="Internal").ap()

    xp = din("xp", [D, 1024]); xo = din("xo", [D, 2048]); xh = din("xh", [D, 2048])
    cond = din("cond", [128, 32])
    w_ada = din("w_ada", [D, 9 * D]); b_ada = din("b_ada", [128, 144])
    ng = din("ng", [128, 48]); fg = din("fg", [128, 16])
    ffn_in = din("ffn_in", [2, D, 2 * DFF]); ffn_out = din("ffn_out", [2, DFF, D])
    w_in = din("w_in", [D, 8192]); w_glu = din("w_glu", [1024, 1024])
    w_up_a = din("w_up_a", [1024, D]); w_up_b = din("w_up_b", [1024, D]); w_out = din("w_out", [D, D])
    lamre = din("lamre", [128, 64]); lamim = din("lamim", [128, 64]); ldt = din("ldt", [128, 64])
    bpr = din("bpr", [64, 128, 128]); bpi = din("bpi", [64, 128, 128])
    cpr = din("cpr", [64, 128, 128]); cpi = din("cpi", [64, 128, 128])
    dsk = din("dsk", [128, 8]); h0 = din("h0", [128, 128])
    ckT = din("ckT", [128, 4096]); cvt = din("cvt", [128, 4096])
    rpbx = din("rpbx", [64, 16 * NSLOT * 64])
    cst = din("cst", [128, 128 + 256])

    yp = dout("yp", [D, 1024]); ys = dout("ys", [D, 2048])
    nk = dout("nk", [1024, 1024]); nv = dout("nv", [1024, 1024]); ns = dout("ns", [128, 512])

    x1s = dscr("x1s", [D, 3072]); qs = dscr("qs", [1024, 3072], BF16); ks = dscr("ks", [1024, 3584], BF16)
    vs = dscr("vs", [3584, 1040], BF16); us = dscr("us", [1024, 5120], BF16)
    yas = dscr("yas", [1024, 3072]); ybs = dscr("ybs", [1024, 3072], BF16)
    rot = dscr("rot", [64, 128, 1536])

    P = Prog(nc)
    with contextlib.ExitStack() as st:
        E = st.enter_context
        arena_t = E(nc.sbuf_tensor("arena", [128, ARENA], F32))
        psb = [E(nc.psum_tensor("ps%d" % i, [128, 512], F32)) for i in range(8)]
        A = Arena(arena_t[:], ARENA)
        bankc = [0]

        def bank():
            b = bankc[0] % 8
            bankc[0] += 1
            return b, psb[b][:], ("ps", b)

        mods = A.f32(288).rearrange("p (m c) -> p m c", c=2)
        der = A.f32(160).rearrange("p (n k c) -> p n k c", n=5, k=16)
        ngs = A.f32(48); fgs = A.f32(16)
        onesf = A.bf(128)
        csts = A.f32(384)
        identb = A.bf(128)
        selb = A.bf(256)
        epsb = A.f32(1); hpib = A.f32(1)
        rmag = A.f32(64)
        CL = A.f32(640).rearrange("p (k c) -> p k c", k=10)
        SL = A.f32(640).rearrange("p (k c) -> p k c", k=10)
        c255 = A.f32(64); s255 = A.f32(64)
        dsks = A.f32(8)
        carry = A.f32(128).rearrange("p (d c r) -> p d c r", d=2, r=2)
        fin = A.f32(512).rearrange("p (s d r c) -> p s d r c", s=4, d=2, r=2)
        ATv = A.bf(8 * T).rearrange("p (k t) -> p k t", k=8)
        PERS = A.off

        def V(eng, f, reads, writes):
            return P.op(eng, f, reads, writes)

        class _Stop(Exception):
            pass

        def chk(n):
            if STAGE == n:
                raise _Stop()

        try:
            P.dma("sp", lambda e: e.dma_start(out=csts, in_=cst), writes=["csts"], key="csts")
            P.dma("sp", lambda e: e.dma_start(out=ngs, in_=ng), writes=["ngs"], key="ngs")
            P.dma("sp", lambda e: e.dma_start(out=fgs, in_=fg), writes=["fgs"], key="fgs")
            P.dma("sp", lambda e: e.dma_start(out=dsks, in_=dsk), writes=["dsks"], key="dsks")
            V("dve", lambda e: e.memset(onesf, 1.0), [], ["onesf"])
            V("dve", lambda e: e.memset(epsb, EPS), [], ["epsb"])
            V("dve", lambda e: e.memset(hpib, math.pi / 2), [], ["hpib"])
            V("dve", lambda e: e.tensor_copy(out=identb, in_=csts[:, 0:128]), ["csts"], ["identb"])
            V("dve", lambda e: e.tensor_copy(out=selb, in_=csts[:, 128:384]), ["csts"], ["selb"])

            chk(0.1)
            wctr = [0]
            wslots = [None] * 4

            def wslot():
                s = wctr[0] % 4
                wctr[0] += 1
                return wslots[s], ("ws", s)

            def wdma(dst, src, res):
                P.dma("pool", lambda e: e.dma_start(out=dst, in_=src), writes=[res], key=res)

            tmpc = [0]
            tmps = [None] * 6

            def tmp():
                i = tmpc[0] % 6
                tmpc[0] += 1
                return tmps[i], ("tmp", i)

            def alloc_common():
                A.reset(PERS)
                xv = A.f32(16 * T).rearrange("p (k t) -> p k t", k=16)
                hv = A.bf(16 * T).rearrange("p (k t) -> p k t", k=16)
                for i in range(4):
                    wslots[i] = A.bf(8192)
                for i in range(6):
                    tmps[i] = A.f32(T)
                rs = A.f32(T)
                return xv, hv, rs

            xv, hv, rs = alloc_common()
            cnd = A.f32(32); scb = A.bf(32); bad = A.f32(144)
            P.dma("sp", lambda e: e.dma_start(out=cnd, in_=cond), writes=["cnd"], key="cnd")
            P.dma("sp", lambda e: e.dma_start(out=bad, in_=b_ada), writes=["bad"], key="bad")
            V("act", lambda e: e.activation(out=scb, in_=cnd, func=AF.Silu), ["cnd"], ["scb"])
            chk(0.2)
            scb3 = scb.rearrange("p (k c) -> p k c", c=2)
            wav = w_ada.rearrange("(k p) n -> p k n", p=128)
            for blk in range(36):
                wv, wr = wslot()
                wv3 = wv.rearrange("p (k n) -> p k n", k=16)
                wdma(wv3, wav[:, :, blk * 512:(blk + 1) * 512], wr)
                for mc in range(4):
                    m = blk * 4 + mc
                    b, ps, pr = bank()
                    for k in range(16):
                        V("pe", lambda e, ps=ps, wv3=wv3, k=k, mc=mc: e.matmul(ps[:, 0:2], lhsT=wv3[:, k, mc * 128:(mc + 1) * 128], rhs=scb3[:, k, :], start=(k == 0), stop=(k == 15)),
                          [wr, "scb"], [pr])
                    V("dve", lambda e, ps=ps, m=m: e.tensor_scalar(out=mods[:, m, :], in0=ps[:, 0:2], scalar1=bad[:, m:m + 1], scalar2=None, op0=ALU.add),
                      [pr, "bad"], ["mods"])
                chk(0.3 if blk == 0 else -1)
            chk(0.5)
            for n in range(3):
                V("dve", lambda e, n=n: e.tensor_scalar(out=der[:, n], in0=mods[:, 16 * (3 * n + 1):16 * (3 * n + 2), :], scalar1=1.0, scalar2=None, op0=ALU.add),
                  ["mods"], ["der"])
                V("dve", lambda e, n=n: e.tensor_tensor(out=der[:, n], in0=der[:, n], in1=ngs[:, n * 16:(n + 1) * 16].unsqueeze(2).to_broadcast([128, 16, 2]), op=ALU.mult),
                  ["der", "ngs"], ["der"])
            V("dve", lambda e: e.tensor_scalar(out=der[:, 3], in0=mods[:, 32:48, :], scalar1=0.5, scalar2=None, op0=ALU.mult), ["mods"], ["der"])
            V("dve", lambda e: e.tensor_scalar(out=der[:, 4], in0=mods[:, 128:144, :], scalar1=0.5, scalar2=None, op0=ALU.mult), ["mods"], ["der"])

            chk(1)
            XR = [("x", k) for k in range(16)]

            def rstd_of(xv, rs):
                b, ps, pr = bank()
                for k in range(16):
                    sq, sr = tmp()
                    sqb = sq.bitcast(BF16)[:, 0:T]
                    V("act", lambda e, sqb=sqb, k=k: e.activation(out=sqb, in_=xv[:, k, :], func=AF.Square), [("x", k)], [sr])
                    V("pe", lambda e, ps=ps, sqb=sqb, k=k: e.matmul(ps, lhsT=onesf, rhs=sqb, start=(k == 0), stop=(k == 15)), [sr, "onesf"], [pr])
                V("act", lambda e, ps=ps: e.activation(out=rs, in_=ps, func=AF.Sqrt, bias=epsb[:, 0:1], scale=1.0 / D), [pr, "epsb"], ["rs"])
                V("dve", lambda e: e.reciprocal(out=rs, in_=rs), ["rs"], ["rs"])

            def norm_mod(xv, hv, rs, n, ci):
                rstd_of(xv, rs)
                for k in range(16):
                    tk, tr = tmp()
                    V("dve", lambda e, tk=tk, k=k: e.scalar_tensor_tensor(out=tk, in0=xv[:, k, :], scalar=der[:, n, k, ci:ci + 1], in1=rs, op0=ALU.mult, op1=ALU.mult),
                      [("x", k), "rs", "der"], [tr])
                    V("act", lambda e, tk=tk, k=k: e.activation(out=hv[:, k, :], in_=tk, func=AF.Identity, bias=mods[:, 48 * n + k, ci:ci + 1], scale=1.0),
                      [tr, "mods"], [("h", k)])

            fin_v = ffn_in.rearrange("f (k p) n -> f p k n", p=128)
            fout_v = ffn_out.rearrange("f (j p) n -> f p j n", p=128)

            def ffn(f, xv, hv, actv, ci):
                for jb in range(22):
                    wv, wr = wslot()
                    wv4 = wv.rearrange("p (g k n) -> p g k n", g=2, k=16)
                    wdma(wv4[:, 0], fin_v[f, :, :, jb * 256:(jb + 1) * 256], wr)
                    wdma(wv4[:, 1], fin_v[f, :, :, DFF + jb * 256:DFF + (jb + 1) * 256], wr)
                    for jj in range(2):
                        j = 2 * jb + jj
                        bg, psg, prg = bank()
                        bu, psu, pru = bank()
                        for k in range(16):
                            V("pe", lambda e, psg=psg, k=k, jj=jj, wv4=wv4: e.matmul(psg, lhsT=wv4[:, 0, k, jj * 128:(jj + 1) * 128], rhs=hv[:, k, :], start=(k == 0), stop=(k == 15)),
                              [wr, ("h", k)], [prg])
                        for k in range(16):
                            V("pe", lambda e, psu=psu, k=k, jj=jj, wv4=wv4: e.matmul(psu, lhsT=wv4[:, 1, k, jj * 128:(jj + 1) * 128], rhs=hv[:, k, :], start=(k == 0), stop=(k == 15)),
                              [wr, ("h", k)], [pru])
                        sg, sr = tmp()
                        V("act", lambda e, sg=sg, psg=psg: e.activation(out=sg, in_=psg, func=AF.Silu), [prg], [sr])
                        V("dve", lambda e, sg=sg, psu=psu, j=j: e.tensor_tensor(out=actv[:, j, :], in0=sg, in1=psu, op=ALU.mult), [sr, pru], [("act", j)])
                hgi = 3 if f == 0 else 4
                AR = [("act", j) for j in range(44)]
                for m in range(16):
                    wv, wr = wslot()
                    wv3 = wv[:, 0:44 * 128].rearrange("p (j n) -> p j n", j=44)
                    wdma(wv3, fout_v[f, :, :, m * 128:(m + 1) * 128], wr)
                    b, ps, pr = bank()
                    for j in range(44):
                        V("pe", lambda e, ps=ps, j=j, wv3=wv3: e.matmul(ps, lhsT=wv3[:, j, :], rhs=actv[:, j, :], start=(j == 0), stop=(j == 43)),
                          [wr, ("act", j)], [pr])
                    V("dve", lambda e, ps=ps, m=m: e.scalar_tensor_tensor(out=xv[:, m, :], in0=ps, scalar=der[:, hgi, m, ci:ci + 1], in1=xv[:, m, :], op0=ALU.mult, op1=ALU.add),
                      [pr, ("x", m), "der"], [("x", m)])

            def linear_fm(rhs_fn, rhs_res, KC, wsrc_fn, nblocks, evac):
                for cb in range(nblocks):
                    wv, wr = wslot()
                    wv3 = wv[:, 0:KC * 512].rearrange("p (k n) -> p k n", k=KC)
                    wdma(wv3, wsrc_fn(cb), wr)
                    for mc in range(4):
                        b, ps, pr = bank()
                        for k in range(KC):
                            V("pe", lambda e, ps=ps, k=k, mc=mc, wv3=wv3: e.matmul(ps, lhsT=wv3[:, k, mc * 128:(mc + 1) * 128], rhs=rhs_fn(k), start=(k == 0), stop=(k == KC - 1)),
                              [wr] + rhs_res(k), [pr])
                        evac(cb * 4 + mc, ps, pr)

            win_v = w_in.rearrange("(k p) n -> p k n", p=128)

            def phaseA(ti):
                g0 = ti * T
                kind = "p" if ti < 2 else ("o" if ti < 6 else "h")
                ci = 0 if kind == "p" else 1
                src = {"p": xp, "o": xo, "h": xh}[kind]
                t0 = {"p": g0, "o": g0 - 1024, "h": g0 - 3072}[kind]
                P.dma("sp", lambda e: e.dma_start(out=xv, in_=src.rearrange("(k p) t -> p k t", p=128)[:, :, t0:t0 + T]), writes=XR, key="xld")
                norm_mod(xv, hv, rs, 0, ci)
                if ti == 0:
                    chk(1.1)
                ffn(0, xv, hv, actv, ci)
                if ti == 0:
                    chk(1.3)
                if kind != "h":
                    P.dma("sp", lambda e: e.dma_start(out=x1s.rearrange("(k p) t -> p k t", p=128)[:, :, g0:g0 + T], in_=xv), reads=XR, writes=[("x1s", ti)], key="xst")
                norm_mod(xv, hv, rs, 1, ci)
                if ti == 0:
                    chk(1.4)
                hfn = lambda k: hv[:, k, :]
                pst = actv[:, 0:8, :]
                PSTR = [("act", j) for j in range(8)]
                if kind != "h":
                    def ev_q(c, ps, pr):
                        V("act", lambda e: e.activation(out=pst[:, c, :], in_=ps, func=AF.Identity, scale=0.125), [pr], [("act", c)])
                    linear_fm(hfn, (lambda k: [("h", k)]), 16, lambda cb: win_v[:, :, cb * 512:(cb + 1) * 512], 2, ev_q)
                    P.dma(KSQ, lambda e: e.dma_start(out=qs.rearrange("(c p) t -> p c t", p=128)[:, :, g0:g0 + T], in_=pst), reads=PSTR, writes=[("qs", ti)], key="pst")
                    if ti == 0:
                        chk(1.5)
                if kind != "h" or ti == 6:
                    def ev_k(c, ps, pr):
                        if kind != "p":
                            V("act", lambda e: e.activation(out=pst[:, c, :], in_=ps, func=AF.Copy), [pr], [("act", c)])
                        else:
                            kf, kr = tmp()
                            V("dve", lambda e: e.tensor_copy(out=kf, in_=ps), [pr], [kr])
                            V("act", lambda e: e.activation(out=pst[:, c, :], in_=kf, func=AF.Copy), [kr], [("act", c)])
                            P.dma("sp", lambda e: e.dma_start(out=nk[c * 128:(c + 1) * 128, g0:g0 + T], in_=kf), reads=[kr], writes=["nk"], key=kr)
                    linear_fm(hfn, (lambda k: [("h", k)]), 16, lambda cb: win_v[:, :, 1024 + cb * 512:1024 + (cb + 1) * 512], 2, ev_k)
                    if ti == 0:
                        chk(1.55)
                    P.dma(KSQ, lambda e: e.dma_start(out=ks.rearrange("(c p) t -> p c t", p=128)[:, :, g0:g0 + T], in_=pst), reads=PSTR, writes=[("ks", ti)], key="pst")
                    if ti == 0:
                        chk(1.6)
                    vst = actv[:, 8:17, :].rearrange("p a t -> p (a t)")[:, 0:4 * 1040].rearrange("p (b h e) -> p b h e", b=4, h=16)
                    VSTR = [("act", j) for j in range(8, 17)]
                    V("dve", lambda e: e.memset(vst[:, :, :, 64:65], 1.0), [], VSTR)
                    for cb in range(2):
                        wv, wr = wslot()
                        wv3 = wv.rearrange("p (k n) -> p k n", k=16)
                        wdma(wv3, win_v[:, :, 2048 + cb * 512:2048 + (cb + 1) * 512], wr)
                        for tb in range(4):
                            b, ps, pr = bank()
                            for k in range(16):
                                V("pe", lambda e, ps=ps, k=k, tb=tb, wv3=wv3: e.matmul(ps, lhsT=hv[:, k, tb * 128:(tb + 1) * 128], rhs=wv3[:, k, :], start=(k == 0), stop=(k == 15)),
                                  [wr, ("h", k)], [pr])
                            if kind != "p":
                                V("act", lambda e, ps=ps, tb=tb, cb=cb: e.activation(out=vst[:, tb, cb * 8:(cb + 1) * 8, 0:64], in_=ps.rearrange("p (h d) -> p h d", h=8), func=AF.Copy), [pr], VSTR)
                            else:
                                vf, vr = tmp()
                                V("dve", lambda e, vf=vf, ps=ps: e.tensor_copy(out=vf, in_=ps), [pr], [vr])
                                V("act", lambda e, vf=vf, tb=tb, cb=cb: e.activation(out=vst[:, tb, cb * 8:(cb + 1) * 8, 0:64], in_=vf.rearrange("p (h d) -> p h d", h=8), func=AF.Copy), [vr], VSTR)
                                P.dma("sp", lambda e, vf=vf, tb=tb, cb=cb: e.dma_start(out=nv[g0 + tb * 128:g0 + (tb + 1) * 128, cb * 512:(cb + 1) * 512], in_=vf), reads=[vr], writes=["nv"], key=vr)
                    P.dma(KSQ, lambda e: e.dma_start(out=vs.rearrange("(b p) e -> p b e", p=128)[:, g0 // 128:g0 // 128 + 4, :], in_=vst.rearrange("p b h e -> p b (h e)")), reads=VSTR, writes=[("vs", ti)], key="vst")

                if ti == 0:
                    chk(1.7)

                def ev_u(c, ps, pr):
                    V("act", lambda e: e.activation(out=pst[:, c, :], in_=ps, func=AF.Copy), [pr], [("act", c)])
                linear_fm(hfn, (lambda k: [("h", k)]), 16, lambda cb: win_v[:, :, 3072 + cb * 512:3072 + (cb + 1) * 512], 2, ev_u)
                P.dma(KSQ, lambda e: e.dma_start(out=us.rearrange("(c p) t -> p c t", p=128)[:, :, g0:g0 + T], in_=pst), reads=PSTR, writes=[("us", ti)], key="pst")

            P.barrier()
            xv, hv, rs = alloc_common()
            actv = A.bf(44 * T).rearrange("p (j t) -> p j t", j=44)
            for ti in range(10):
                phaseA(ti)
                chk(2 if ti == 0 else (3 if ti == 9 else -1))

            P.barrier()
            A.reset(PERS)
            bbr = A.bf(64 * 128).rearrange("p (c n) -> p c n", c=64)
            bbi = A.bf(64 * 128).rearrange("p (c n) -> p c n", c=64)
            ctr_ = A.bf(64 * 128).rearrange("p (c n) -> p c n", c=64)
            cti = A.bf(64 * 128).rearrange("p (c n) -> p c n", c=64)
            S5BASE = A.off
            sm = {}
            for nm in ["lre", "lim", "dt", "ldr", "ldi", "c", "s", "t", "are", "aim", "mag", "fre", "fim", "nfi", "u1", "u2"]:
                sm[nm] = A.f32(64)
            h0s = A.f32(128)
            P.dma("sp", lambda e: e.dma_start(out=sm["lre"], in_=lamre), writes=["lre"], key="lre")
            P.dma("sp", lambda e: e.dma_start(out=sm["lim"], in_=lamim), writes=["lim"], key="lim")
            P.dma("sp", lambda e: e.dma_start(out=sm["dt"], in_=ldt), writes=["dt"], key="dt")
            P.dma("sp", lambda e: e.dma_start(out=h0s, in_=h0), writes=["h0s"], key="h0s")

            SMR = ["sm", "h0s", "lre", "lim", "dt"]

            def TT(o, a, b, op, eng="dve"):
                V(eng, lambda e: e.tensor_tensor(out=sm[o] if isinstance(o, str) else o, in0=sm[a] if isinstance(a, str) else a, in1=sm[b] if isinstance(b, str) else b, op=op), SMR, ["sm"])

            def TS(o, a, s1, s2, op0, op1=None):
                if op1 is None:
                    V("dve", lambda e: e.tensor_scalar(out=sm[o] if isinstance(o, str) else o, in0=sm[a] if isinstance(a, str) else a, scalar1=s1, scalar2=None, op0=op0), SMR, ["sm"])
                else:
                    V("dve", lambda e: e.tensor_scalar(out=sm[o] if isinstance(o, str) else o, in0=sm[a] if isinstance(a, str) else a, scalar1=s1, scalar2=s2, op0=op0, op1=op1), SMR, ["sm"])

            V("act", lambda e: e.activation(out=sm["dt"], in_=sm["dt"], func=AF.Exp), ["dt"], ["sm"])
            V("dve", lambda e: e.tensor_tensor(out=sm["ldr"], in0=sm["lre"], in1=sm["dt"], op=ALU.mult), ["lre", "sm"], ["sm"])
            V("dve", lambda e: e.tensor_tensor(out=sm["ldi"], in0=sm["lim"], in1=sm["dt"], op=ALU.mult), ["lim", "sm"], ["sm"])
            V("act", lambda e: e.activation(out=rmag, in_=sm["ldr"], func=AF.Exp), ["sm"], ["sm"])
            V("act", lambda e: e.activation(out=sm["s"], in_=sm["ldi"], func=AF.Sin, scale=1.0 / 32), ["sm"], ["sm"])
            V("act", lambda e: e.activation(out=sm["c"], in_=sm["ldi"], func=AF.Sin, bias=hpib[:, 0:1], scale=1.0 / 32), ["sm", "hpib"], ["sm"])

            def dbl(co, so, cin, sin_):
                TT("t", sin_, sin_, ALU.mult)
                V("dve", lambda e: e.scalar_tensor_tensor(out=so, in0=sin_ if not isinstance(sin_, str) else sm[sin_], scalar=2.0, in1=cin if not isinstance(cin, str) else sm[cin], op0=ALU.mult, op1=ALU.mult), ["sm"], ["sm"])
                TS(co, "t", -2.0, 1.0, ALU.mult, ALU.add)

            for i in range(5):
                if i < 4:
                    dbl(sm["u1"], sm["u2"], "c", "s")
                    TT("c", "u1", "u1", ALU.max)
                    TT("s", "u2", "u2", ALU.max)
                else:
                    dbl(CL[:, 0, :], SL[:, 0, :], "c", "s")
            for k in range(1, 10):
                dbl(CL[:, k, :], SL[:, k, :], CL[:, k - 1, :], SL[:, k - 1, :])
            TT("u1", CL[:, 8, :], CL[:, 0, :], ALU.mult); TT("u2", SL[:, 8, :], SL[:, 0, :], ALU.mult); TT(c255, "u1", "u2", ALU.add)
            TT("u1", SL[:, 8, :], CL[:, 0, :], ALU.mult); TT("u2", CL[:, 8, :], SL[:, 0, :], ALU.mult); TT(s255, "u1", "u2", ALU.subtract)
            TT("are", rmag, CL[:, 0, :], ALU.mult); TT("aim", rmag, SL[:, 0, :], ALU.mult)
            TT("u1", "lre", "lre", ALU.mult); TT("u2", "lim", "lim", ALU.mult); TT("mag", "u1", "u2", ALU.add)
            V("dve", lambda e: e.reciprocal(out=sm["mag"], in_=sm["mag"]), ["sm"], ["sm"])
            TS("are", "are", -1.0, None, ALU.add)
            TT("u1", "are", "lre", ALU.mult); TT("u2", "aim", "lim", ALU.mult); TT("fre", "u1", "u2", ALU.add); TT("fre", "fre", "mag", ALU.mult)
            TT("u1", "aim", "lre", ALU.mult); TT("u2", "are", "lim", ALU.mult); TT("fim", "u1", "u2", ALU.subtract); TT("fim", "fim", "mag", ALU.mult)
            TS("nfi", "fim", -1.0, None, ALU.mult)
            h04 = h0s.rearrange("p (d r c) -> p d r c", d=2, r=2)
            for d_ in range(2):
                cs = CL[:, 0, d_ * 32:(d_ + 1) * 32]; ss = SL[:, 0, d_ * 32:(d_ + 1) * 32]
                TT(sm["u1"][:, 0:32], h04[:, d_, 0, :], cs, ALU.mult); TT(sm["u2"][:, 0:32], h04[:, d_, 1, :], ss, ALU.mult)
                TT(carry[:, d_, :, 0], sm["u1"][:, 0:32], sm["u2"][:, 0:32], ALU.subtract)
                TT(sm["u1"][:, 0:32], h04[:, d_, 0, :], ss, ALU.mult); TT(sm["u2"][:, 0:32], h04[:, d_, 1, :], cs, ALU.mult)
                TT(carry[:, d_, :, 1], sm["u1"][:, 0:32], sm["u2"][:, 0:32], ALU.add)
            V("dve", lambda e: e.memset(fin.rearrange("p s d r c -> p (s d r c)"), 0.0), ["sm"], ["fin", "sm"])
            chk(4)
            PRO2 = A.off
            EC = A.f32(8 * T).rearrange("p (c t) -> p c t", c=8)
            ES = A.f32(8 * T).rearrange("p (c t) -> p c t", c=8)
            T1 = A.f32(8 * 256).rearrange("p (c t) -> p c t", c=8)
            T2 = A.f32(8 * 256).rearrange("p (c t) -> p c t", c=8)
            ESn = A.f32(8 * T).rearrange("p (c t) -> p c t", c=8)
            for q in range(8):
                V("dve", lambda e: e.memset(EC[:, :, 0:1], 1.0), ["ec"], ["ec"])
                V("dve", lambda e: e.memset(ES[:, :, 0:1], 0.0), ["ec"], ["ec"])
                for k in range(9):
                    m = 1 << k
                    cm = CL[:, k, q * 8:(q + 1) * 8].unsqueeze(2).to_broadcast([128, 8, m])
                    smm = SL[:, k, q * 8:(q + 1) * 8].unsqueeze(2).to_broadcast([128, 8, m])
                    c_, s_ = EC[:, :, 0:m], ES[:, :, 0:m]
                    t1, t2 = T1[:, :, 0:m], T2[:, :, 0:m]
                    for (a_, b_, c2, d2, op, dst) in ((c_, cm, s_, smm, ALU.subtract, EC[:, :, m:2 * m]), (s_, cm, c_, smm, ALU.add, ES[:, :, m:2 * m])):
                        V("dve", lambda e, a_=a_, b_=b_, t1=t1: e.tensor_tensor(out=t1, in0=a_, in1=b_, op=ALU.mult), ["ec", "sm"], ["t1"])
                        V("dve", lambda e, c2=c2, d2=d2, t2=t2: e.tensor_tensor(out=t2, in0=c2, in1=d2, op=ALU.mult), ["ec", "sm"], ["t2"])
                        V("dve", lambda e, dst=dst, t1=t1, t2=t2, op=op: e.tensor_tensor(out=dst, in0=t1, in1=t2, op=op), ["t1", "t2"], ["ec"])
                rv = rot.rearrange("c p t -> p c t")
                P.dma("sp", lambda e, q=q: e.dma_start(out=rv[:, q * 8:(q + 1) * 8, 0:T], in_=EC), reads=["ec"], writes=["rot"], key="ecst")
                P.dma("sp", lambda e, q=q: e.dma_start(out=rv[:, q * 8:(q + 1) * 8, T:2 * T], in_=ES), reads=["ec"], writes=["rot"], key="ecst")
                V("dve", lambda e: e.tensor_scalar(out=ESn, in0=ES, scalar1=-1.0, scalar2=None, op0=ALU.mult), ["ec"], ["esn"])
                P.dma("sp", lambda e, q=q: e.dma_start(out=rv[:, q * 8:(q + 1) * 8, 2 * T:3 * T], in_=ESn), reads=["esn"], writes=["rot"], key="esnst")
            P.barrier()
            A.reset(PRO2)
            bl = [A.f32(8 * 128).rearrange("p (c n) -> p c n", c=8) for _ in range(2)]
            Dm = [A.f32(128) for _ in range(3)]
            for g8 in range(8):
                P.dma("sp", lambda e, g8=g8: e.dma_start(out=bl[0], in_=bpr[g8 * 8:(g8 + 1) * 8].rearrange("c p n -> p c n")), writes=["bl0"], key="bl0")
                P.dma("sp", lambda e, g8=g8: e.dma_start(out=bl[1], in_=bpi[g8 * 8:(g8 + 1) * 8].rearrange("c p n -> p c n")), writes=["bl1"], key="bl1")
                for i in range(8):
                    dc = g8 * 8 + i
                    for di, nm in enumerate(("fre", "fim", "nfi")):
                        V("dve", lambda e, di=di, nm=nm, dc=dc: e.tensor_scalar(out=Dm[di], in0=csts[:, 0:128], scalar1=sm[nm][:, dc:dc + 1], scalar2=None, op0=ALU.mult), ["csts", "sm"], [("Dm", di)])
                    ba, pa, ra = bank()
                    bb_, pb_, rb = bank()
                    V("pe", lambda e, pa=pa, i=i: e.matmul(pa[:, 0:128], lhsT=bl[0][:, i, :], rhs=Dm[0], start=True, stop=False), ["bl0", ("Dm", 0)], [ra])
                    V("pe", lambda e, pa=pa, i=i: e.matmul(pa[:, 0:128], lhsT=bl[1][:, i, :], rhs=Dm[2], start=False, stop=True), ["bl1", ("Dm", 2)], [ra])
                    V("pe", lambda e, pb_=pb_, i=i: e.matmul(pb_[:, 0:128], lhsT=bl[1][:, i, :], rhs=Dm[0], start=True, stop=False), ["bl1", ("Dm", 0)], [rb])
                    V("pe", lambda e, pb_=pb_, i=i: e.matmul(pb_[:, 0:128], lhsT=bl[0][:, i, :], rhs=Dm[1], start=False, stop=True), ["bl0", ("Dm", 1)], [rb])
                    V("act", lambda e, pa=pa, dc=dc: e.activation(out=bbr[:, dc, :], in_=pa[:, 0:128], func=AF.Copy), [ra], ["bbt"])
                    V("act", lambda e, pb_=pb_, dc=dc: e.activation(out=bbi[:, dc, :], in_=pb_[:, 0:128], func=AF.Copy), [rb], ["bbt"])
            for g8 in range(8):
                P.dma("sp", lambda e, g8=g8: e.dma_start(out=bl[0], in_=cpr[g8 * 8:(g8 + 1) * 8].rearrange("c p n -> p c n")), writes=["bl0"], key="bl0")
                P.dma("sp", lambda e, g8=g8: e.dma_start(out=bl[1], in_=cpi[g8 * 8:(g8 + 1) * 8].rearrange("c p n -> p c n")), writes=["bl1"], key="bl1")
                V("act", lambda e, g8=g8: e.activation(out=ctr_[:, g8 * 8:(g8 + 1) * 8, :], in_=bl[0], func=AF.Copy), ["bl0"], ["bbt"])
                V("act", lambda e, g8=g8: e.activation(out=cti[:, g8 * 8:(g8 + 1) * 8, :], in_=bl[1], func=AF.Identity, scale=-1.0), ["bl1"], ["bbt"])

            chk(5)
            P.barrier()
            A.reset(S5BASE + 16 * 64 + 128)
            ut = A.bf(8 * T).rearrange("p (k t) -> p k t", k=8)
            rts = [A.f32(3 * T) for _ in range(4)]
            tbs = [[A.f32(T) for _ in range(6)] for _ in range(3)]
            prod = [A.bf(16 * T).rearrange("p (j q t) -> p j q t", j=4, q=4) for _ in range(2)]
            ybf = [A.f32(T) for _ in range(2)]
            ybb = [A.bf(T) for _ in range(2)]
            zl = A.f32(128)
            gt = [A.f32(T) for _ in range(3)]
            rtc = [0]

            def s5_tile(g0, dirn, rev, nseg, use_carry, mode, yoff=None, fin_seq=None):
                ln = T // nseg
                P.dma("sp", lambda e: e.dma_start(out=ut, in_=us.rearrange("(c p) t -> p c t", p=128)[:, :, g0:g0 + T]), writes=["ut"], key="ut")
                yasv = yas.rearrange("(c p) t -> p c t", p=128)
                ybsv = ybs.rearrange("(c p) t -> p c t", p=128)
                zl4 = zl[:, 0:32 * nseg * 2].rearrange("p (c s r) -> p c s r", c=32, r=2)

                def seg3(ap):
                    return ap.rearrange("p (s l) -> p s l", s=nseg)

                items = [(uc, j) for uc in range(8) for j in range(4)]
                ctxs = {}

                def st_a0(i):
                    uc, j = items[i]
                    c = 4 * uc + j
                    dc = dirn * 32 + c
                    ri = rtc[0] % 4
                    rtc[0] += 1
                    rt = rts[ri]
                    rr = ("rt", ri)
                    P.dma("sp", lambda e: e.dma_start(out=rt, in_=rot[dc]), reads=["rot"], writes=[rr], key=rr)
                    Cs = rt[:, 0:ln]; Ss = rt[:, T:T + ln]
                    if rev:
                        Cs = Cs[:, ::-1]; Ss = Ss[:, ::-1]
                    Sn = rt[:, 2 * T:2 * T + ln]
                    if rev:
                        Sn = Sn[:, ::-1]
                    C3 = Cs.unsqueeze(1).to_broadcast([128, nseg, ln]); S3 = Ss.unsqueeze(1).to_broadcast([128, nseg, ln])
                    N3 = Sn.unsqueeze(1).to_broadcast([128, nseg, ln])
                    if mode == "B" and j == 0:
                        ysl = uc % 2
                        P.dma("sp", lambda e: e.dma_start(out=ybf[ysl], in_=yasv[:, uc, yoff:yoff + T]), reads=["yas"], writes=[("ybf", ysl)], key=("ybf", ysl))
                    b1, p1, r1 = bank()
                    b2, p2, r2 = bank()
                    V("pe", lambda e: e.matmul(p1, lhsT=bbr[:, dc, :], rhs=ut[:, uc, :], start=True, stop=True), ["bbt", "ut"], [r1])
                    V("pe", lambda e: e.matmul(p2, lhsT=bbi[:, dc, :], rhs=ut[:, uc, :], start=True, stop=True), ["bbt", "ut"], [r2])
                    si = i % 3
                    ctxs[i] = dict(uc=uc, j=j, c=c, dc=dc, rr=rr, C3=C3, S3=S3, N3=N3, t=tbs[si], sx="_%d" % si, p=(p1, r1, p2, r2))

                def st_a1(i):
                    cx = ctxs[i]
                    rr, C3, S3, N3, sx = cx["rr"], cx["C3"], cx["S3"], cx["N3"], cx["sx"]
                    p1, r1, p2, r2 = cx["p"]
                    t1, t2, t3, t4, zsr, zsi = cx["t"]
                    V("dve", lambda e: e.tensor_tensor(out=seg3(t1), in0=seg3(p1), in1=C3, op=ALU.mult), [r1, rr], ["t1" + sx])
                    V("dve", lambda e: e.tensor_tensor(out=seg3(t2), in0=seg3(p2), in1=S3, op=ALU.mult), [r2, rr], ["t2" + sx])
                    V("dve", lambda e: e.tensor_tensor(out=seg3(t3), in0=seg3(p2), in1=C3, op=ALU.mult), [r2, rr], ["t3" + sx])
                    V("dve", lambda e: e.tensor_tensor(out=seg3(t4), in0=seg3(p1), in1=N3, op=ALU.mult), [r1, rr], ["t4" + sx])
                    identf = csts[:, 0:128]
                    bz1, pz1, rz1 = bank()
                    bz2, pz2, rz2 = bank()
                    V("pe", lambda e: e.matmul(pz1, lhsT=identf, rhs=t1, start=True, stop=False), ["csts", "t1" + sx], [rz1])
                    V("pe", lambda e: e.matmul(pz1, lhsT=identf, rhs=t2, start=False, stop=True), ["csts", "t2" + sx], [rz1])
                    V("pe", lambda e: e.matmul(pz2, lhsT=identf, rhs=t3, start=True, stop=False), ["csts", "t3" + sx], [rz2])
                    V("pe", lambda e: e.matmul(pz2, lhsT=identf, rhs=t4, start=False, stop=True), ["csts", "t4" + sx], [rz2])
                    cx["z"] = (pz1, rz1, pz2, rz2)

                def st_b(i):
                    cx = ctxs[i]
                    c, dc, rr, C3, S3, sx = cx["c"], cx["dc"], cx["rr"], cx["C3"], cx["S3"], cx["sx"]
                    t1, t2, t3, t4, zsr, zsi = cx["t"]
                    rbc = rmag[:, dc:dc + 1].to_broadcast([128, ln])
                    for s_ in range(nseg):
                        sl = slice(s_ * ln, (s_ + 1) * ln)
                        pz1, rz1, pz2, rz2 = cx["z"]
                        for (zo, zin, rix, zres, ores) in ((zsr, pz1, 0, rz1, "zsr" + sx), (zsi, pz2, 1, rz2, "zsi" + sx)):
                            o_ = zo[:, sl]; i_ = zin[:, sl]
                            if rev:
                                o_ = o_[:, ::-1]; i_ = i_[:, ::-1]
                            init = carry[:, dirn, c, rix:rix + 1] if use_carry else 0.0
                            V("dve", lambda e, o_=o_, i_=i_, init=init: e.tensor_tensor_scan(out=o_, data0=rbc, data1=i_, initial=init, op0=ALU.mult, op1=ALU.add),
                              [zres, "carry", "sm"], [ores])
                    pos = 0 if rev else ln - 1
                    V("act", lambda e: e.activation(out=zl4[:, c, :, 0], in_=seg3(zsr)[:, :, pos], func=AF.Copy), ["zsr" + sx], ["zl"])
                    V("act", lambda e: e.activation(out=zl4[:, c, :, 1], in_=seg3(zsi)[:, :, pos], func=AF.Copy), ["zsi" + sx], ["zl"])
                    if mode != "N":
                        N3 = cx["N3"]
                        par_, j_ = cx["uc"] % 2, cx["j"]
                        pr_ = ("prod", par_)
                        pd = prod[par_]
                        V("pool", lambda e: e.tensor_tensor(out=seg3(pd[:, j_, 0, :]), in0=seg3(zsr), in1=C3, op=ALU.mult), ["zsr" + sx, rr], [pr_])
                        V("pool", lambda e: e.tensor_tensor(out=seg3(pd[:, j_, 1, :]), in0=seg3(zsi), in1=N3, op=ALU.mult), ["zsi" + sx, rr], [pr_])
                        V("pool", lambda e: e.tensor_tensor(out=seg3(pd[:, j_, 2, :]), in0=seg3(zsr), in1=S3, op=ALU.mult), ["zsr" + sx, rr], [pr_])
                        V("pool", lambda e: e.tensor_tensor(out=seg3(pd[:, j_, 3, :]), in0=seg3(zsi), in1=C3, op=ALU.mult), ["zsi" + sx, rr], [pr_])

                def st_c(i):
                    cx = ctxs.pop(i)
                    uc, j = cx["uc"], cx["j"]
                    par = uc % 2
                    if mode == "N" or j != 3:
                        return
                    pr_ = ("prod", par)
                    pd = prod[par]
                    by, py, ry = bank()
                    n = 0
                    for jj in range(4):
                        dcc = dirn * 32 + 4 * uc + jj
                        for q in range(4):
                            tab = ctr_ if q < 2 else cti
                            V("pe", lambda e, dcc=dcc, jj=jj, q=q, tab=tab, n=n: e.matmul(py, lhsT=tab[:, dcc, :], rhs=pd[:, jj, q, :], start=(n == 0), stop=(n == 15)), ["bbt", pr_], [ry])
                            n += 1
                    ysl = uc % 2
                    if mode == "A":
                        V("act", lambda e: e.activation(out=ybf[ysl], in_=py, func=AF.Copy), [ry], [("ybf", ysl)])
                        P.dma("sp", lambda e: e.dma_start(out=yasv[:, uc, yoff:yoff + T], in_=ybf[ysl]), reads=[("ybf", ysl)], writes=["yas"], key=("ybf", ysl))
                    else:
                        g1, g2, g3 = gt
                        V("dve", lambda e: e.tensor_tensor(out=g1, in0=py, in1=ybf[ysl], op=ALU.add), [ry, ("ybf", ysl)], ["g1"])
                        V("dve", lambda e: e.scalar_tensor_tensor(out=g1, in0=ut[:, uc, :], scalar=dsks[:, uc:uc + 1], in1=g1, op0=ALU.mult, op1=ALU.add), ["ut", "g1", "dsks"], ["g1"])
                        V("act", lambda e: e.activation(out=g2, in_=g1, func=AF.Square), ["g1"], ["g2"])
                        V("dve", lambda e: e.tensor_scalar(out=g2, in0=g2, scalar1=0.044715, scalar2=1.0, op0=ALU.mult, op1=ALU.add), ["g2"], ["g2"])
                        V("pool", lambda e: e.tensor_tensor(out=g2, in0=g2, in1=g1, op=ALU.mult), ["g2", "g1"], ["g2"])
                        V("act", lambda e: e.activation(out=g3, in_=g2, func=AF.Sigmoid, scale=1.5957691216057308), ["g2"], ["g3"])
                        V("pool", lambda e: e.tensor_tensor(out=ybb[ysl], in0=g1, in1=g3, op=ALU.mult), ["g1", "g3"], [("ybb", ysl)])
                        P.dma("sp", lambda e: e.dma_start(out=ybsv[:, uc, yoff:yoff + T], in_=ybb[ysl]), reads=[("ybb", ysl)], writes=["ybs"], key=("ybb", ysl))

                n_it = len(items)
                for i in range(n_it + 3):
                    if i < n_it:
                        st_a0(i)
                    if 0 <= i - 1 < n_it:
                        st_a1(i - 1)
                    if 0 <= i - 2 < n_it:
                        st_b(i - 2)
                    if 0 <= i - 3 < n_it:
                        st_c(i - 3)
                u1 = gt[0][:, 0:64].rearrange("p (c s) -> p c s", c=32)[:, :, 0:nseg]
                u2 = gt[1][:, 0:64].rearrange("p (c s) -> p c s", c=32)[:, :, 0:nseg]
                if use_carry:
                    cc = CL[:, 9, dirn * 32:(dirn + 1) * 32]; ss = SL[:, 9, dirn * 32:(dirn + 1) * 32]
                    outs = (carry[:, dirn, :, 0], carry[:, dirn, :, 1])
                    zre_, zim_ = zl4[:, :, 0, 0], zl4[:, :, 0, 1]
                    u1_, u2_ = gt[0][:, 0:32], gt[1][:, 0:32]
                else:
                    cc = c255[:, dirn * 32:(dirn + 1) * 32].unsqueeze(2).to_broadcast([128, 32, nseg])
                    ss = s255[:, dirn * 32:(dirn + 1) * 32].unsqueeze(2).to_broadcast([128, 32, nseg])
                    fv = fin[:, fin_seq:fin_seq + nseg, dirn]
                    outs = (fv[:, :, 0, :].rearrange("p s c -> p c s"), fv[:, :, 1, :].rearrange("p s c -> p c s"))
                    zre_, zim_ = zl4[:, :, :, 0], zl4[:, :, :, 1]
                    u1_, u2_ = u1, u2
                V("dve", lambda e: e.tensor_tensor(out=u1_, in0=zre_, in1=cc, op=ALU.mult), ["zl", "sm"], ["g1"])
                V("dve", lambda e: e.tensor_tensor(out=u2_, in0=zim_, in1=ss, op=ALU.mult), ["zl", "sm"], ["g2"])
                V("dve", lambda e: e.tensor_tensor(out=outs[0], in0=u1_, in1=u2_, op=ALU.subtract), ["g1", "g2"], ["carry", "fin"])
                V("dve", lambda e: e.tensor_tensor(out=u1_, in0=zre_, in1=ss, op=ALU.mult), ["zl", "sm"], ["g1"])
                V("dve", lambda e: e.tensor_tensor(out=u2_, in0=zim_, in1=cc, op=ALU.mult), ["zl", "sm"], ["g2"])
                V("dve", lambda e: e.tensor_tensor(out=outs[1], in0=u1_, in1=u2_, op=ALU.add), ["g1", "g2"], ["carry", "fin"])

            for pt in range(2):
                s5_tile(pt * T, 0, False, 2, False, "A", yoff=pt * T, fin_seq=2 * pt)
                s5_tile(pt * T, 1, True, 2, False, "B", yoff=pt * T, fin_seq=2 * pt)
            for i in range(4):
                s5_tile(1024 + i * T, 0, False, 1, True, "A", yoff=1024 + i * T)
            for k in (3, 2, 1, 0):
                s5_tile(3072 + k * T, 1, True, 1, True, "N")
            for i in (3, 2, 1, 0):
                s5_tile(1024 + i * T, 1, True, 1, True, "B", yoff=1024 + i * T)
            P.dma("sp", lambda e: e.dma_start(out=ns, in_=fin.rearrange("p s d r c -> p (s d r c)")), reads=["fin"], writes=["ns"], key="nsst")

            chk(6)
            def attention(ti):
                g0 = ti * T
                prompt = ti < 2
                A.reset(PERS)
                Qt = A.bf(8 * T).rearrange("p (k t) -> p k t", k=8)
                nkw = T if prompt else 1024
                Kw = A.bf(8 * nkw).rearrange("p (k t) -> p k t", k=8)
                nvb = nkw // 128
                Vw = A.bf(nvb * 1040).rearrange("p (b h e) -> p b h e", b=nvb, h=16)
                npart = 128 if prompt else 64
                Osb = A.bf(8 * 1024).rearrange("p (b f) -> p b f", b=8)
                pTs = [A.bf(4 * T).rearrange("p (b t) -> p b t", b=4) for _ in range(2)]
                rcs = [A.f32(4) for _ in range(4)]
                rcc = [0]

                def tmp():
                    i = rcc[0] % 4
                    rcc[0] += 1
                    return rcs[i], ("rc", i)
                qv = qs.rearrange("(c p) t -> p c t", p=128)
                kv = ks.rearrange("(c p) t -> p c t", p=128)
                vv = vs.rearrange("(b p) e -> p b e", p=128)
                P.dma("sp", lambda e: e.dma_start(out=Qt, in_=qv[:, :, g0:g0 + T]), writes=["Qt"], key="Qt")
                if prompt:
                    k0 = g0
                else:
                    r0 = 8 * (ti - 2)
                    wr0 = max(r0 - 4, 0)
                    k0 = 1024 + wr0 * 64
                P.dma("sp", lambda e: e.dma_start(out=Kw, in_=kv[:, :, k0:k0 + nkw]), writes=["Kw"], key="Kw")
                P.dma("sp", lambda e: e.dma_start(out=Vw.rearrange("p b h e -> p b (h e)"), in_=vv[:, k0 // 128:k0 // 128 + nvb, :]), writes=["Vw"], key="Vw")
                if not prompt:
                    cK = A.bf(8 * 512).rearrange("p (k t) -> p k t", k=8)
                    cV = A.bf(4 * 1040).rearrange("p (b h e) -> p b h e", b=4, h=16)
                    _CACHE.setdefault("offs", {})[("Tb", ti)] = A.off
                    Tb = A.bf(16 * NSLOT * 64).rearrange("p (h s q) -> p h s q", h=16, s=NSLOT)
                    pTl = [A.bf(320) for _ in range(2)]
                    P.dma("pool", lambda e: e.dma_start(out=cK.rearrange("p k t -> p (k t)").rearrange("p (a b) -> p a b", a=4), in_=ckT.rearrange("p (a b) -> p a b", a=4)), writes=["cK"], key="cK")
                    V("dve", lambda e: e.memset(cV[:, :, :, 64:65], 1.0), [], ["cV"])
                    P.dma("pool", lambda e: e.dma_start(out=cV[:, :, :, 0:64], in_=cvt.rearrange("p (b h d) -> p b h d", b=4, h=16)), writes=["cV"], key="cV")
                    P.dma("pool", lambda e: e.dma_start(out=Tb[0:64].rearrange("p h s q -> p h (s q)"), in_=rpbx.rearrange("p (h x) -> p h x", h=16)), writes=["Tb"], key="Tb")
                    P.dma("pool", lambda e: e.dma_start(out=Tb[64:128].rearrange("p h s q -> p h (s q)"), in_=rpbx.rearrange("p (h x) -> p h x", h=16)), writes=["Tb"], key="Tb")
                if ti == 2:
                    chk(8.01)
                bfv = lambda ps: ps.bitcast(BF16)
                pc = [0]
                if prompt:
                    for hg in range(4):
                        for s_ in range(2):
                            pts = []
                            for hh in range(4):
                                h = hg * 4 + hh
                                ch, pb = h // 2, 64 * (h % 2)
                                b, ps, pr = bank()
                                for kb in range(2):
                                    V("pe", lambda e, ps=ps, kb=kb, ch=ch, pb=pb, s_=s_: e.matmul(ps[:, kb * 256:(kb + 1) * 256], lhsT=Kw[pb:pb + 64, ch, s_ * 256 + kb * 128:s_ * 256 + (kb + 1) * 128],
                                                                                                 rhs=Qt[pb:pb + 64, ch, s_ * 256:(s_ + 1) * 256], start=True, stop=True), ["Kw", "Qt"], [pr])
                                pi = pc[0] % 8
                                pc[0] += 1
                                pT = pTs[pi // 4][:, pi % 4, :]
                                V("act", lambda e, pT=pT, ps=ps: e.activation(out=pT, in_=ps, func=AF.Exp), [pr], [("pT", pi)])
                                pts.append((pT, ("pT", pi)))
                            for qb in range(2):
                                b, ps, pr = bank()
                                o4 = ps.rearrange("p (h e) -> p h e", h=4)
                                for hh in range(4):
                                    h = hg * 4 + hh
                                    pT, ptr = pts[hh]
                                    for kb in range(2):
                                        V("pe", lambda e, o4=o4, hh=hh, pT=pT, kb=kb, qb=qb, h=h, s_=s_: e.matmul(o4[:, hh, 0:65], lhsT=pT[:, kb * 256 + qb * 128:kb * 256 + (qb + 1) * 128],
                                                                                                                  rhs=Vw[:, s_ * 2 + kb, h, :], start=(kb == 0), stop=(kb == 1)), [ptr, "Vw"], [pr])
                                rc, rcr = tmp()
                                rc4 = rc[:, 0:4].unsqueeze(2)
                                V("dve", lambda e, rc4=rc4, o4=o4: e.reciprocal(out=rc4, in_=o4[:, :, 64:65]), [pr], [rcr])
                                V("dve", lambda e, rc4=rc4, o4=o4, s_=s_, qb=qb, hg=hg: e.tensor_tensor(out=Osb[:, s_ * 2 + qb, hg * 256:(hg + 1) * 256].rearrange("p (h d) -> p h d", h=4), in0=o4[:, :, 0:64],
                                                                                                   in1=rc4.to_broadcast([128, 4, 64]), op=ALU.mult), [pr, rcr], ["Osb"])
                    for fc in range(8):
                        b, ps, pr = bank()
                        pbv = bfv(ps)
                        for blk in range(4):
                            V("pe", lambda e, pbv=pbv, blk=blk, fc=fc: e.transpose(out=pbv[:, blk * 128:(blk + 1) * 128], in_=Osb[:, blk, fc * 128:(fc + 1) * 128], identity=identb), ["Osb", "identb"], [pr])
                        V("act", lambda e, pbv=pbv, fc=fc: e.activation(out=ATv[:, fc, :], in_=pbv[:, 0:T], func=AF.Copy), [pr], ["AT"])
                else:
                    sbc = [0]
                    obc = [0]

                    def sbank():
                        bi = sbc[0] % 6
                        sbc[0] += 1
                        return bi, psb[bi][:], ("ps", bi)

                    def obank():
                        bi = 6 + obc[0] % 2
                        obc[0] += 1
                        return bi, psb[bi][:], ("ps", bi)

                    jobs = [(h, half, r4) for h in range(16) for half in range(2) for r4 in range(4)]
                    jctx = {}

                    def S1(n):
                        h, half, r4 = jobs[n]
                        ch, pb = h // 2, 64 * (h % 2)
                        pTc = pTs[h % 2]
                        if half == 0 and r4 == 0:
                            for blk in range(4):
                                b_, ps_, pr_ = sbank()
                                V("pe", lambda e, ps_=ps_, blk=blk: e.matmul(ps_, lhsT=cK[pb:pb + 64, ch, blk * 128:(blk + 1) * 128], rhs=Qt[pb:pb + 64, ch, :], start=True, stop=True), ["cK", "Qt"], [pr_])
                                V("act", lambda e, ps_=ps_, blk=blk: e.activation(out=pTc[:, blk, :], in_=ps_, func=AF.Exp), [pr_], [("pTc", h % 2)])
                        rr_ = half * 4 + r4
                        r_p = r0 + rr_
                        if r_p < 4:
                            sr, npair, edge = 0, 4, True
                        else:
                            sr, npair, edge = (r_p - 4) & ~1, 5, False
                        b, ps, pr = sbank()
                        for pp in range(npair):
                            kt = (sr - wr0 + 2 * pp) * 64
                            V("pe", lambda e, pp=pp, kt=kt: e.matmul(ps[:, pp * 64:(pp + 1) * 64], lhsT=Kw[pb:pb + 64, ch, kt:kt + 128], rhs=Qt[pb:pb + 64, ch, rr_ * 64:(rr_ + 1) * 64],
                                                                     start=True, stop=False), ["Kw", "Qt"], [pr])
                            for jj in range(2):
                                off = sr + 2 * pp + jj - r_p
                                slot = (11 + off + 3) if edge else (off + 5)
                                assert 0 <= slot < NSLOT
                                V("pe", lambda e, pp=pp, jj=jj, slot=slot: e.matmul(ps[:, pp * 64:(pp + 1) * 64], lhsT=selb[pb:pb + 64, jj * 128:(jj + 1) * 128], rhs=Tb[pb:pb + 64, h, slot, :],
                                                                                 start=False, stop=(jj == 1)), ["selb", "Tb"], [pr])
                        li = pc[0] % 2
                        pc[0] += 1
                        pl = pTl[li]
                        V("act", lambda e: e.activation(out=pl[:, 0:npair * 64], in_=ps[:, 0:npair * 64], func=AF.Exp), [pr], [("pTl", li)])
                        jctx[n] = (pl, li, sr, npair, rr_, pTc)

                    ocur = [None]

                    def S2(n):
                        h, half, r4 = jobs[n]
                        pl, li, sr, npair, rr_, pTc = jctx.pop(n)
                        if r4 == 0:
                            bo, pso, pro = obank()
                            ocur[0] = (pso.rearrange("p (r e) -> p r e", r=4), pro)
                        o4, pro = ocur[0]
                        for pp in range(npair):
                            V("pe", lambda e, pp=pp: e.matmul(o4[0:64, r4, 0:65], lhsT=pl[:, pp * 64:(pp + 1) * 64], rhs=Vw[:, (sr - wr0) // 2 + pp, h, :], start=(pp == 0), stop=False),
                              [("pTl", li), "Vw"], [pro])
                        for blk in range(4):
                            V("pe", lambda e, blk=blk: e.matmul(o4[0:64, r4, 0:65], lhsT=pTc[:, blk, rr_ * 64:(rr_ + 1) * 64], rhs=cV[:, blk, h, :], start=False, stop=(blk == 3)),
                              [("pTc", h % 2), "cV"], [pro])
                        if r4 == 3:
                            rc, rcr = tmp()
                            rc4 = rc[0:64, 0:4].unsqueeze(2)
                            V("dve", lambda e: e.reciprocal(out=rc4, in_=o4[0:64, :, 64:65]), [pro], [rcr])
                            V("dve", lambda e: e.tensor_tensor(out=Osb[0:64, half * 4:(half + 1) * 4, h * 64:(h + 1) * 64], in0=o4[0:64, :, 0:64], in1=rc4.to_broadcast([64, 4, 64]), op=ALU.mult),
                              [pro, rcr], ["Osb"])

                    S1(0)
                    for n in range(len(jobs)):
                        if n + 1 < len(jobs):
                            S1(n + 1)
                        S2(n)
                    if ti == 2:
                        chk(8.04)
                    for fc in range(8):
                        b, ps, pr = bank()
                        pbv = bfv(ps)
                        for rr_ in range(8):
                            V("pe", lambda e, pbv=pbv, rr_=rr_, fc=fc: e.transpose(out=pbv[:, rr_ * 64:(rr_ + 1) * 64], in_=Osb[0:64, rr_, fc * 128:(fc + 1) * 128], identity=identb[0:64, 0:64]), ["Osb", "identb"], [pr])
                        V("act", lambda e, pbv=pbv, fc=fc: e.activation(out=ATv[:, fc, :], in_=pbv[:, 0:T], func=AF.Copy), [pr], ["AT"])

            def phaseB(ti):
                g0 = ti * T
                ci = 0 if ti < 2 else 1
                P.barrier()
                attention(ti)
                if ti == 0:
                    chk(6.1)
                if ti == 2:
                    chk(8.1)
                P.barrier()
                xv, hv, rs = alloc_common()
                yb = A.bf(8 * T).rearrange("p (k t) -> p k t", k=8)
                yb2 = A.bf(8 * T).rearrange("p (k t) -> p k t", k=8)
                mg_ = A.bf(16 * T).rearrange("p (k t) -> p k t", k=16)
                P.dma("sp", lambda e: e.dma_start(out=yb, in_=ybs.rearrange("(c p) t -> p c t", p=128)[:, :, g0:g0 + T]), writes=["yb"], key="yb")
                P.dma("sp", lambda e: e.dma_start(out=xv, in_=x1s.rearrange("(k p) t -> p k t", p=128)[:, :, g0:g0 + T]), writes=XR, key="xld")
                wv, wr = wslot()
                wv3 = wv.rearrange("p (k n) -> p k n", k=8)
                wdma(wv3, w_glu.rearrange("(k p) n -> p k n", p=128), wr)
                for m in range(8):
                    b, ps, pr = bank()
                    for k in range(8):
                        V("pe", lambda e, ps=ps, k=k, m=m, wv3=wv3: e.matmul(ps, lhsT=wv3[:, k, m * 128:(m + 1) * 128], rhs=yb[:, k, :], start=(k == 0), stop=(k == 7)), [wr, "yb"], [pr])
                    sg, sr_ = tmp()
                    V("act", lambda e, sg=sg, ps=ps: e.activation(out=sg, in_=ps, func=AF.Sigmoid), [pr], [sr_])
                    V("dve", lambda e, sg=sg, m=m: e.tensor_tensor(out=yb2[:, m, :], in0=sg, in1=yb[:, m, :], op=ALU.mult), [sr_, "yb"], ["yb2"])
                norm_mod(xv, hv, rs, 1, ci)
                if ti == 0:
                    chk(6.3)
                upa = w_up_a.rearrange("(k p) n -> p k n", p=128)
                upb = w_up_b.rearrange("(k p) n -> p k n", p=128)
                for mg in range(4):
                    wa, wra = wslot()
                    wa3 = wa.rearrange("p (k n) -> p k n", k=16)
                    wdma(wa3, win_v[:, :, 4096 + mg * 512:4096 + (mg + 1) * 512], wra)
                    wb, wrb = wslot()
                    wb3 = wb.rearrange("p (k n) -> p k n", k=16)
                    wdma(wb3, win_v[:, :, 6144 + mg * 512:6144 + (mg + 1) * 512], wrb)
                    wu, wru = wslot()
                    wu4 = wu.rearrange("p (g k n) -> p g k n", g=2, k=8)
                    wdma(wu4[:, 0], upa[:, :, mg * 512:(mg + 1) * 512], wru)
                    wdma(wu4[:, 1], upb[:, :, mg * 512:(mg + 1) * 512], wru)
                    for mm in range(4):
                        m = mg * 4 + mm
                        b1, p1, r1 = bank(); b2, p2, r2 = bank(); b3, p3, r3 = bank(); b4, p4, r4 = bank()
                        for k in range(16):
                            V("pe", lambda e, p1=p1, k=k, mm=mm, wa3=wa3: e.matmul(p1, lhsT=wa3[:, k, mm * 128:(mm + 1) * 128], rhs=hv[:, k, :], start=(k == 0), stop=(k == 15)), [wra, ("h", k)], [r1])
                        for k in range(8):
                            V("pe", lambda e, p2=p2, k=k, mm=mm, wu4=wu4: e.matmul(p2, lhsT=wu4[:, 0, k, mm * 128:(mm + 1) * 128], rhs=ATv[:, k, :], start=(k == 0), stop=(k == 7)), [wru, "AT"], [r2])
                        for k in range(16):
                            V("pe", lambda e, p3=p3, k=k, mm=mm, wb3=wb3: e.matmul(p3, lhsT=wb3[:, k, mm * 128:(mm + 1) * 128], rhs=hv[:, k, :], start=(k == 0), stop=(k == 15)), [wrb, ("h", k)], [r3])
                        for k in range(8):
                            V("pe", lambda e, p4=p4, k=k, mm=mm, wu4=wu4: e.matmul(p4, lhsT=wu4[:, 1, k, mm * 128:(mm + 1) * 128], rhs=yb2[:, k, :], start=(k == 0), stop=(k == 7)), [wru, "yb2"], [r4])
                        s1, s1r = tmp(); s2, s2r = tmp()
                        V("act", lambda e, s1=s1, p1=p1: e.activation(out=s1, in_=p1, func=AF.Sigmoid), [r1], [s1r])
                        V("dve", lambda e, s1=s1, p2=p2: e.tensor_tensor(out=s1, in0=s1, in1=p2, op=ALU.mult), [s1r, r2], [s1r])
                        V("act", lambda e, s2=s2, p3=p3: e.activation(out=s2, in_=p3, func=AF.Sigmoid), [r3], [s2r])
                        V("dve", lambda e, s2=s2, p4=p4: e.tensor_tensor(out=s2, in0=s2, in1=p4, op=ALU.mult), [s2r, r4], [s2r])
                        V("pool", lambda e, s1=s1, s2=s2, m=m: e.tensor_tensor(out=mg_[:, m, :], in0=s1, in1=s2, op=ALU.add), [s1r, s2r], [("mg", m)])

                if ti == 0:
                    chk(6.5)

                def ev_o(c, ps, pr):
                    V("dve", lambda e: e.scalar_tensor_tensor(out=xv[:, c, :], in0=ps, scalar=mods[:, 80 + c, ci:ci + 1], in1=xv[:, c, :], op0=ALU.mult, op1=ALU.add), [pr, ("x", c), "mods"], [("x", c)])
                linear_fm(lambda k: mg_[:, k, :], (lambda k: [("mg", k)]), 16, lambda cb: w_out.rearrange("(k p) n -> p k n", p=128)[:, :, cb * 512:(cb + 1) * 512], 4, ev_o)
                P.barrier()
                if ti == 0:
                    chk(6.7)
                xv2, hv2, rs2 = alloc_common()
                actv2 = A.bf(44 * T).rearrange("p (j t) -> p j t", j=44)
                norm_mod(xv2, hv2, rs2, 2, ci)
                ffn(1, xv2, hv2, actv2, ci)
                rstd_of(xv2, rs2)
                for k in range(16):
                    V("dve", lambda e, k=k: e.scalar_tensor_tensor(out=xv2[:, k, :], in0=xv2[:, k, :], scalar=fgs[:, k:k + 1], in1=rs2, op0=ALU.mult, op1=ALU.mult), [("x", k), "rs", "fgs"], [("x", k)])
                dst = yp if ti < 2 else ys
                t0 = g0 if ti < 2 else g0 - 1024
                P.dma("sp", lambda e: e.dma_start(out=dst.rearrange("(k p) t -> p k t", p=128)[:, :, t0:t0 + T], in_=xv2), reads=XR, writes=[("y", ti)], key="xst")

            for ti in range(6):
                phaseB(ti)
                chk(7 + ti)
        except _Stop:
            pass
        P.emit()
    return nc, P.stats


def _bias_table(rpb, flip):
    tb = np.full((64, 16, NSLOT, 64), NEG, np.float32)
    kc = np.arange(64)[:, None]
    qc = np.arange(64)[None, :]
    if not flip:
        cok = (kc >= np.clip(qc - 8, 0, 48)) & (kc <= np.clip(qc - 8, 0, 48) + 15)
        dc = kc - qc + 15
    else:
        cok = (kc >= np.clip(qc - 7, 0, 48)) & (kc <= np.clip(qc - 7, 0, 48) + 15)
        dc = qc - kc + 15
    dcc = np.clip(dc, 0, 30)
    for slot in range(NSLOT):
        edge = slot >= 11
        off = (slot - 11 - 3) if edge else (slot - 5)
        dr = (7 - off) if flip else (off + 7)
        if edge:
            valid = 0 <= dr <= 14
        else:
            valid = (-3 <= off <= 4) if flip else (-4 <= off <= 3)
        if not valid:
            continue
        g = rpb[:, dr, :][:, dcc]
        g = np.where(cok[None], g, np.float32(NEG))
        tb[:, :, slot, :] = np.transpose(g, (1, 0, 2))
    return np.ascontiguousarray(tb.reshape(64, -1))


def _pad_bc(w, c_layout):
    out = np.zeros((2, 32, 2, 64, 4, 2, 16), np.float32)
    for c in range(32):
        for gj in range(2):
            g = 2 * c + gj
            if c_layout:
                blk = np.transpose(w[:, g], (0, 2, 1))
            else:
                blk = w[:, g]
            out[:, c, gj, :, c % 4, gj, :] = blk
    return np.ascontiguousarray(out.reshape(64, 128, 128))


_CACHE = {}


def kernel(x_prompt, x_sample, cache_k, cache_v, state_ssm, c, c_ctx, w_ada, b_ada, norm_g,
           ffn_in, ffn_out, w_in, rpb, s5_lam_re, s5_lam_im, s5_log_dt, s5_b_re, s5_b_im,
           s5_c_re, s5_c_im, s5_d, w_glu, w_up_a, w_up_b, w_out, final_g):
    f = np.float32
    A_ = lambda a: np.ascontiguousarray(np.asarray(a, dtype=f))
    x_prompt, x_sample = A_(x_prompt), A_(x_sample)
    if "nc" not in _CACHE:
        _CACHE["nc"] = build_program()
    nc, stats = _CACHE["nc"]
    print("program stats (ops, eng counts, nsems, max dma sem):", stats, flush=True)

    def chunked(v, nchunk):
        return np.ascontiguousarray(np.asarray(v, f).reshape(nchunk, 128).T)

    shared = dict(
        w_ada=A_(w_ada[0]), b_ada=chunked(b_ada[0], 144),
        ng=np.ascontiguousarray(np.concatenate([chunked(norm_g[0, n], 16) for n in range(3)], axis=1)),
        fg=chunked(final_g, 16), ffn_in=A_(ffn_in[0]), ffn_out=A_(ffn_out[0]), w_in=A_(w_in[0]),
        w_glu=A_(w_glu[0]), w_up_a=A_(w_up_a[0]), w_up_b=A_(w_up_b[0]), w_out=A_(w_out[0]),
        dsk=chunked(s5_d[0], 8),
    )
    cstm = np.zeros((128, 384), f)
    cstm[:, 0:128] = np.eye(128, dtype=f)
    cstm[0:64, 128:192] = np.eye(64, dtype=f)
    cstm[0:64, 256 + 64:256 + 128] = np.eye(64, dtype=f)
    cstm[64:128, 128:192] = np.eye(64, dtype=f)
    cstm[64:128, 256 + 64:256 + 128] = np.eye(64, dtype=f)
    shared["cst"] = cstm

    def chan(a):
        a = np.asarray(a, f).reshape(2, 32, 2, 64)
        return np.ascontiguousarray(np.transpose(a, (2, 3, 0, 1)).reshape(128, 64))

    per_orient = {}
    for flip in (False, True):
        sw = (lambda a: np.asarray(a, f)[::-1]) if flip else (lambda a: np.asarray(a, f))
        ldt_full = np.broadcast_to(np.asarray(s5_log_dt[0], f)[:, :, None], (2, 64, 64))
        per_orient[flip] = dict(
            lamre=chan(sw(s5_lam_re[0])), lamim=chan(sw(s5_lam_im[0])), ldt=chan(sw(ldt_full)),
            bpr=_pad_bc(sw(s5_b_re[0]), False), bpi=_pad_bc(sw(s5_b_im[0]), False),
            cpr=_pad_bc(sw(s5_c_re[0]), True), cpi=_pad_bc(sw(s5_c_im[0]), True),
            rpbx=_bias_table(np.asarray(rpb[0], f), flip),
        )

    in_maps = []
    for core in range(8):
        flip = (core % 2 == 1)
        b = core // 2
        xpT = x_prompt[4 * core:4 * core + 4]
        xsq = x_sample[b]
        if flip:
            xpT = xpT[:, ::-1]
            xsq = xsq[::-1]
        m = dict(shared)
        m.update(per_orient[flip])
        m["xp"] = np.ascontiguousarray(xpT.reshape(1024, D).T)
        m["xo"] = np.ascontiguousarray(xsq[0:2048].T)
        m["xh"] = np.ascontiguousarray(xsq[2048:4096].T)
        cc = np.stack([np.asarray(c_ctx, f), np.asarray(c[b], f)], axis=1)
        m["cond"] = np.ascontiguousarray(cc.reshape(16, 128, 2).transpose(1, 0, 2).reshape(128, 32))
        st_ = np.asarray(state_ssm[b, 0], f)
        if flip:
            st_ = st_[::-1]
        st_ = st_.reshape(2, 2, 32, 2, 64)
        m["h0"] = np.ascontiguousarray(np.transpose(st_, (3, 4, 0, 1, 2)).reshape(128, 128))
        ck = np.asarray(cache_k[b, 0], f)
        m["ckT"] = np.ascontiguousarray(np.transpose(ck.reshape(8, 2, 512, 64), (1, 3, 0, 2)).reshape(128, 4096))
        cv = np.asarray(cache_v[b, 0], f)
        m["cvt"] = np.ascontiguousarray(np.transpose(cv.reshape(16, 4, 128, 64), (2, 1, 0, 3)).reshape(128, 4096))
        in_maps.append(m)

    if _CACHE.get("sim_hook") is not None:
        R = _CACHE["sim_hook"](nc, in_maps)
    else:
        res = run_bass_kernel_spmd(nc, in_maps, core_ids=list(range(8)))
        R = res.results
    y_prompt = np.zeros((32, 256, D), f); y_sample = np.zeros((4, 4096, D), f)
    nck = np.zeros((32, 1, 16, 256, 64), f); ncv = np.zeros((32, 1, 16, 256, 64), f)
    nss = np.zeros((32, 1, 2, 2, 64, 64), f)
    for core in range(8):
        flip = (core % 2 == 1)
        b = core // 2
        r = R[core]
        ypc = np.asarray(r["yp"]).T.reshape(4, 256, D)
        ysc = np.asarray(r["ys"]).T
        kk = np.asarray(r["nk"]).reshape(16, 64, 4, 256)
        kk = np.transpose(kk, (2, 0, 3, 1))
        vv = np.asarray(r["nv"]).reshape(4, 256, 16, 64)
        vv = np.transpose(vv, (0, 2, 1, 3))
        s_ = np.asarray(r["ns"]).reshape(2, 64, 4, 2, 2, 32)
        s_ = np.transpose(s_, (2, 3, 4, 5, 0, 1)).reshape(4, 2, 2, 64, 64)
        if flip:
            ypc = ypc[:, ::-1]
            kk = kk[:, :, ::-1]
            vv = vv[:, :, ::-1]
            s_ = s_[:, ::-1]
            y_sample[b, 2048:4096] = ysc[::-1]
        else:
            y_sample[b, 0:2048] = ysc
        y_prompt[4 * core:4 * core + 4] = ypc
        nck[4 * core:4 * core + 4, 0] = kk
        ncv[4 * core:4 * core + 4, 0] = vv
        nss[4 * core:4 * core + 4, 0] = s_
    return (y_prompt, y_sample, nck, ncv, nss)
```

```python
import contextlib
import math
import numpy as np
import concourse.bass as bass
import concourse.mybir as mybir
from concourse.bass_utils import run_bass_kernel_spmd

F32 = mybir.dt.float32
BF16 = mybir.dt.bfloat16
ALU = mybir.AluOpType
AF = mybir.ActivationFunctionType

T = 512
NEG = -30000.0
EPS = 1e-6
D = 2048
DFF = 5632
NSLOT = 22
ARENA = 52800
STAGE = 99
KSQ = 'act'


class Prog:
    ENGS = ("pe", "dve", "act", "pool", "sp")

    def __init__(self, nc):
        self.nc = nc
        self.ops = []
        self.last_w = {}
        self.readers = {}
        self.pending = {}
        self.last_eng = {}
        self.dma_since = []

    def _add(self, eng, fn, reads, writes, dma_key=None):
        idx = len(self.ops)
        deps = set()
        for r in reads:
            w = self.last_w.get(r)
            if w is not None:
                deps.add(w)
        for r in writes:
            w = self.last_w.get(r)
            if w is not None:
                deps.add(w)
            rl = self.readers.get(r)
            if rl:
                deps.update(rl.values())
        rkey = eng if dma_key is None else ("dma", idx)
        for r in reads:
            self.readers.setdefault(r, {})[rkey] = idx
        for r in writes:
            self.last_w[r] = idx
            self.readers[r] = {}
        pb = self.pending.pop(eng, None)
        if pb:
            deps.update(pb)
        deps.discard(idx)
        self.ops.append(dict(eng=eng, fn=fn, deps=deps, dma=dma_key))
        self.last_eng[eng] = idx
        if dma_key is not None:
            self.dma_since.append(idx)
        return idx

    def op(self, eng, fn, reads=(), writes=()):
        return self._add(eng, fn, tuple(reads), tuple(writes))

    def dma(self, queue, fn, reads=(), writes=(), key=None):
        return self._add(queue, fn, tuple(reads), tuple(writes), dma_key=key)

    def barrier(self):
        b = set(self.last_eng.values()) | set(self.dma_since)
        self.dma_since = []
        for e in self.ENGS:
            self.pending[e] = set(b) | self.pending.get(e, set())

    def emit(self):
        nc = self.nc
        ops = self.ops
        needed = set()
        for i, o in enumerate(ops):
            nd = set()
            for d in o["deps"]:
                od = ops[d]
                if od["dma"] is None and od["eng"] == "pe" and o["eng"] == "pe" and o["dma"] is None:
                    continue
                nd.add(d)
            o["deps"] = nd
            needed |= nd
        cnt = {e: 0 for e in self.ENGS}
        dcnt = {}
        for i, o in enumerate(ops):
            if o["dma"] is not None:
                k = o["dma"]
                dcnt[k] = dcnt.get(k, 0) + 16
                o["sig"] = ("d:" + str(k), dcnt[k])
            elif i in needed:
                cnt[o["eng"]] += 1
                o["sig"] = ("e:" + o["eng"], cnt[o["eng"]])
            else:
                o["sig"] = None
        semnames = ["e:" + e for e in self.ENGS] + ["d:" + str(k) for k in dcnt]
        self.stats = (len(ops), dict(cnt), len(semnames), max(dcnt.values()) if dcnt else 0)
        per_eng = {e: [] for e in self.ENGS}
        for i, o in enumerate(ops):
            per_eng[o["eng"]].append(i)
        with contextlib.ExitStack() as st:
            sems = {}
            for j, n in enumerate(semnames):
                sems[n] = st.enter_context(nc.semaphore("s%d" % j))
            block = st.enter_context(nc.Block())

            def make(e):
                def body(eng):
                    known = {}
                    for i in per_eng[e]:
                        o = ops[i]
                        w = {}
                        for d in o["deps"]:
                            s, v = ops[d]["sig"]
                            if v > w.get(s, 0):
                                w[s] = v
                        for s, v in w.items():
                            if known.get(s, 0) >= v:
                                continue
                            known[s] = v
                            eng.wait_ge(sems[s], v)
                        ins = o["fn"](eng)
                        if o["sig"] is not None:
                            s, v = o["sig"]
                            ins.then_inc(sems[s], 16 if o["dma"] is not None else 1)
                    if e == "sp":
                        for k, v in dcnt.items():
                            if known.get("d:" + str(k), 0) < v:
                                eng.wait_ge(sems["d:" + str(k)], v)
                return body

            block.tensor(make("pe"))
            block.vector(make("dve"))
            block.scalar(make("act"))
            block.gpsimd(make("pool"))
            block.sync(make("sp"))


class Arena:
    def __init__(self, ap, size):
        self.ap, self.size, self.off = ap, size, 0

    def f32(self, n):
        v = self.ap[:, self.off:self.off + n]
        self.off += n
        assert self.off <= self.size, ("arena overflow", self.off)
        return v

    def bf(self, n):
        nf = (n + 1) // 2
        return self.f32(nf).bitcast(BF16)

    def reset(self, to):
        self.off = to


def build_program():
    nc = bass.Bass("TRN2", target_bir_lowering=False)

    def din(name, shape, dt=F32):
        return nc.dram_tensor(name, list(shape), dt, kind="ExternalInput").ap()

    def dout(name, shape, dt=F32):
        return nc.dram_tensor(name, list(shape), dt, kind="ExternalOutput").ap()

    def dscr(name, shape, dt=F32):
        return nc.dram_tensor(name, list(shape), dt, kind="Internal").ap()

    xp = din("xp", [D, 1024]); xo = din("xo", [D, 2048]); xh = din("xh", [D, 2048])
    cond = din("cond", [128, 32])
    w_ada = din("w_ada", [D, 9 * D]); b_ada = din("b_ada", [128, 144])
    ng = din("ng", [128, 48]); fg = din("fg", [128, 16])
    ffn_in = din("ffn_in", [2, D, 2 * DFF]); ffn_out = din("ffn_out", [2, DFF, D])
    w_in = din("w_in", [D, 8192]); w_glu = din("w_glu", [1024, 1024])
    w_up_a = din("w_up_a", [1024, D]); w_up_b = din("w_up_b", [1024, D]); w_out = din("w_out", [D, D])
    lamre = din("lamre", [128, 64]); lamim = din("lamim", [128, 64]); ldt = din("ldt", [128, 64])
    bpr = din("bpr", [64, 128, 128]); bpi = din("bpi", [64, 128, 128])
    cpr = din("cpr", [64, 128, 128]); cpi = din("cpi", [64, 128, 128])
    dsk = din("dsk", [128, 8]); h0 = din("h0", [128, 128])
    ckT = din("ckT", [128, 4096]); cvt = din("cvt", [128, 4096])
    rpbx = din("rpbx", [64, 16 * NSLOT * 64])
    cst = din("cst", [128, 128 + 256])

    yp = dout("yp", [D, 1024]); ys = dout("ys", [D, 2048])
    nk = dout("nk", [1024, 1024]); nv = dout("nv", [1024, 1024]); ns = dout("ns", [128, 512])

    x1s = dscr("x1s", [D, 3072]); qs = dscr("qs", [1024, 3072], BF16); ks = dscr("ks", [1024, 3584], BF16)
    vs = dscr("vs", [3584, 1040], BF16); us = dscr("us", [1024, 5120], BF16)
    yas = dscr("yas", [1024, 3072]); ybs = dscr("ybs", [1024, 3072], BF16)
    rot = dscr("rot", [64, 128, 1536])

    P = Prog(nc)
    with contextlib.ExitStack() as st:
        E = st.enter_context
        arena_t = E(nc.sbuf_tensor("arena", [128, ARENA], F32))
        psb = [E(nc.psum_tensor("ps%d" % i, [128, 512], F32)) for i in range(8)]
        A = Arena(arena_t[:], ARENA)
        bankc = [0]

        def bank():
            b = bankc[0] % 8
            bankc[0] += 1
            return b, psb[b][:], ("ps", b)

        mods = A.f32(288).rearrange("p (m c) -> p m c", c=2)
        der = A.f32(160).rearrange("p (n k c) -> p n k c", n=5, k=16)
        ngs = A.f32(48); fgs = A.f32(16)
        onesf = A.bf(128)
        csts = A.f32(384)
        identb = A.bf(128)
        selb = A.bf(256)
        epsb = A.f32(1); hpib = A.f32(1)
        rmag = A.f32(64)
        CL = A.f32(640).rearrange("p (k c) -> p k c", k=10)
        SL = A.f32(640).rearrange("p (k c) -> p k c", k=10)
        c255 = A.f32(64); s255 = A.f32(64)
        dsks = A.f32(8)
        carry = A.f32(128).rearrange("p (d c r) -> p d c r", d=2, r=2)
        fin = A.f32(512).rearrange("p (s d r c) -> p s d r c", s=4, d=2, r=2)
        ATv = A.bf(8 * T).rearrange("p (k t) -> p k t", k=8)
        PERS = A.off

        def V(eng, f, reads, writes):
            return P.op(eng, f, reads, writes)

        class _Stop(Exception):
            pass

        def chk(n):
            if STAGE == n:
                raise _Stop()

        try:
            P.dma("sp", lambda e: e.dma_start(out=csts, in_=cst), writes=["csts"], key="csts")
            P.dma("sp", lambda e: e.dma_start(out=ngs, in_=ng), writes=["ngs"], key="ngs")
            P.dma("sp", lambda e: e.dma_start(out=fgs, in_=fg), writes=["fgs"], key="fgs")
            P.dma("sp", lambda e: e.dma_start(out=dsks, in_=dsk), writes=["dsks"], key="dsks")
            V("dve", lambda e: e.memset(onesf, 1.0), [], ["onesf"])
            V("dve", lambda e: e.memset(epsb, EPS), [], ["epsb"])
            V("dve", lambda e: e.memset(hpib, math.pi / 2), [], ["hpib"])
            V("dve", lambda e: e.tensor_copy(out=identb, in_=csts[:, 0:128]), ["csts"], ["identb"])
            V("dve", lambda e: e.tensor_copy(out=selb, in_=csts[:, 128:384]), ["csts"], ["selb"])

            chk(0.1)
            wctr = [0]
            wslots = [None] * 4

            def wslot():
                s = wctr[0] % 4
                wctr[0] += 1
                return wslots[s], ("ws", s)

            def wdma(dst, src, res):
                P.dma("pool", lambda e: e.dma_start(out=dst, in_=src), writes=[res], key=res)

            tmpc = [0]
            tmps = [None] * 6

            def tmp():
                i = tmpc[0] % 6
                tmpc[0] += 1
                return tmps[i], ("tmp", i)

            def alloc_common():
                A.reset(PERS)
                xv = A.f32(16 * T).rearrange("p (k t) -> p k t", k=16)
                hv = A.bf(16 * T).rearrange("p (k t) -> p k t", k=16)
                for i in range(4):
                    wslots[i] = A.bf(8192)
                for i in range(6):
                    tmps[i] = A.f32(T)
                rs = A.f32(T)
                return xv, hv, rs

            xv, hv, rs = alloc_common()
            cnd = A.f32(32); scb = A.bf(32); bad = A.f32(144)
            P.dma("sp", lambda e: e.dma_start(out=cnd, in_=cond), writes=["cnd"], key="cnd")
            P.dma("sp", lambda e: e.dma_start(out=bad, in_=b_ada), writes=["bad"], key="bad")
            V("act", lambda e: e.activation(out=scb, in_=cnd, func=AF.Silu), ["cnd"], ["scb"])
            chk(0.2)
            scb3 = scb.rearrange("p (k c) -> p k c", c=2)
            wav = w_ada.rearrange("(k p) n -> p k n", p=128)
            for blk in range(36):
                wv, wr = wslot()
                wv3 = wv.rearrange("p (k n) -> p k n", k=16)
                wdma(wv3, wav[:, :, blk * 512:(blk + 1) * 512], wr)
                for mc in range(4):
                    m = blk * 4 + mc
                    b, ps, pr = bank()
                    for k in range(16):
                        V("pe", lambda e, ps=ps, wv3=wv3, k=k, mc=mc: e.matmul(ps[:, 0:2], lhsT=wv3[:, k, mc * 128:(mc + 1) * 128], rhs=scb3[:, k, :], start=(k == 0), stop=(k == 15)),
                          [wr, "scb"], [pr])
                    V("dve", lambda e, ps=ps, m=m: e.tensor_scalar(out=mods[:, m, :], in0=ps[:, 0:2], scalar1=bad[:, m:m + 1], scalar2=None, op0=ALU.add),
                      [pr, "bad"], ["mods"])
                chk(0.3 if blk == 0 else -1)
            chk(0.5)
            for n in range(3):
                V("dve", lambda e, n=n: e.tensor_scalar(out=der[:, n], in0=mods[:, 16 * (3 * n + 1):16 * (3 * n + 2), :], scalar1=1.0, scalar2=None, op0=ALU.add),
                  ["mods"], ["der"])
                V("dve", lambda e, n=n: e.tensor_tensor(out=der[:, n], in0=der[:, n], in1=ngs[:, n * 16:(n + 1) * 16].unsqueeze(2).to_broadcast([128, 16, 2]), op=ALU.mult),
                  ["der", "ngs"], ["der"])
            V("dve", lambda e: e.tensor_scalar(out=der[:, 3], in0=mods[:, 32:48, :], scalar1=0.5, scalar2=None, op0=ALU.mult), ["mods"], ["der"])
            V("dve", lambda e: e.tensor_scalar(out=der[:, 4], in0=mods[:, 128:144, :], scalar1=0.5, scalar2=None, op0=ALU.mult), ["mods"], ["der"])

            chk(1)
            XR = [("x", k) for k in range(16)]

            def rstd_of(xv, rs):
                b, ps, pr = bank()
                for k in range(16):
                    sq, sr = tmp()
                    sqb = sq.bitcast(BF16)[:, 0:T]
                    V("act", lambda e, sqb=sqb, k=k: e.activation(out=sqb, in_=xv[:, k, :], func=AF.Square), [("x", k)], [sr])
                    V("pe", lambda e, ps=ps, sqb=sqb, k=k: e.matmul(ps, lhsT=onesf, rhs=sqb, start=(k == 0), stop=(k == 15)), [sr, "onesf"], [pr])
                V("act", lambda e, ps=ps: e.activation(out=rs, in_=ps, func=AF.Sqrt, bias=epsb[:, 0:1], scale=1.0 / D), [pr, "epsb"], ["rs"])
                V("dve", lambda e: e.reciprocal(out=rs, in_=rs), ["rs"], ["rs"])

            def norm_mod(xv, hv, rs, n, ci):
                rstd_of(xv, rs)
                for k in range(16):
                    tk, tr = tmp()
                    V("dve", lambda e, tk=tk, k=k: e.scalar_tensor_tensor(out=tk, in0=xv[:, k, :], scalar=der[:, n, k, ci:ci + 1], in1=rs, op0=ALU.mult, op1=ALU.mult),
                      [("x", k), "rs", "der"], [tr])
                    V("act", lambda e, tk=tk, k=k: e.activation(out=hv[:, k, :], in_=tk, func=AF.Identity, bias=mods[:, 48 * n + k, ci:ci + 1], scale=1.0),
                      [tr, "mods"], [("h", k)])

            fin_v = ffn_in.rearrange("f (k p) n -> f p k n", p=128)
            fout_v = ffn_out.rearrange("f (j p) n -> f p j n", p=128)

            def ffn(f, xv, hv, actv, ci):
                for jb in range(22):
                    wv, wr = wslot()
                    wv4 = wv.rearrange("p (g k n) -> p g k n", g=2, k=16)
                    wdma(wv4[:, 0], fin_v[f, :, :, jb * 256:(jb + 1) * 256], wr)
                    wdma(wv4[:, 1], fin_v[f, :, :, DFF + jb * 256:DFF + (jb + 1) * 256], wr)
                    for jj in range(2):
                        j = 2 * jb + jj
                        bg, psg, prg = bank()
                        bu, psu, pru = bank()
                        for k in range(16):
                            V("pe", lambda e, psg=psg, k=k, jj=jj, wv4=wv4: e.matmul(psg, lhsT=wv4[:, 0, k, jj * 128:(jj + 1) * 128], rhs=hv[:, k, :], start=(k == 0), stop=(k == 15)),
                              [wr, ("h", k)], [prg])
                        for k in range(16):
                            V("pe", lambda e, psu=psu, k=k, jj=jj, wv4=wv4: e.matmul(psu, lhsT=wv4[:, 1, k, jj * 128:(jj + 1) * 128], rhs=hv[:, k, :], start=(k == 0), stop=(k == 15)),
                              [wr, ("h", k)], [pru])
                        sg, sr = tmp()
                        V("act", lambda e, sg=sg, psg=psg: e.activation(out=sg, in_=psg, func=AF.Silu), [prg], [sr])
                        V("dve", lambda e, sg=sg, psu=psu, j=j: e.tensor_tensor(out=actv[:, j, :], in0=sg, in1=psu, op=ALU.mult), [sr, pru], [("act", j)])
                hgi = 3 if f == 0 else 4
                AR = [("act", j) for j in range(44)]
                for m in range(16):
                    wv, wr = wslot()
                    wv3 = wv[:, 0:44 * 128].rearrange("p (j n) -> p j n", j=44)
                    wdma(wv3, fout_v[f, :, :, m * 128:(m + 1) * 128], wr)
                    b, ps, pr = bank()
                    for j in range(44):
                        V("pe", lambda e, ps=ps, j=j, wv3=wv3: e.matmul(ps, lhsT=wv3[:, j, :], rhs=actv[:, j, :], start=(j == 0), stop=(j == 43)),
                          [wr, ("act", j)], [pr])
                    V("dve", lambda e, ps=ps, m=m: e.scalar_tensor_tensor(out=xv[:, m, :], in0=ps, scalar=der[:, hgi, m, ci:ci + 1], in1=xv[:, m, :], op0=ALU.mult, op1=ALU.add),
                      [pr, ("x", m), "der"], [("x", m)])

            def linear_fm(rhs_fn, rhs_res, KC, wsrc_fn, nblocks, evac):
                for cb in range(nblocks):
                    wv, wr = wslot()
                    wv3 = wv[:, 0:KC * 512].rearrange("p (k n) -> p k n", k=KC)
                    wdma(wv3, wsrc_fn(cb), wr)
                    for mc in range(4):
                        b, ps, pr = bank()
                        for k in range(KC):
                            V("pe", lambda e, ps=ps, k=k, mc=mc, wv3=wv3: e.matmul(ps, lhsT=wv3[:, k, mc * 128:(mc + 1) * 128], rhs=rhs_fn(k), start=(k == 0), stop=(k == KC - 1)),
                              [wr] + rhs_res(k), [pr])
                        evac(cb * 4 + mc, ps, pr)

            win_v = w_in.rearrange("(k p) n -> p k n", p=128)

            def phaseA(ti):
                g0 = ti * T
                kind = "p" if ti < 2 else ("o" if ti < 6 else "h")
                ci = 0 if kind == "p" else 1
                src = {"p": xp, "o": xo, "h": xh}[kind]
                t0 = {"p": g0, "o": g0 - 1024, "h": g0 - 3072}[kind]
                P.dma("sp", lambda e: e.dma_start(out=xv, in_=src.rearrange("(k p) t -> p k t", p=128)[:, :, t0:t0 + T]), writes=XR, key="xld")
                norm_mod(xv, hv, rs, 0, ci)
                if ti == 0:
                    chk(1.1)
                ffn(0, xv, hv, actv, ci)
                if ti == 0:
                    chk(1.3)
                if kind != "h":
                    P.dma("sp", lambda e: e.dma_start(out=x1s.rearrange("(k p) t -> p k t", p=128)[:, :, g0:g0 + T], in_=xv), reads=XR, writes=[("x1s", ti)], key="xst")
                norm_mod(xv, hv, rs, 1, ci)
                if ti == 0:
                    chk(1.4)
                hfn = lambda k: hv[:, k, :]
                pst = actv[:, 0:8, :]
                PSTR = [("act", j) for j in range(8)]
                if kind != "h":
                    def ev_q(c, ps, pr):
                        V("act", lambda e: e.activation(out=pst[:, c, :], in_=ps, func=AF.Identity, scale=0.125), [pr], [("act", c)])
                    linear_fm(hfn, (lambda k: [("h", k)]), 16, lambda cb: win_v[:, :, cb * 512:(cb + 1) * 512], 2, ev_q)
                    P.dma(KSQ, lambda e: e.dma_start(out=qs.rearrange("(c p) t -> p c t", p=128)[:, :, g0:g0 + T], in_=pst), reads=PSTR, writes=[("qs", ti)], key="pst")
                    if ti == 0:
                        chk(1.5)
                if kind != "h" or ti == 6:
                    def ev_k(c, ps, pr):
                        if kind != "p":
                            V("act", lambda e: e.activation(out=pst[:, c, :], in_=ps, func=AF.Copy), [pr], [("act", c)])
                        else:
                            kf, kr = tmp()
                            V("dve", lambda e: e.tensor_copy(out=kf, in_=ps), [pr], [kr])
                            V("act", lambda e: e.activation(out=pst[:, c, :], in_=kf, func=AF.Copy), [kr], [("act", c)])
                            P.dma("sp", lambda e: e.dma_start(out=nk[c * 128:(c + 1) * 128, g0:g0 + T], in_=kf), reads=[kr], writes=["nk"], key=kr)
                    linear_fm(hfn, (lambda k: [("h", k)]), 16, lambda cb: win_v[:, :, 1024 + cb * 512:1024 + (cb + 1) * 512], 2, ev_k)
                    if ti == 0:
                        chk(1.55)
                    P.dma(KSQ, lambda e: e.dma_start(out=ks.rearrange("(c p) t -> p c t", p=128)[:, :, g0:g0 + T], in_=pst), reads=PSTR, writes=[("ks", ti)], key="pst")
                    if ti == 0:
                        chk(1.6)
                    vst = actv[:, 8:17, :].rearrange("p a t -> p (a t)")[:, 0:4 * 1040].rearrange("p (b h e) -> p b h e", b=4, h=16)
                    VSTR = [("act", j) for j in range(8, 17)]
                    V("dve", lambda e: e.memset(vst[:, :, :, 64:65], 1.0), [], VSTR)
                    for cb in range(2):
                        wv, wr = wslot()
                        wv3 = wv.rearrange("p (k n) -> p k n", k=16)
                        wdma(wv3, win_v[:, :, 2048 + cb * 512:2048 + (cb + 1) * 512], wr)
                        for tb in range(4):
                            b, ps, pr = bank()
                            for k in range(16):
                                V("pe", lambda e, ps=ps, k=k, tb=tb, wv3=wv3: e.matmul(ps, lhsT=hv[:, k, tb * 128:(tb + 1) * 128], rhs=wv3[:, k, :], start=(k == 0), stop=(k == 15)),
                                  [wr, ("h", k)], [pr])
                            if kind != "p":
                                V("act", lambda e, ps=ps, tb=tb, cb=cb: e.activation(out=vst[:, tb, cb * 8:(cb + 1) * 8, 0:64], in_=ps.rearrange("p (h d) -> p h d", h=8), func=AF.Copy), [pr], VSTR)
                            else:
                                vf, vr = tmp()
                                V("dve", lambda e, vf=vf, ps=ps: e.tensor_copy(out=vf, in_=ps), [pr], [vr])
                                V("act", lambda e, vf=vf, tb=tb, cb=cb: e.activation(out=vst[:, tb, cb * 8:(cb + 1) * 8, 0:64], in_=vf.rearrange("p (h d) -> p h d", h=8), func=AF.Copy), [vr], VSTR)
                                P.dma("sp", lambda e, vf=vf, tb=tb, cb=cb: e.dma_start(out=nv[g0 + tb * 128:g0 + (tb + 1) * 128, cb * 512:(cb + 1) * 512], in_=vf), reads=[vr], writes=["nv"], key=vr)
                    P.dma(KSQ, lambda e: e.dma_start(out=vs.rearrange("(b p) e -> p b e", p=128)[:, g0 // 128:g0 // 128 + 4, :], in_=vst.rearrange("p b h e -> p b (h e)")), reads=VSTR, writes=[("vs", ti)], key="vst")

                if ti == 0:
                    chk(1.7)

                def ev_u(c, ps, pr):
                    V("act", lambda e: e.activation(out=pst[:, c, :], in_=ps, func=AF.Copy), [pr], [("act", c)])
                linear_fm(hfn, (lambda k: [("h", k)]), 16, lambda cb: win_v[:, :, 3072 + cb * 512:3072 + (cb + 1) * 512], 2, ev_u)
                P.dma(KSQ, lambda e: e.dma_start(out=us.rearrange("(c p) t -> p c t", p=128)[:, :, g0:g0 + T], in_=pst), reads=PSTR, writes=[("us", ti)], key="pst")

            P.barrier()
            xv, hv, rs = alloc_common()
            actv = A.bf(44 * T).rearrange("p (j t) -> p j t", j=44)
            for ti in range(10):
                phaseA(ti)
                chk(2 if ti == 0 else (3 if ti == 9 else -1))

            P.barrier()
            A.reset(PERS)
            bbr = A.bf(64 * 128).rearrange("p (c n) -> p c n", c=64)
            bbi = A.bf(64 * 128).rearrange("p (c n) -> p c n", c=64)
            ctr_ = A.bf(64 * 128).rearrange("p (c n) -> p c n", c=64)
            cti = A.bf(64 * 128).rearrange("p (c n) -> p c n", c=64)
            S5BASE = A.off
            sm = {}
            for nm in ["lre", "lim", "dt", "ldr", "ldi", "c", "s", "t", "are", "aim", "mag", "fre", "fim", "nfi", "u1", "u2"]:
                sm[nm] = A.f32(64)
            h0s = A.f32(128)
            P.dma("sp", lambda e: e.dma_start(out=sm["lre"], in_=lamre), writes=["lre"], key="lre")
            P.dma("sp", lambda e: e.dma_start(out=sm["lim"], in_=lamim), writes=["lim"], key="lim")
            P.dma("sp", lambda e: e.dma_start(out=sm["dt"], in_=ldt), writes=["dt"], key="dt")
            P.dma("sp", lambda e: e.dma_start(out=h0s, in_=h0), writes=["h0s"], key="h0s")

            SMR = ["sm", "h0s", "lre", "lim", "dt"]

            def TT(o, a, b, op, eng="dve"):
                V(eng, lambda e: e.tensor_tensor(out=sm[o] if isinstance(o, str) else o, in0=sm[a] if isinstance(a, str) else a, in1=sm[b] if isinstance(b, str) else b, op=op), SMR, ["sm"])

            def TS(o, a, s1, s2, op0, op1=None):
                if op1 is None:
                    V("dve", lambda e: e.tensor_scalar(out=sm[o] if isinstance(o, str) else o, in0=sm[a] if isinstance(a, str) else a, scalar1=s1, scalar2=None, op0=op0), SMR, ["sm"])
                else:
                    V("dve", lambda e: e.tensor_scalar(out=sm[o] if isinstance(o, str) else o, in0=sm[a] if isinstance(a, str) else a, scalar1=s1, scalar2=s2, op0=op0, op1=op1), SMR, ["sm"])

            V("act", lambda e: e.activation(out=sm["dt"], in_=sm["dt"], func=AF.Exp), ["dt"], ["sm"])
            V("dve", lambda e: e.tensor_tensor(out=sm["ldr"], in0=sm["lre"], in1=sm["dt"], op=ALU.mult), ["lre", "sm"], ["sm"])
            V("dve", lambda e: e.tensor_tensor(out=sm["ldi"], in0=sm["lim"], in1=sm["dt"], op=ALU.mult), ["lim", "sm"], ["sm"])
            V("act", lambda e: e.activation(out=rmag, in_=sm["ldr"], func=AF.Exp), ["sm"], ["sm"])
            V("act", lambda e: e.activation(out=sm["s"], in_=sm["ldi"], func=AF.Sin, scale=1.0 / 32), ["sm"], ["sm"])
            V("act", lambda e: e.activation(out=sm["c"], in_=sm["ldi"], func=AF.Sin, bias=hpib[:, 0:1], scale=1.0 / 32), ["sm", "hpib"], ["sm"])

            def dbl(co, so, cin, sin_):
                TT("t", sin_, sin_, ALU.mult)
                V("dve", lambda e: e.scalar_tensor_tensor(out=so, in0=sin_ if not isinstance(sin_, str) else sm[sin_], scalar=2.0, in1=cin if not isinstance(cin, str) else sm[cin], op0=ALU.mult, op1=ALU.mult), ["sm"], ["sm"])
                TS(co, "t", -2.0, 1.0, ALU.mult, ALU.add)

            for i in range(5):
                if i < 4:
                    dbl(sm["u1"], sm["u2"], "c", "s")
                    TT("c", "u1", "u1", ALU.max)
                    TT("s", "u2", "u2", ALU.max)
                else:
                    dbl(CL[:, 0, :], SL[:, 0, :], "c", "s")
            for k in range(1, 10):
                dbl(CL[:, k, :], SL[:, k, :], CL[:, k - 1, :], SL[:, k - 1, :])
            TT("u1", CL[:, 8, :], CL[:, 0, :], ALU.mult); TT("u2", SL[:, 8, :], SL[:, 0, :], ALU.mult); TT(c255, "u1", "u2", ALU.add)
            TT("u1", SL[:, 8, :], CL[:, 0, :], ALU.mult); TT("u2", CL[:, 8, :], SL[:, 0, :], ALU.mult); TT(s255, "u1", "u2", ALU.subtract)
            TT("are", rmag, CL[:, 0, :], ALU.mult); TT("aim", rmag, SL[:, 0, :], ALU.mult)
            TT("u1", "lre", "lre", ALU.mult); TT("u2", "lim", "lim", ALU.mult); TT("mag", "u1", "u2", ALU.add)
            V("dve", lambda e: e.reciprocal(out=sm["mag"], in_=sm["mag"]), ["sm"], ["sm"])
            TS("are", "are", -1.0, None, ALU.add)
            TT("u1", "are", "lre", ALU.mult); TT("u2", "aim", "lim", ALU.mult); TT("fre", "u1", "u2", ALU.add); TT("fre", "fre", "mag", ALU.mult)
            TT("u1", "aim", "lre", ALU.mult); TT("u2", "are", "lim", ALU.mult); TT("fim", "u1", "u2", ALU.subtract); TT("fim", "fim", "mag", ALU.mult)
            TS("nfi", "fim", -1.0, None, ALU.mult)
            h04 = h0s.rearrange("p (d r c) -> p d r c", d=2, r=2)
            for d_ in range(2):
                cs = CL[:, 0, d_ * 32:(d_ + 1) * 32]; ss = SL[:, 0, d_ * 32:(d_ + 1) * 32]
                TT(sm["u1"][:, 0:32], h04[:, d_, 0, :], cs, ALU.mult); TT(sm["u2"][:, 0:32], h04[:, d_, 1, :], ss, ALU.mult)
                TT(carry[:, d_, :, 0], sm["u1"][:, 0:32], sm["u2"][:, 0:32], ALU.subtract)
                TT(sm["u1"][:, 0:32], h04[:, d_, 0, :], ss, ALU.mult); TT(sm["u2"][:, 0:32], h04[:, d_, 1, :], cs, ALU.mult)
                TT(carry[:, d_, :, 1], sm["u1"][:, 0:32], sm["u2"][:, 0:32], ALU.add)
            V("dve", lambda e: e.memset(fin.rearrange("p s d r c -> p (s d r c)"), 0.0), ["sm"], ["fin", "sm"])
            chk(4)
            PRO2 = A.off
            EC = A.f32(8 * T).rearrange("p (c t) -> p c t", c=8)
            ES = A.f32(8 * T).rearrange("p (c t) -> p c t", c=8)
            T1 = A.f32(8 * 256).rearrange("p (c t) -> p c t", c=8)
            T2 = A.f32(8 * 256).rearrange("p (c t) -> p c t", c=8)
            ESn = A.f32(8 * T).rearrange("p (c t) -> p c t", c=8)
            for q in range(8):
                V("dve", lambda e: e.memset(EC[:, :, 0:1], 1.0), ["ec"], ["ec"])
                V("dve", lambda e: e.memset(ES[:, :, 0:1], 0.0), ["ec"], ["ec"])
                for k in range(9):
                    m = 1 << k
                    cm = CL[:, k, q * 8:(q + 1) * 8].unsqueeze(2).to_broadcast([128, 8, m])
                    smm = SL[:, k, q * 8:(q + 1) * 8].unsqueeze(2).to_broadcast([128, 8, m])
                    c_, s_ = EC[:, :, 0:m], ES[:, :, 0:m]
                    t1, t2 = T1[:, :, 0:m], T2[:, :, 0:m]
                    for (a_, b_, c2, d2, op, dst) in ((c_, cm, s_, smm, ALU.subtract, EC[:, :, m:2 * m]), (s_, cm, c_, smm, ALU.add, ES[:, :, m:2 * m])):
                        V("dve", lambda e, a_=a_, b_=b_, t1=t1: e.tensor_tensor(out=t1, in0=a_, in1=b_, op=ALU.mult), ["ec", "sm"], ["t1"])
                        V("dve", lambda e, c2=c2, d2=d2, t2=t2: e.tensor_tensor(out=t2, in0=c2, in1=d2, op=ALU.mult), ["ec", "sm"], ["t2"])
                        V("dve", lambda e, dst=dst, t1=t1, t2=t2, op=op: e.tensor_tensor(out=dst, in0=t1, in1=t2, op=op), ["t1", "t2"], ["ec"])
                rv = rot.rearrange("c p t -> p c t")
                P.dma("sp", lambda e, q=q: e.dma_start(out=rv[:, q * 8:(q + 1) * 8, 0:T], in_=EC), reads=["ec"], writes=["rot"], key="ecst")
                P.dma("sp", lambda e, q=q: e.dma_start(out=rv[:, q * 8:(q + 1) * 8, T:2 * T], in_=ES), reads=["ec"], writes=["rot"], key="ecst")
                V("dve", lambda e: e.tensor_scalar(out=ESn, in0=ES, scalar1=-1.0, scalar2=None, op0=ALU.mult), ["ec"], ["esn"])
                P.dma("sp", lambda e, q=q: e.dma_start(out=rv[:, q * 8:(q + 1) * 8, 2 * T:3 * T], in_=ESn), reads=["esn"], writes=["rot"], key="esnst")
            P.barrier()
            A.reset(PRO2)
            bl = [A.f32(8 * 128).rearrange("p (c n) -> p c n", c=8) for _ in range(2)]
            Dm = [A.f32(128) for _ in range(3)]
            for g8 in range(8):
                P.dma("sp", lambda e, g8=g8: e.dma_start(out=bl[0], in_=bpr[g8 * 8:(g8 + 1) * 8].rearrange("c p n -> p c n")), writes=["bl0"], key="bl0")
                P.dma("sp", lambda e, g8=g8: e.dma_start(out=bl[1], in_=bpi[g8 * 8:(g8 + 1) * 8].rearrange("c p n -> p c n")), writes=["bl1"], key="bl1")
                for i in range(8):
                    dc = g8 * 8 + i
                    for di, nm in enumerate(("fre", "fim", "nfi")):
                        V("dve", lambda e, di=di, nm=nm, dc=dc: e.tensor_scalar(out=Dm[di], in0=csts[:, 0:128], scalar1=sm[nm][:, dc:dc + 1], scalar2=None, op0=ALU.mult), ["csts", "sm"], [("Dm", di)])
                    ba, pa, ra = bank()
                    bb_, pb_, rb = bank()
                    V("pe", lambda e, pa=pa, i=i: e.matmul(pa[:, 0:128], lhsT=bl[0][:, i, :], rhs=Dm[0], start=True, stop=False), ["bl0", ("Dm", 0)], [ra])
                    V("pe", lambda e, pa=pa, i=i: e.matmul(pa[:, 0:128], lhsT=bl[1][:, i, :], rhs=Dm[2], start=False, stop=True), ["bl1", ("Dm", 2)], [ra])
                    V("pe", lambda e, pb_=pb_, i=i: e.matmul(pb_[:, 0:128], lhsT=bl[1][:, i, :], rhs=Dm[0], start=True, stop=False), ["bl1", ("Dm", 0)], [rb])
                    V("pe", lambda e, pb_=pb_, i=i: e.matmul(pb_[:, 0:128], lhsT=bl[0][:, i, :], rhs=Dm[1], start=False, stop=True), ["bl0", ("Dm", 1)], [rb])
                    V("act", lambda e, pa=pa, dc=dc: e.activation(out=bbr[:, dc, :], in_=pa[:, 0:128], func=AF.Copy), [ra], ["bbt"])
                    V("act", lambda e, pb_=pb_, dc=dc: e.activation(out=bbi[:, dc, :], in_=pb_[:, 0:128], func=AF.Copy), [rb], ["bbt"])
            for g8 in range(8):
                P.dma("sp", lambda e, g8=g8: e.dma_start(out=bl[0], in_=cpr[g8 * 8:(g8 + 1) * 8].rearrange("c p n -> p c n")), writes=["bl0"], key="bl0")
                P.dma("sp", lambda e, g8=g8: e.dma_start(out=bl[1], in_=cpi[g8 * 8:(g8 + 1) * 8].rearrange("c p n -> p c n")), writes=["bl1"], key="bl1")
                V("act", lambda e, g8=g8: e.activation(out=ctr_[:, g8 * 8:(g8 + 1) * 8, :], in_=bl[0], func=AF.Copy), ["bl0"], ["bbt"])
                V("act", lambda e, g8=g8: e.activation(out=cti[:, g8 * 8:(g8 + 1) * 8, :], in_=bl[1], func=AF.Identity, scale=-1.0), ["bl1"], ["bbt"])

            chk(5)
            P.barrier()
            A.reset(S5BASE + 16 * 64 + 128)
            ut = A.bf(8 * T).rearrange("p (k t) -> p k t", k=8)
            rts = [A.f32(3 * T) for _ in range(4)]
            tbs = [[A.f32(T) for _ in range(6)] for _ in range(3)]
            prod = [A.bf(16 * T).rearrange("p (j q t) -> p j q t", j=4, q=4) for _ in range(2)]
            ybf = [A.f32(T) for _ in range(2)]
            ybb = [A.bf(T) for _ in range(2)]
            zl = A.f32(128)
            gt = [A.f32(T) for _ in range(3)]
            rtc = [0]

            def s5_tile(g0, dirn, rev, nseg, use_carry, mode, yoff=None, fin_seq=None):
                ln = T // nseg
                P.dma("sp", lambda e: e.dma_start(out=ut, in_=us.rearrange("(c p) t -> p c t", p=128)[:, :, g0:g0 + T]), writes=["ut"], key="ut")
                yasv = yas.rearrange("(c p) t -> p c t", p=128)
                ybsv = ybs.rearrange("(c p) t -> p c t", p=128)
                zl4 = zl[:, 0:32 * nseg * 2].rearrange("p (c s r) -> p c s r", c=32, r=2)

                def seg3(ap):
                    return ap.rearrange("p (s l) -> p s l", s=nseg)

                items = [(uc, j) for uc in range(8) for j in range(4)]
                ctxs = {}

                def st_a0(i):
                    uc, j = items[i]
                    c = 4 * uc + j
                    dc = dirn * 32 + c
                    ri = rtc[0] % 4
                    rtc[0] += 1
                    rt = rts[ri]
                    rr = ("rt", ri)
                    P.dma("sp", lambda e: e.dma_start(out=rt, in_=rot[dc]), reads=["rot"], writes=[rr], key=rr)
                    Cs = rt[:, 0:ln]; Ss = rt[:, T:T + ln]
                    if rev:
                        Cs = Cs[:, ::-1]; Ss = Ss[:, ::-1]
                    Sn = rt[:, 2 * T:2 * T + ln]
                    if rev:
                        Sn = Sn[:, ::-1]
                    C3 = Cs.unsqueeze(1).to_broadcast([128, nseg, ln]); S3 = Ss.unsqueeze(1).to_broadcast([128, nseg, ln])
                    N3 = Sn.unsqueeze(1).to_broadcast([128, nseg, ln])
                    if mode == "B" and j == 0:
                        ysl = uc % 2
                        P.dma("sp", lambda e: e.dma_start(out=ybf[ysl], in_=yasv[:, uc, yoff:yoff + T]), reads=["yas"], writes=[("ybf", ysl)], key=("ybf", ysl))
                    b1, p1, r1 = bank()
                    b2, p2, r2 = bank()
                    V("pe", lambda e: e.matmul(p1, lhsT=bbr[:, dc, :], rhs=ut[:, uc, :], start=True, stop=True), ["bbt", "ut"], [r1])
                    V("pe", lambda e: e.matmul(p2, lhsT=bbi[:, dc, :], rhs=ut[:, uc, :], start=True, stop=True), ["bbt", "ut"], [r2])
                    si = i % 3
                    ctxs[i] = dict(uc=uc, j=j, c=c, dc=dc, rr=rr, C3=C3, S3=S3, N3=N3, t=tbs[si], sx="_%d" % si, p=(p1, r1, p2, r2))

                def st_a1(i):
                    cx = ctxs[i]
                    rr, C3, S3, N3, sx = cx["rr"], cx["C3"], cx["S3"], cx["N3"], cx["sx"]
                    p1, r1, p2, r2 = cx["p"]
                    t1, t2, t3, t4, zsr, zsi = cx["t"]
                    V("dve", lambda e: e.tensor_tensor(out=seg3(t1), in0=seg3(p1), in1=C3, op=ALU.mult), [r1, rr], ["t1" + sx])
                    V("dve", lambda e: e.tensor_tensor(out=seg3(t2), in0=seg3(p2), in1=S3, op=ALU.mult), [r2, rr], ["t2" + sx])
                    V("dve", lambda e: e.tensor_tensor(out=seg3(t3), in0=seg3(p2), in1=C3, op=ALU.mult), [r2, rr], ["t3" + sx])
                    V("dve", lambda e: e.tensor_tensor(out=seg3(t4), in0=seg3(p1), in1=N3, op=ALU.mult), [r1, rr], ["t4" + sx])
                    identf = csts[:, 0:128]
                    bz1, pz1, rz1 = bank()
                    bz2, pz2, rz2 = bank()
                    V("pe", lambda e: e.matmul(pz1, lhsT=identf, rhs=t1, start=True, stop=False), ["csts", "t1" + sx], [rz1])
                    V("pe", lambda e: e.matmul(pz1, lhsT=identf, rhs=t2, start=False, stop=True), ["csts", "t2" + sx], [rz1])
                    V("pe", lambda e: e.matmul(pz2, lhsT=identf, rhs=t3, start=True, stop=False), ["csts", "t3" + sx], [rz2])
                    V("pe", lambda e: e.matmul(pz2, lhsT=identf, rhs=t4, start=False, stop=True), ["csts", "t4" + sx], [rz2])
                    cx["z"] = (pz1, rz1, pz2, rz2)

                def st_b(i):
                    cx = ctxs[i]
                    c, dc, rr, C3, S3, sx = cx["c"], cx["dc"], cx["rr"], cx["C3"], cx["S3"], cx["sx"]
                    t1, t2, t3, t4, zsr, zsi = cx["t"]
                    rbc = rmag[:, dc:dc + 1].to_broadcast([128, ln])
                    for s_ in range(nseg):
                        sl = slice(s_ * ln, (s_ + 1) * ln)
                        pz1, rz1, pz2, rz2 = cx["z"]
                        for (zo, zin, rix, zres, ores) in ((zsr, pz1, 0, rz1, "zsr" + sx), (zsi, pz2, 1, rz2, "zsi" + sx)):
                            o_ = zo[:, sl]; i_ = zin[:, sl]
                            if rev:
                                o_ = o_[:, ::-1]; i_ = i_[:, ::-1]
                            init = carry[:, dirn, c, rix:rix + 1] if use_carry else 0.0
                            V("dve", lambda e, o_=o_, i_=i_, init=init: e.tensor_tensor_scan(out=o_, data0=rbc, data1=i_, initial=init, op0=ALU.mult, op1=ALU.add),
                              [zres, "carry", "sm"], [ores])
                    pos = 0 if rev else ln - 1
                    V("act", lambda e: e.activation(out=zl4[:, c, :, 0], in_=seg3(zsr)[:, :, pos], func=AF.Copy), ["zsr" + sx], ["zl"])
                    V("act", lambda e: e.activation(out=zl4[:, c, :, 1], in_=seg3(zsi)[:, :, pos], func=AF.Copy), ["zsi" + sx], ["zl"])
                    if mode != "N":
                        N3 = cx["N3"]
                        par_, j_ = cx["uc"] % 2, cx["j"]
                        pr_ = ("prod", par_)
                        pd = prod[par_]
                        V("pool", lambda e: e.tensor_tensor(out=seg3(pd[:, j_, 0, :]), in0=seg3(zsr), in1=C3, op=ALU.mult), ["zsr" + sx, rr], [pr_])
                        V("pool", lambda e: e.tensor_tensor(out=seg3(pd[:, j_, 1, :]), in0=seg3(zsi), in1=N3, op=ALU.mult), ["zsi" + sx, rr], [pr_])
                        V("pool", lambda e: e.tensor_tensor(out=seg3(pd[:, j_, 2, :]), in0=seg3(zsr), in1=S3, op=ALU.mult), ["zsr" + sx, rr], [pr_])
                        V("pool", lambda e: e.tensor_tensor(out=seg3(pd[:, j_, 3, :]), in0=seg3(zsi), in1=C3, op=ALU.mult), ["zsi" + sx, rr], [pr_])

                def st_c(i):
                    cx = ctxs.pop(i)
                    uc, j = cx["uc"], cx["j"]
                    par = uc % 2
                    if mode == "N" or j != 3:
                        return
                    pr_ = ("prod", par)
                    pd = prod[par]
                    by, py, ry = bank()
                    n = 0
                    for jj in range(4):
                        dcc = dirn * 32 + 4 * uc + jj
                        for q in range(4):
                            tab = ctr_ if q < 2 else cti
                            V("pe", lambda e, dcc=dcc, jj=jj, q=q, tab=tab, n=n: e.matmul(py, lhsT=tab[:, dcc, :], rhs=pd[:, jj, q, :], start=(n == 0), stop=(n == 15)), ["bbt", pr_], [ry])
                            n += 1
                    ysl = uc % 2
                    if mode == "A":
                        V("act", lambda e: e.activation(out=ybf[ysl], in_=py, func=AF.Copy), [ry], [("ybf", ysl)])
                        P.dma("sp", lambda e: e.dma_start(out=yasv[:, uc, yoff:yoff + T], in_=ybf[ysl]), reads=[("ybf", ysl)], writes=["yas"], key=("ybf", ysl))
                    else:
                        g1, g2, g3 = gt
                        V("dve", lambda e: e.tensor_tensor(out=g1, in0=py, in1=ybf[ysl], op=ALU.add), [ry, ("ybf", ysl)], ["g1"])
                        V("dve", lambda e: e.scalar_tensor_tensor(out=g1, in0=ut[:, uc, :], scalar=dsks[:, uc:uc + 1], in1=g1, op0=ALU.mult, op1=ALU.add), ["ut", "g1", "dsks"], ["g1"])
                        V("act", lambda e: e.activation(out=g2, in_=g1, func=AF.Square), ["g1"], ["g2"])
                        V("dve", lambda e: e.tensor_scalar(out=g2, in0=g2, scalar1=0.044715, scalar2=1.0, op0=ALU.mult, op1=ALU.add), ["g2"], ["g2"])
                        V("pool", lambda e: e.tensor_tensor(out=g2, in0=g2, in1=g1, op=ALU.mult), ["g2", "g1"], ["g2"])
                        V("act", lambda e: e.activation(out=g3, in_=g2, func=AF.Sigmoid, scale=1.5957691216057308), ["g2"], ["g3"])
                        V("pool", lambda e: e.tensor_tensor(out=ybb[ysl], in0=g1, in1=g3, op=ALU.mult), ["g1", "g3"], [("ybb", ysl)])
                        P.dma("sp", lambda e: e.dma_start(out=ybsv[:, uc, yoff:yoff + T], in_=ybb[ysl]), reads=[("ybb", ysl)], writes=["ybs"], key=("ybb", ysl))

                n_it = len(items)
                for i in range(n_it + 3):
                    if i < n_it:
                        st_a0(i)
                    if 0 <= i - 1 < n_it:
                        st_a1(i - 1)
                    if 0 <= i - 2 < n_it:
                        st_b(i - 2)
                    if 0 <= i - 3 < n_it:
                        st_c(i - 3)
                u1 = gt[0][:, 0:64].rearrange("p (c s) -> p c s", c=32)[:, :, 0:nseg]
                u2 = gt[1][:, 0:64].rearrange("p (c s) -> p c s", c=32)[:, :, 0:nseg]
                if use_carry:
                    cc = CL[:, 9, dirn * 32:(dirn + 1) * 32]; ss = SL[:, 9, dirn * 32:(dirn + 1) * 32]
                    outs = (carry[:, dirn, :, 0], carry[:, dirn, :, 1])
                    zre_, zim_ = zl4[:, :, 0, 0], zl4[:, :, 0, 1]
                    u1_, u2_ = gt[0][:, 0:32], gt[1][:, 0:32]
                else:
                    cc = c255[:, dirn * 32:(dirn + 1) * 32].unsqueeze(2).to_broadcast([128, 32, nseg])
                    ss = s255[:, dirn * 32:(dirn + 1) * 32].unsqueeze(2).to_broadcast([128, 32, nseg])
                    fv = fin[:, fin_seq:fin_seq + nseg, dirn]
                    outs = (fv[:, :, 0, :].rearrange("p s c -> p c s"), fv[:, :, 1, :].rearrange("p s c -> p c s"))
                    zre_, zim_ = zl4[:, :, :, 0], zl4[:, :, :, 1]
                    u1_, u2_ = u1, u2
                V("dve", lambda e: e.tensor_tensor(out=u1_, in0=zre_, in1=cc, op=ALU.mult), ["zl", "sm"], ["g1"])
                V("dve", lambda e: e.tensor_tensor(out=u2_, in0=zim_, in1=ss, op=ALU.mult), ["zl", "sm"], ["g2"])
                V("dve", lambda e: e.tensor_tensor(out=outs[0], in0=u1_, in1=u2_, op=ALU.subtract), ["g1", "g2"], ["carry", "fin"])
                V("dve", lambda e: e.tensor_tensor(out=u1_, in0=zre_, in1=ss, op=ALU.mult), ["zl", "sm"], ["g1"])
                V("dve", lambda e: e.tensor_tensor(out=u2_, in0=zim_, in1=cc, op=ALU.mult), ["zl", "sm"], ["g2"])
                V("dve", lambda e: e.tensor_tensor(out=outs[1], in0=u1_, in1=u2_, op=ALU.add), ["g1", "g2"], ["carry", "fin"])

            for pt in range(2):
                s5_tile(pt * T, 0, False, 2, False, "A", yoff=pt * T, fin_seq=2 * pt)
                s5_tile(pt * T, 1, True, 2, False, "B", yoff=pt * T, fin_seq=2 * pt)
            for i in range(4):
                s5_tile(1024 + i * T, 0, False, 1, True, "A", yoff=1024 + i * T)
            for k in (3, 2, 1, 0):
                s5_tile(3072 + k * T, 1, True, 1, True, "N")
            for i in (3, 2, 1, 0):
                s5_tile(1024 + i * T, 1, True, 1, True, "B", yoff=1024 + i * T)
            P.dma("sp", lambda e: e.dma_start(out=ns, in_=fin.rearrange("p s d r c -> p (s d r c)")), reads=["fin"], writes=["ns"], key="nsst")

            chk(6)
            def attention(ti):
                g0 = ti * T
                prompt = ti < 2
                A.reset(PERS)
                Qt = A.bf(8 * T).rearrange("p (k t) -> p k t", k=8)
                nkw = T if prompt else 1024
                Kw = A.bf(8 * nkw).rearrange("p (k t) -> p k t", k=8)
                nvb = nkw // 128
                Vw = A.bf(nvb * 1040).rearrange("p (b h e) -> p b h e", b=nvb, h=16)
                npart = 128 if prompt else 64
                Osb = A.bf(8 * 1024).rearrange("p (b f) -> p b f", b=8)
                pTs = [A.bf(4 * T).rearrange("p (b t) -> p b t", b=4) for _ in range(2)]
                rcs = [A.f32(4) for _ in range(4)]
                rcc = [0]

                def tmp():
                    i = rcc[0] % 4
                    rcc[0] += 1
                    return rcs[i], ("rc", i)
                qv = qs.rearrange("(c p) t -> p c t", p=128)
                kv = ks.rearrange("(c p) t -> p c t", p=128)
                vv = vs.rearrange("(b p) e -> p b e", p=128)
                P.dma("sp", lambda e: e.dma_start(out=Qt, in_=qv[:, :, g0:g0 + T]), writes=["Qt"], key="Qt")
                if prompt:
                    k0 = g0
                else:
                    r0 = 8 * (ti - 2)
                    wr0 = max(r0 - 4, 0)
                    k0 = 1024 + wr0 * 64
                P.dma("sp", lambda e: e.dma_start(out=Kw, in_=kv[:, :, k0:k0 + nkw]), writes=["Kw"], key="Kw")
                P.dma("sp", lambda e: e.dma_start(out=Vw.rearrange("p b h e -> p b (h e)"), in_=vv[:, k0 // 128:k0 // 128 + nvb, :]), writes=["Vw"], key="Vw")
                if not prompt:
                    cK = A.bf(8 * 512).rearrange("p (k t) -> p k t", k=8)
                    cV = A.bf(4 * 1040).rearrange("p (b h e) -> p b h e", b=4, h=16)
                    _CACHE.setdefault("offs", {})[("Tb", ti)] = A.off
                    Tb = A.bf(16 * NSLOT * 64).rearrange("p (h s q) -> p h s q", h=16, s=NSLOT)
                    pTl = [A.bf(320) for _ in range(2)]
                    P.dma("pool", lambda e: e.dma_start(out=cK.rearrange("p k t -> p (k t)").rearrange("p (a b) -> p a b", a=4), in_=ckT.rearrange("p (a b) -> p a b", a=4)), writes=["cK"], key="cK")
                    V("dve", lambda e: e.memset(cV[:, :, :, 64:65], 1.0), [], ["cV"])
                    P.dma("pool", lambda e: e.dma_start(out=cV[:, :, :, 0:64], in_=cvt.rearrange("p (b h d) -> p b h d", b=4, h=16)), writes=["cV"], key="cV")
                    P.dma("pool", lambda e: e.dma_start(out=Tb[0:64].rearrange("p h s q -> p h (s q)"), in_=rpbx.rearrange("p (h x) -> p h x", h=16)), writes=["Tb"], key="Tb")
                    P.dma("pool", lambda e: e.dma_start(out=Tb[64:128].rearrange("p h s q -> p h (s q)"), in_=rpbx.rearrange("p (h x) -> p h x", h=16)), writes=["Tb"], key="Tb")
                if ti == 2:
                    chk(8.01)
                bfv = lambda ps: ps.bitcast(BF16)
                pc = [0]
                if prompt:
                    for hg in range(4):
                        for s_ in range(2):
                            pts = []
                            for hh in range(4):
                                h = hg * 4 + hh
                                ch, pb = h // 2, 64 * (h % 2)
                                b, ps, pr = bank()
                                for kb in range(2):
                                    V("pe", lambda e, ps=ps, kb=kb, ch=ch, pb=pb, s_=s_: e.matmul(ps[:, kb * 256:(kb + 1) * 256], lhsT=Kw[pb:pb + 64, ch, s_ * 256 + kb * 128:s_ * 256 + (kb + 1) * 128],
                                                                                                 rhs=Qt[pb:pb + 64, ch, s_ * 256:(s_ + 1) * 256], start=True, stop=True), ["Kw", "Qt"], [pr])
                                pi = pc[0] % 8
                                pc[0] += 1
                                pT = pTs[pi // 4][:, pi % 4, :]
                                V("act", lambda e, pT=pT, ps=ps: e.activation(out=pT, in_=ps, func=AF.Exp), [pr], [("pT", pi)])
                                pts.append((pT, ("pT", pi)))
                            for qb in range(2):
                                b, ps, pr = bank()
                                o4 = ps.rearrange("p (h e) -> p h e", h=4)
                                for hh in range(4):
                                    h = hg * 4 + hh
                                    pT, ptr = pts[hh]
                                    for kb in range(2):
                                        V("pe", lambda e, o4=o4, hh=hh, pT=pT, kb=kb, qb=qb, h=h, s_=s_: e.matmul(o4[:, hh, 0:65], lhsT=pT[:, kb * 256 + qb * 128:kb * 256 + (qb + 1) * 128],
                                                                                                                  rhs=Vw[:, s_ * 2 + kb, h, :], start=(kb == 0), stop=(kb == 1)), [ptr, "Vw"], [pr])
                                rc, rcr = tmp()
                                rc4 = rc[:, 0:4].unsqueeze(2)
                                V("dve", lambda e, rc4=rc4, o4=o4: e.reciprocal(out=rc4, in_=o4[:, :, 64:65]), [pr], [rcr])
                                V("dve", lambda e, rc4=rc4, o4=o4, s_=s_, qb=qb, hg=hg: e.tensor_tensor(out=Osb[:, s_ * 2 + qb, hg * 256:(hg + 1) * 256].rearrange("p (h d) -> p h d", h=4), in0=o4[:, :, 0:64],
                                                                                                   in1=rc4.to_broadcast([128, 4, 64]), op=ALU.mult), [pr, rcr], ["Osb"])
                    for fc in range(8):
                        b, ps, pr = bank()
                        pbv = bfv(ps)
                        for blk in range(4):
                            V("pe", lambda e, pbv=pbv, blk=blk, fc=fc: e.transpose(out=pbv[:, blk * 128:(blk + 1) * 128], in_=Osb[:, blk, fc * 128:(fc + 1) * 128], identity=identb), ["Osb", "identb"], [pr])
                        V("act", lambda e, pbv=pbv, fc=fc: e.activation(out=ATv[:, fc, :], in_=pbv[:, 0:T], func=AF.Copy), [pr], ["AT"])
                else:
                    sbc = [0]
                    obc = [0]

                    def sbank():
                        bi = sbc[0] % 6
                        sbc[0] += 1
                        return bi, psb[bi][:], ("ps", bi)

                    def obank():
                        bi = 6 + obc[0] % 2
                        obc[0] += 1
                        return bi, psb[bi][:], ("ps", bi)

                    jobs = [(h, half, r4) for h in range(16) for half in range(2) for r4 in range(4)]
                    jctx = {}

                    def S1(n):
                        h, half, r4 = jobs[n]
                        ch, pb = h // 2, 64 * (h % 2)
                        pTc = pTs[h % 2]
                        if half == 0 and r4 == 0:
                            for blk in range(4):
                                b_, ps_, pr_ = sbank()
                                V("pe", lambda e, ps_=ps_, blk=blk: e.matmul(ps_, lhsT=cK[pb:pb + 64, ch, blk * 128:(blk + 1) * 128], rhs=Qt[pb:pb + 64, ch, :], start=True, stop=True), ["cK", "Qt"], [pr_])
                                V("act", lambda e, ps_=ps_, blk=blk: e.activation(out=pTc[:, blk, :], in_=ps_, func=AF.Exp), [pr_], [("pTc", h % 2)])
                        rr_ = half * 4 + r4
                        r_p = r0 + rr_
                        if r_p < 4:
                            sr, npair, edge = 0, 4, True
                        else:
                            sr, npair, edge = (r_p - 4) & ~1, 5, False
                        b, ps, pr = sbank()
                        for pp in range(npair):
                            kt = (sr - wr0 + 2 * pp) * 64
                            V("pe", lambda e, pp=pp, kt=kt: e.matmul(ps[:, pp * 64:(pp + 1) * 64], lhsT=Kw[pb:pb + 64, ch, kt:kt + 128], rhs=Qt[pb:pb + 64, ch, rr_ * 64:(rr_ + 1) * 64],
                                                                     start=True, stop=False), ["Kw", "Qt"], [pr])
                            for jj in range(2):
                                off = sr + 2 * pp + jj - r_p
                                slot = (11 + off + 3) if edge else (off + 5)
                                assert 0 <= slot < NSLOT
                                V("pe", lambda e, pp=pp, jj=jj, slot=slot: e.matmul(ps[:, pp * 64:(pp + 1) * 64], lhsT=selb[pb:pb + 64, jj * 128:(jj + 1) * 128], rhs=Tb[pb:pb + 64, h, slot, :],
                                                                                 start=False, stop=(jj == 1)), ["selb", "Tb"], [pr])
                        li = pc[0] % 2
                        pc[0] += 1
                        pl = pTl[li]
                        V("act", lambda e: e.activation(out=pl[:, 0:npair * 64], in_=ps[:, 0:npair * 64], func=AF.Exp), [pr], [("pTl", li)])
                        jctx[n] = (pl, li, sr, npair, rr_, pTc)

                    ocur = [None]

                    def S2(n):
                        h, half, r4 = jobs[n]
                        pl, li, sr, npair, rr_, pTc = jctx.pop(n)
                        if r4 == 0:
                            bo, pso, pro = obank()
                            ocur[0] = (pso.rearrange("p (r e) -> p r e", r=4), pro)
                        o4, pro = ocur[0]
                        for pp in range(npair):
                            V("pe", lambda e, pp=pp: e.matmul(o4[0:64, r4, 0:65], lhsT=pl[:, pp * 64:(pp + 1) * 64], rhs=Vw[:, (sr - wr0) // 2 + pp, h, :], start=(pp == 0), stop=False),
                              [("pTl", li), "Vw"], [pro])
                        for blk in range(4):
                            V("pe", lambda e, blk=blk: e.matmul(o4[0:64, r4, 0:65], lhsT=pTc[:, blk, rr_ * 64:(rr_ + 1) * 64], rhs=cV[:, blk, h, :], start=False, stop=(blk == 3)),
                              [("pTc", h % 2), "cV"], [pro])
                        if r4 == 3:
                            rc, rcr = tmp()
                            rc4 = rc[0:64, 0:4].unsqueeze(2)
                            V("dve", lambda e: e.reciprocal(out=rc4, in_=o4[0:64, :, 64:65]), [pro], [rcr])
                            V("dve", lambda e: e.tensor_tensor(out=Osb[0:64, half * 4:(half + 1) * 4, h * 64:(h + 1) * 64], in0=o4[0:64, :, 0:64], in1=rc4.to_broadcast([64, 4, 64]), op=ALU.mult),
                              [pro, rcr], ["Osb"])

                    S1(0)
                    for n in range(len(jobs)):
                        if n + 1 < len(jobs):
                            S1(n + 1)
                        S2(n)
                    if ti == 2:
                        chk(8.04)
                    for fc in range(8):
                        b, ps, pr = bank()
                        pbv = bfv(ps)
                        for rr_ in range(8):
                            V("pe", lambda e, pbv=pbv, rr_=rr_, fc=fc: e.transpose(out=pbv[:, rr_ * 64:(rr_ + 1) * 64], in_=Osb[0:64, rr_, fc * 128:(fc + 1) * 128], identity=identb[0:64, 0:64]), ["Osb", "identb"], [pr])
                        V("act", lambda e, pbv=pbv, fc=fc: e.activation(out=ATv[:, fc, :], in_=pbv[:, 0:T], func=AF.Copy), [pr], ["AT"])

            def phaseB(ti):
                g0 = ti * T
                ci = 0 if ti < 2 else 1
                P.barrier()
                attention(ti)
                if ti == 0:
                    chk(6.1)
                if ti == 2:
                    chk(8.1)
                P.barrier()
                xv, hv, rs = alloc_common()
                yb = A.bf(8 * T).rearrange("p (k t) -> p k t", k=8)
                yb2 = A.bf(8 * T).rearrange("p (k t) -> p k t", k=8)
                mg_ = A.bf(16 * T).rearrange("p (k t) -> p k t", k=16)
                P.dma("sp", lambda e: e.dma_start(out=yb, in_=ybs.rearrange("(c p) t -> p c t", p=128)[:, :, g0:g0 + T]), writes=["yb"], key="yb")
                P.dma("sp", lambda e: e.dma_start(out=xv, in_=x1s.rearrange("(k p) t -> p k t", p=128)[:, :, g0:g0 + T]), writes=XR, key="xld")
                wv, wr = wslot()
                wv3 = wv.rearrange("p (k n) -> p k n", k=8)
                wdma(wv3, w_glu.rearrange("(k p) n -> p k n", p=128), wr)
                for m in range(8):
                    b, ps, pr = bank()
                    for k in range(8):
                        V("pe", lambda e, ps=ps, k=k, m=m, wv3=wv3: e.matmul(ps, lhsT=wv3[:, k, m * 128:(m + 1) * 128], rhs=yb[:, k, :], start=(k == 0), stop=(k == 7)), [wr, "yb"], [pr])
                    sg, sr_ = tmp()
                    V("act", lambda e, sg=sg, ps=ps: e.activation(out=sg, in_=ps, func=AF.Sigmoid), [pr], [sr_])
                    V("dve", lambda e, sg=sg, m=m: e.tensor_tensor(out=yb2[:, m, :], in0=sg, in1=yb[:, m, :], op=ALU.mult), [sr_, "yb"], ["yb2"])
                norm_mod(xv, hv, rs, 1, ci)
                if ti == 0:
                    chk(6.3)
                upa = w_up_a.rearrange("(k p) n -> p k n", p=128)
                upb = w_up_b.rearrange("(k p) n -> p k n", p=128)
                for mg in range(8):
                    wg, wrg = wslot()
                    wg4 = wg.rearrange("p (g k n) -> p g k n", g=2, k=16)
                    wdma(wg4[:, 0], win_v[:, :, 4096 + mg * 256:4096 + (mg + 1) * 256], wrg)
                    wdma(wg4[:, 1], win_v[:, :, 6144 + mg * 256:6144 + (mg + 1) * 256], wrg)
                    wu, wru = wslot()
                    wu4 = wu[:, 0:4096].rearrange("p (g k n) -> p g k n", g=2, k=8)
                    wdma(wu4[:, 0], upa[:, :, mg * 256:(mg + 1) * 256], wru)
                    wdma(wu4[:, 1], upb[:, :, mg * 256:(mg + 1) * 256], wru)
                    for mm in range(2):
                        m = mg * 2 + mm
                        b1, p1, r1 = bank(); b2, p2, r2 = bank(); b3, p3, r3 = bank(); b4, p4, r4 = bank()
                        for k in range(16):
                            V("pe", lambda e, p1=p1, k=k, mm=mm, wg4=wg4: e.matmul(p1, lhsT=wg4[:, 0, k, mm * 128:(mm + 1) * 128], rhs=hv[:, k, :], start=(k == 0), stop=(k == 15)), [wrg, ("h", k)], [r1])
                        for k in range(8):
                            V("pe", lambda e, p2=p2, k=k, mm=mm, wu4=wu4: e.matmul(p2, lhsT=wu4[:, 0, k, mm * 128:(mm + 1) * 128], rhs=ATv[:, k, :], start=(k == 0), stop=(k == 7)), [wru, "AT"], [r2])
                        for k in range(16):
                            V("pe", lambda e, p3=p3, k=k, mm=mm, wg4=wg4: e.matmul(p3, lhsT=wg4[:, 1, k, mm * 128:(mm + 1) * 128], rhs=hv[:, k, :], start=(k == 0), stop=(k == 15)), [wrg, ("h", k)], [r3])
                        for k in range(8):
                            V("pe", lambda e, p4=p4, k=k, mm=mm, wu4=wu4: e.matmul(p4, lhsT=wu4[:, 1, k, mm * 128:(mm + 1) * 128], rhs=yb2[:, k, :], start=(k == 0), stop=(k == 7)), [wru, "yb2"], [r4])
                        s1, s1r = tmp(); s2, s2r = tmp()
                        V("act", lambda e, s1=s1, p1=p1: e.activation(out=s1, in_=p1, func=AF.Sigmoid), [r1], [s1r])
                        V("dve", lambda e, s1=s1, p2=p2: e.tensor_tensor(out=s1, in0=s1, in1=p2, op=ALU.mult), [s1r, r2], [s1r])
                        V("act", lambda e, s2=s2, p3=p3: e.activation(out=s2, in_=p3, func=AF.Sigmoid), [r3], [s2r])
                        V("dve", lambda e, s2=s2, p4=p4: e.tensor_tensor(out=s2, in0=s2, in1=p4, op=ALU.mult), [s2r, r4], [s2r])
                        V("pool", lambda e, s1=s1, s2=s2, m=m: e.tensor_tensor(out=mg_[:, m, :], in0=s1, in1=s2, op=ALU.add), [s1r, s2r], [("mg", m)])

                if ti == 0:
                    chk(6.5)

                def ev_o(c, ps, pr):
                    V("dve", lambda e: e.scalar_tensor_tensor(out=xv[:, c, :], in0=ps, scalar=mods[:, 80 + c, ci:ci + 1], in1=xv[:, c, :], op0=ALU.mult, op1=ALU.add), [pr, ("x", c), "mods"], [("x", c)])
                linear_fm(lambda k: mg_[:, k, :], (lambda k: [("mg", k)]), 16, lambda cb: w_out.rearrange("(k p) n -> p k n", p=128)[:, :, cb * 512:(cb + 1) * 512], 4, ev_o)
                P.barrier()
                if ti == 0:
                    chk(6.7)
                xv2, hv2, rs2 = alloc_common()
                actv2 = A.bf(44 * T).rearrange("p (j t) -> p j t", j=44)
                norm_mod(xv2, hv2, rs2, 2, ci)
                ffn(1, xv2, hv2, actv2, ci)
                rstd_of(xv2, rs2)
                for k in range(16):
                    V("dve", lambda e, k=k: e.scalar_tensor_tensor(out=xv2[:, k, :], in0=xv2[:, k, :], scalar=fgs[:, k:k + 1], in1=rs2, op0=ALU.mult, op1=ALU.mult), [("x", k), "rs", "fgs"], [("x", k)])
                dst = yp if ti < 2 else ys
                t0 = g0 if ti < 2 else g0 - 1024
                P.dma("sp", lambda e: e.dma_start(out=dst.rearrange("(k p) t -> p k t", p=128)[:, :, t0:t0 + T], in_=xv2), reads=XR, writes=[("y", ti)], key="xst")

            for ti in range(6):
                phaseB(ti)
                chk(7 + ti)
        except _Stop:
            pass
        P.emit()
    return nc, P.stats


def _bias_table(rpb, flip):
    tb = np.full((64, 16, NSLOT, 64), NEG, np.float32)
    kc = np.arange(64)[:, None]
    qc = np.arange(64)[None, :]
    if not flip:
        cok = (kc >= np.clip(qc - 8, 0, 48)) & (kc <= np.clip(qc - 8, 0, 48) + 15)
        dc = kc - qc + 15
    else:
        cok = (kc >= np.clip(qc - 7, 0, 48)) & (kc <= np.clip(qc - 7, 0, 48) + 15)
        dc = qc - kc + 15
    dcc = np.clip(dc, 0, 30)
    for slot in range(NSLOT):
        edge = slot >= 11
        off = (slot - 11 - 3) if edge else (slot - 5)
        dr = (7 - off) if flip else (off + 7)
        if edge:
            valid = 0 <= dr <= 14
        else:
            valid = (-3 <= off <= 4) if flip else (-4 <= off <= 3)
        if not valid:
            continue
        g = rpb[:, dr, :][:, dcc]
        g = np.where(cok[None], g, np.float32(NEG))
        tb[:, :, slot, :] = np.transpose(g, (1, 0, 2))
    return np.ascontiguousarray(tb.reshape(64, -1))


def _pad_bc(w, c_layout):
    out = np.zeros((2, 32, 2, 64, 4, 2, 16), np.float32)
    for c in range(32):
        for gj in range(2):
            g = 2 * c + gj
            if c_layout:
                blk = np.transpose(w[:, g], (0, 2, 1))
            else:
                blk = w[:, g]
            out[:, c, gj, :, c % 4, gj, :] = blk
    return np.ascontiguousarray(out.reshape(64, 128, 128))


_CACHE = {}


def kernel(x_prompt, x_sample, cache_k, cache_v, state_ssm, c, c_ctx, w_ada, b_ada, norm_g,
           ffn_in, ffn_out, w_in, rpb, s5_lam_re, s5_lam_im, s5_log_dt, s5_b_re, s5_b_im,
           s5_c_re, s5_c_im, s5_d, w_glu, w_up_a, w_up_b, w_out, final_g):
    f = np.float32
    A_ = lambda a: np.ascontiguousarray(np.asarray(a, dtype=f))
    x_prompt, x_sample = A_(x_prompt), A_(x_sample)
    if "nc" not in _CACHE:
        _CACHE["nc"] = build_program()
    nc, stats = _CACHE["nc"]
    print("program stats (ops, eng counts, nsems, max dma sem):", stats, flush=True)

    def chunked(v, nchunk):
        return np.ascontiguousarray(np.asarray(v, f).reshape(nchunk, 128).T)

    shared = dict(
        w_ada=A_(w_ada[0]), b_ada=chunked(b_ada[0], 144),
        ng=np.ascontiguousarray(np.concatenate([chunked(norm_g[0, n], 16) for n in range(3)], axis=1)),
        fg=chunked(final_g, 16), ffn_in=A_(ffn_in[0]), ffn_out=A_(ffn_out[0]), w_in=A_(w_in[0]),
        w_glu=A_(w_glu[0]), w_up_a=A_(w_up_a[0]), w_up_b=A_(w_up_b[0]), w_out=A_(w_out[0]),
        dsk=chunked(s5_d[0], 8),
    )
    cstm = np.zeros((128, 384), f)
    cstm[:, 0:128] = np.eye(128, dtype=f)
    cstm[0:64, 128:192] = np.eye(64, dtype=f)
    cstm[0:64, 256 + 64:256 + 128] = np.eye(64, dtype=f)
    cstm[64:128, 128:192] = np.eye(64, dtype=f)
    cstm[64:128, 256 + 64:256 + 128] = np.eye(64, dtype=f)
    shared["cst"] = cstm

    def chan(a):
        a = np.asarray(a, f).reshape(2, 32, 2, 64)
        return np.ascontiguousarray(np.transpose(a, (2, 3, 0, 1)).reshape(128, 64))

    per_orient = {}
    for flip in (False, True):
        sw = (lambda a: np.asarray(a, f)[::-1]) if flip else (lambda a: np.asarray(a, f))
        ldt_full = np.broadcast_to(np.asarray(s5_log_dt[0], f)[:, :, None], (2, 64, 64))
        per_orient[flip] = dict(
            lamre=chan(sw(s5_lam_re[0])), lamim=chan(sw(s5_lam_im[0])), ldt=chan(sw(ldt_full)),
            bpr=_pad_bc(sw(s5_b_re[0]), False), bpi=_pad_bc(sw(s5_b_im[0]), False),
            cpr=_pad_bc(sw(s5_c_re[0]), True), cpi=_pad_bc(sw(s5_c_im[0]), True),
            rpbx=_bias_table(np.asarray(rpb[0], f), flip),
        )

    in_maps = []
    for core in range(8):
        flip = (core % 2 == 1)
        b = core // 2
        xpT = x_prompt[4 * core:4 * core + 4]
        xsq = x_sample[b]
        if flip:
            xpT = xpT[:, ::-1]
            xsq = xsq[::-1]
        m = dict(shared)
        m.update(per_orient[flip])
        m["xp"] = np.ascontiguousarray(xpT.reshape(1024, D).T)
        m["xo"] = np.ascontiguousarray(xsq[0:2048].T)
        m["xh"] = np.ascontiguousarray(xsq[2048:4096].T)
        cc = np.stack([np.asarray(c_ctx, f), np.asarray(c[b], f)], axis=1)
        m["cond"] = np.ascontiguousarray(cc.reshape(16, 128, 2).transpose(1, 0, 2).reshape(128, 32))
        st_ = np.asarray(state_ssm[b, 0], f)
        if flip:
            st_ = st_[::-1]
        st_ = st_.reshape(2, 2, 32, 2, 64)
        m["h0"] = np.ascontiguousarray(np.transpose(st_, (3, 4, 0, 1, 2)).reshape(128, 128))
        ck = np.asarray(cache_k[b, 0], f)
        m["ckT"] = np.ascontiguousarray(np.transpose(ck.reshape(8, 2, 512, 64), (1, 3, 0, 2)).reshape(128, 4096))
        cv = np.asarray(cache_v[b, 0], f)
        m["cvt"] = np.ascontiguousarray(np.transpose(cv.reshape(16, 4, 128, 64), (2, 1, 0, 3)).reshape(128, 4096))
        in_maps.append(m)

    if _CACHE.get("sim_hook") is not None:
        R = _CACHE["sim_hook"](nc, in_maps)
    else:
        res = run_bass_kernel_spmd(nc, in_maps, core_ids=list(range(8)))
        R = res.results
    y_prompt = np.zeros((32, 256, D), f); y_sample = np.zeros((4, 4096, D), f)
    nck = np.zeros((32, 1, 16, 256, 64), f); ncv = np.zeros((32, 1, 16, 256, 64), f)
    nss = np.zeros((32, 1, 2, 2, 64, 64), f)
    for core in range(8):
        flip = (core % 2 == 1)
        b = core // 2
        r = R[core]
        ypc = np.asarray(r["yp"]).T.reshape(4, 256, D)
        ysc = np.asarray(r["ys"]).T
        kk = np.asarray(r["nk"]).reshape(16, 64, 4, 256)
        kk = np.transpose(kk, (2, 0, 3, 1))
        vv = np.asarray(r["nv"]).reshape(4, 256, 16, 64)
        vv = np.transpose(vv, (0, 2, 1, 3))
        s_ = np.asarray(r["ns"]).reshape(2, 64, 4, 2, 2, 32)
        s_ = np.transpose(s_, (2, 3, 4, 5, 0, 1)).reshape(4, 2, 2, 64, 64)
        if flip:
            ypc = ypc[:, ::-1]
            kk = kk[:, :, ::-1]
            vv = vv[:, :, ::-1]
            s_ = s_[:, ::-1]
            y_sample[b, 2048:4096] = ysc[::-1]
        else:
            y_sample[b, 0:2048] = ysc
        y_prompt[4 * core:4 * core + 4] = ypc
        nck[4 * core:4 * core + 4, 0] = kk
        ncv[4 * core:4 * core + 4, 0] = vv
        nss[4 * core:4 * core + 4, 0] = s_
    return (y_prompt, y_sample, nck, ncv, nss)
```

```python
import contextlib
import math
import numpy as np
import concourse.bass as bass
import concourse.mybir as mybir
from concourse.bass_utils import run_bass_kernel_spmd

F32 = mybir.dt.float32
BF16 = mybir.dt.bfloat16
ALU = mybir.AluOpType
AF = mybir.ActivationFunctionType

T = 512
NEG = -30000.0
EPS = 1e-6
D = 2048
DFF = 5632
NSLOT = 22
ARENA = 52800
STAGE = 99
KSQ = 'act'


class Prog:
    ENGS = ("pe", "dve", "act", "pool", "sp")

    def __init__(self, nc):
        self.nc = nc
        self.ops = []
        self.last_w = {}
        self.readers = {}
        self.pending = {}
        self.last_eng = {}
        self.dma_since = []

    def _add(self, eng, fn, reads, writes, dma_key=None):
        idx = len(self.ops)
        deps = set()
        for r in reads:
            w = self.last_w.get(r)
            if w is not None:
                deps.add(w)
        for r in writes:
            w = self.last_w.get(r)
            if w is not None:
                deps.add(w)
            rl = self.readers.get(r)
            if rl:
                deps.update(rl.values())
        rkey = eng if dma_key is None else ("dma", idx)
        for r in reads:
            self.readers.setdefault(r, {})[rkey] = idx
        for r in writes:
            self.last_w[r] = idx
            self.readers[r] = {}
        pb = self.pending.pop(eng, None)
        if pb:
            deps.update(pb)
        deps.discard(idx)
        self.ops.append(dict(eng=eng, fn=fn, deps=deps, dma=dma_key))
        self.last_eng[eng] = idx
        if dma_key is not None:
            self.dma_since.append(idx)
        return idx

    def op(self, eng, fn, reads=(), writes=()):
        return self._add(eng, fn, tuple(reads), tuple(writes))

    def dma(self, queue, fn, reads=(), writes=(), key=None):
        return self._add(queue, fn, tuple(reads), tuple(writes), dma_key=key)

    def barrier(self):
        b = set(self.last_eng.values()) | set(self.dma_since)
        self.dma_since = []
        for e in self.ENGS:
            self.pending[e] = set(b) | self.pending.get(e, set())

    def emit(self):
        nc = self.nc
        ops = self.ops
        needed = set()
        for i, o in enumerate(ops):
            nd = set()
            for d in o["deps"]:
                od = ops[d]
                if od["dma"] is None and od["eng"] == "pe" and o["eng"] == "pe" and o["dma"] is None:
                    continue
                nd.add(d)
            o["deps"] = nd
            needed |= nd
        cnt = {e: 0 for e in self.ENGS}
        dcnt = {}
        for i, o in enumerate(ops):
            if o["dma"] is not None:
                k = o["dma"]
                dcnt[k] = dcnt.get(k, 0) + 16
                o["sig"] = ("d:" + str(k), dcnt[k])
            elif i in needed:
                cnt[o["eng"]] += 1
                o["sig"] = ("e:" + o["eng"], cnt[o["eng"]])
            else:
                o["sig"] = None
        semnames = ["e:" + e for e in self.ENGS] + ["d:" + str(k) for k in dcnt]
        self.stats = (len(ops), dict(cnt), len(semnames), max(dcnt.values()) if dcnt else 0)
        per_eng = {e: [] for e in self.ENGS}
        for i, o in enumerate(ops):
            per_eng[o["eng"]].append(i)
        with contextlib.ExitStack() as st:
            sems = {}
            for j, n in enumerate(semnames):
                sems[n] = st.enter_context(nc.semaphore("s%d" % j))
            block = st.enter_context(nc.Block())

            def make(e):
                def body(eng):
                    known = {}
                    for i in per_eng[e]:
                        o = ops[i]
                        w = {}
                        for d in o["deps"]:
                            s, v = ops[d]["sig"]
                            if v > w.get(s, 0):
                                w[s] = v
                        for s, v in w.items():
                            if known.get(s, 0) >= v:
                                continue
                            known[s] = v
                            eng.wait_ge(sems[s], v)
                        ins = o["fn"](eng)
                        if o["sig"] is not None:
                            s, v = o["sig"]
                            ins.then_inc(sems[s], 16 if o["dma"] is not None else 1)
                    if e == "sp":
                        for k, v in dcnt.items():
                            if known.get("d:" + str(k), 0) < v:
                                eng.wait_ge(sems["d:" + str(k)], v)
                return body

            block.tensor(make("pe"))
            block.vector(make("dve"))
            block.scalar(make("act"))
            block.gpsimd(make("pool"))
            block.sync(make("sp"))


class Arena:
    def __init__(self, ap, size):
        self.ap, self.size, self.off = ap, size, 0

    def f32(self, n):
        v = self.ap[:, self.off:self.off + n]
        self.off += n
        assert self.off <= self.size, ("arena overflow", self.off)
        return v

    def bf(self, n):
        nf = (n + 1) // 2
        return self.f32(nf).bitcast(BF16)

    def reset(self, to):
        self.off = to


def build_program():
    nc = bass.Bass("TRN2", target_bir_lowering=False)

    def din(name, shape, dt=F32):
        return nc.dram_tensor(name, list(shape), dt, kind="ExternalInput").ap()

    def dout(name, shape, dt=F32):
        return nc.dram_tensor(name, list(shape), dt, kind="ExternalOutput").ap()

    def dscr(name, shape, dt=F32):
        return nc.dram_tensor(name, list(shape), dt, kind="Internal").ap()

    xp = din("xp", [D, 1024]); xo = din("xo", [D, 2048]); xh = din("xh", [D, 2048])
    cond = din("cond", [128, 32])
    w_ada = din("w_ada", [D, 9 * D]); b_ada = din("b_ada", [128, 144])
    ng = din("ng", [128, 48]); fg = din("fg", [128, 16])
    ffn_in = din("ffn_in", [2, D, 2 * DFF]); ffn_out = din("ffn_out", [2, DFF, D])
    w_in = din("w_in", [D, 8192]); w_glu = din("w_glu", [1024, 1024])
    w_up_a = din("w_up_a", [1024, D]); w_up_b = din("w_up_b", [1024, D]); w_out = din("w_out", [D, D])
    lamre = din("lamre", [128, 64]); lamim = din("lamim", [128, 64]); ldt = din("ldt", [128, 64])
    bpr = din("bpr", [64, 128, 128]); bpi = din("bpi", [64, 128, 128])
    cpr = din("cpr", [64, 128, 128]); cpi = din("cpi", [64, 128, 128])
    dsk = din("dsk", [128, 8]); h0 = din("h0", [128, 128])
    ckT = din("ckT", [128, 4096]); cvt = din("cvt", [128, 4096])
    rpbx = din("rpbx", [64, 16 * NSLOT * 64])
    cst = din("cst", [128, 128 + 256])

    yp = dout("yp", [D, 1024]); ys = dout("ys", [D, 2048])
    nk = dout("nk", [1024, 1024]); nv = dout("nv", [1024, 1024]); ns = dout("ns", [128, 512])

    x1s = dscr("x1s", [D, 3072]); qs = dscr("qs", [1024, 3072], BF16); ks = dscr("ks", [1024, 3584], BF16)
    vs = dscr("vs", [3584, 1040], BF16); us = dscr("us", [1024, 5120], BF16)
    yas = dscr("yas", [1024, 3072]); ybs = dscr("ybs", [1024, 3072], BF16)
    rot = dscr("rot", [64, 128, 1536])

    P = Prog(nc)
    with contextlib.ExitStack() as st:
        E = st.enter_context
        arena_t = E(nc.sbuf_tensor("arena", [128, ARENA], F32))
        psb = [E(nc.psum_tensor("ps%d" % i, [128, 512], F32)) for i in range(8)]
        A = Arena(arena_t[:], ARENA)
        bankc = [0]

        def bank():
            b = bankc[0] % 8
            bankc[0] += 1
            return b, psb[b][:], ("ps", b)

        mods = A.f32(288).rearrange("p (m c) -> p m c", c=2)
        der = A.f32(160).rearrange("p (n k c) -> p n k c", n=5, k=16)
        ngs = A.f32(48); fgs = A.f32(16)
        onesf = A.bf(128)
        csts = A.f32(384)
        identb = A.bf(128)
        selb = A.bf(256)
        epsb = A.f32(1); hpib = A.f32(1)
        rmag = A.f32(64)
        CL = A.f32(640).rearrange("p (k c) -> p k c", k=10)
        SL = A.f32(640).rearrange("p (k c) -> p k c", k=10)
        c255 = A.f32(64); s255 = A.f32(64)
        dsks = A.f32(8)
        carry = A.f32(128).rearrange("p (d c r) -> p d c r", d=2, r=2)
        fin = A.f32(512).rearrange("p (s d r c) -> p s d r c", s=4, d=2, r=2)
        ATv = A.bf(8 * T).rearrange("p (k t) -> p k t", k=8)
        PERS = A.off

        def V(eng, f, reads, writes):
            return P.op(eng, f, reads, writes)

        class _Stop(Exception):
            pass

        def chk(n):
            if STAGE == n:
                raise _Stop()

        try:
            P.dma("sp", lambda e: e.dma_start(out=csts, in_=cst), writes=["csts"], key="csts")
            P.dma("sp", lambda e: e.dma_start(out=ngs, in_=ng), writes=["ngs"], key="ngs")
            P.dma("sp", lambda e: e.dma_start(out=fgs, in_=fg), writes=["fgs"], key="fgs")
            P.dma("sp", lambda e: e.dma_start(out=dsks, in_=dsk), writes=["dsks"], key="dsks")
            V("dve", lambda e: e.memset(onesf, 1.0), [], ["onesf"])
            V("dve", lambda e: e.memset(epsb, EPS), [], ["epsb"])
            V("dve", lambda e: e.memset(hpib, math.pi / 2), [], ["hpib"])
            V("dve", lambda e: e.tensor_copy(out=identb, in_=csts[:, 0:128]), ["csts"], ["identb"])
            V("dve", lambda e: e.tensor_copy(out=selb, in_=csts[:, 128:384]), ["csts"], ["selb"])

            chk(0.1)
            wctr = [0]
            wslots = [None] * 5
            nws = [4]

            def wslot():
                s = wctr[0] % nws[0]
                wctr[0] += 1
                return wslots[s], ("ws", s)

            def wdma(dst, src, res):
                P.dma("pool", lambda e: e.dma_start(out=dst, in_=src), writes=[res], key=res)

            tmpc = [0]
            tmps = [None] * 6

            def tmp():
                i = tmpc[0] % 6
                tmpc[0] += 1
                return tmps[i], ("tmp", i)

            def alloc_common(nslots=4):
                A.reset(PERS)
                nws[0] = nslots
                xv = A.f32(16 * T).rearrange("p (k t) -> p k t", k=16)
                hv = A.bf(16 * T).rearrange("p (k t) -> p k t", k=16)
                for i in range(nslots):
                    wslots[i] = A.bf(8192)
                for i in range(6):
                    tmps[i] = A.f32(T)
                rs = A.f32(T)
                return xv, hv, rs

            xv, hv, rs = alloc_common()
            cnd = A.f32(32); scb = A.bf(32); bad = A.f32(144)
            P.dma("sp", lambda e: e.dma_start(out=cnd, in_=cond), writes=["cnd"], key="cnd")
            P.dma("sp", lambda e: e.dma_start(out=bad, in_=b_ada), writes=["bad"], key="bad")
            V("act", lambda e: e.activation(out=scb, in_=cnd, func=AF.Silu), ["cnd"], ["scb"])
            chk(0.2)
            scb3 = scb.rearrange("p (k c) -> p k c", c=2)
            wav = w_ada.rearrange("(k p) n -> p k n", p=128)
            for blk in range(36):
                wv, wr = wslot()
                wv3 = wv.rearrange("p (k n) -> p k n", k=16)
                wdma(wv3, wav[:, :, blk * 512:(blk + 1) * 512], wr)
                for mc in range(4):
                    m = blk * 4 + mc
                    b, ps, pr = bank()
                    for k in range(16):
                        V("pe", lambda e, ps=ps, wv3=wv3, k=k, mc=mc: e.matmul(ps[:, 0:2], lhsT=wv3[:, k, mc * 128:(mc + 1) * 128], rhs=scb3[:, k, :], start=(k == 0), stop=(k == 15)),
                          [wr, "scb"], [pr])
                    V("dve", lambda e, ps=ps, m=m: e.tensor_scalar(out=mods[:, m, :], in0=ps[:, 0:2], scalar1=bad[:, m:m + 1], scalar2=None, op0=ALU.add),
                      [pr, "bad"], ["mods"])
                chk(0.3 if blk == 0 else -1)
            chk(0.5)
            for n in range(3):
                V("dve", lambda e, n=n: e.tensor_scalar(out=der[:, n], in0=mods[:, 16 * (3 * n + 1):16 * (3 * n + 2), :], scalar1=1.0, scalar2=None, op0=ALU.add),
                  ["mods"], ["der"])
                V("dve", lambda e, n=n: e.tensor_tensor(out=der[:, n], in0=der[:, n], in1=ngs[:, n * 16:(n + 1) * 16].unsqueeze(2).to_broadcast([128, 16, 2]), op=ALU.mult),
                  ["der", "ngs"], ["der"])
            V("dve", lambda e: e.tensor_scalar(out=der[:, 3], in0=mods[:, 32:48, :], scalar1=0.5, scalar2=None, op0=ALU.mult), ["mods"], ["der"])
            V("dve", lambda e: e.tensor_scalar(out=der[:, 4], in0=mods[:, 128:144, :], scalar1=0.5, scalar2=None, op0=ALU.mult), ["mods"], ["der"])

            chk(1)
            XR = [("x", k) for k in range(16)]

            def rstd_of(xv, rs):
                b, ps, pr = bank()
                for k in range(16):
                    sq, sr = tmp()
                    sqb = sq.bitcast(BF16)[:, 0:T]
                    V("act", lambda e, sqb=sqb, k=k: e.activation(out=sqb, in_=xv[:, k, :], func=AF.Square), [("x", k)], [sr])
                    V("pe", lambda e, ps=ps, sqb=sqb, k=k: e.matmul(ps, lhsT=onesf, rhs=sqb, start=(k == 0), stop=(k == 15)), [sr, "onesf"], [pr])
                V("act", lambda e, ps=ps: e.activation(out=rs, in_=ps, func=AF.Sqrt, bias=epsb[:, 0:1], scale=1.0 / D), [pr, "epsb"], ["rs"])
                V("dve", lambda e: e.reciprocal(out=rs, in_=rs), ["rs"], ["rs"])

            def norm_mod(xv, hv, rs, n, ci):
                rstd_of(xv, rs)
                for k in range(16):
                    tk, tr = tmp()
                    V("dve", lambda e, tk=tk, k=k: e.scalar_tensor_tensor(out=tk, in0=xv[:, k, :], scalar=der[:, n, k, ci:ci + 1], in1=rs, op0=ALU.mult, op1=ALU.mult),
                      [("x", k), "rs", "der"], [tr])
                    V("act", lambda e, tk=tk, k=k: e.activation(out=hv[:, k, :], in_=tk, func=AF.Identity, bias=mods[:, 48 * n + k, ci:ci + 1], scale=1.0),
                      [tr, "mods"], [("h", k)])

            fin_v = ffn_in.rearrange("f (k p) n -> f p k n", p=128)
            fout_v = ffn_out.rearrange("f (j p) n -> f p j n", p=128)

            def ffn(f, xv, hv, actv, ci):
                for jb in range(22):
                    wv, wr = wslot()
                    wv4 = wv.rearrange("p (g k n) -> p g k n", g=2, k=16)
                    wdma(wv4[:, 0], fin_v[f, :, :, jb * 256:(jb + 1) * 256], wr)
                    wdma(wv4[:, 1], fin_v[f, :, :, DFF + jb * 256:DFF + (jb + 1) * 256], wr)
                    for jj in range(2):
                        j = 2 * jb + jj
                        bg, psg, prg = bank()
                        bu, psu, pru = bank()
                        for k in range(16):
                            V("pe", lambda e, psg=psg, k=k, jj=jj, wv4=wv4: e.matmul(psg, lhsT=wv4[:, 0, k, jj * 128:(jj + 1) * 128], rhs=hv[:, k, :], start=(k == 0), stop=(k == 15)),
                              [wr, ("h", k)], [prg])
                        for k in range(16):
                            V("pe", lambda e, psu=psu, k=k, jj=jj, wv4=wv4: e.matmul(psu, lhsT=wv4[:, 1, k, jj * 128:(jj + 1) * 128], rhs=hv[:, k, :], start=(k == 0), stop=(k == 15)),
                              [wr, ("h", k)], [pru])
                        sg, sr = tmp()
                        V("act", lambda e, sg=sg, psg=psg: e.activation(out=sg, in_=psg, func=AF.Silu), [prg], [sr])
                        V("dve", lambda e, sg=sg, psu=psu, j=j: e.tensor_tensor(out=actv[:, j, :], in0=sg, in1=psu, op=ALU.mult), [sr, pru], [("act", j)])
                hgi = 3 if f == 0 else 4
                AR = [("act", j) for j in range(44)]
                for m in range(16):
                    wv, wr = wslot()
                    wv3 = wv[:, 0:44 * 128].rearrange("p (j n) -> p j n", j=44)
                    wdma(wv3, fout_v[f, :, :, m * 128:(m + 1) * 128], wr)
                    b, ps, pr = bank()
                    for j in range(44):
                        V("pe", lambda e, ps=ps, j=j, wv3=wv3: e.matmul(ps, lhsT=wv3[:, j, :], rhs=actv[:, j, :], start=(j == 0), stop=(j == 43)),
                          [wr, ("act", j)], [pr])
                    V("dve", lambda e, ps=ps, m=m: e.scalar_tensor_tensor(out=xv[:, m, :], in0=ps, scalar=der[:, hgi, m, ci:ci + 1], in1=xv[:, m, :], op0=ALU.mult, op1=ALU.add),
                      [pr, ("x", m), "der"], [("x", m)])

            def linear_fm(rhs_fn, rhs_res, KC, wsrc_fn, nblocks, evac):
                for cb in range(nblocks):
                    wv, wr = wslot()
                    wv3 = wv[:, 0:KC * 512].rearrange("p (k n) -> p k n", k=KC)
                    wdma(wv3, wsrc_fn(cb), wr)
                    for mc in range(4):
                        b, ps, pr = bank()
                        for k in range(KC):
                            V("pe", lambda e, ps=ps, k=k, mc=mc, wv3=wv3: e.matmul(ps, lhsT=wv3[:, k, mc * 128:(mc + 1) * 128], rhs=rhs_fn(k), start=(k == 0), stop=(k == KC - 1)),
                              [wr] + rhs_res(k), [pr])
                        evac(cb * 4 + mc, ps, pr)

            win_v = w_in.rearrange("(k p) n -> p k n", p=128)

            def phaseA(ti):
                g0 = ti * T
                kind = "p" if ti < 2 else ("o" if ti < 6 else "h")
                ci = 0 if kind == "p" else 1
                src = {"p": xp, "o": xo, "h": xh}[kind]
                t0 = {"p": g0, "o": g0 - 1024, "h": g0 - 3072}[kind]
                P.dma("sp", lambda e: e.dma_start(out=xv, in_=src.rearrange("(k p) t -> p k t", p=128)[:, :, t0:t0 + T]), writes=XR, key="xld")
                norm_mod(xv, hv, rs, 0, ci)
                if ti == 0:
                    chk(1.1)
                ffn(0, xv, hv, actv, ci)
                if ti == 0:
                    chk(1.3)
                if kind != "h":
                    P.dma("sp", lambda e: e.dma_start(out=x1s.rearrange("(k p) t -> p k t", p=128)[:, :, g0:g0 + T], in_=xv), reads=XR, writes=[("x1s", ti)], key="xst")
                norm_mod(xv, hv, rs, 1, ci)
                if ti == 0:
                    chk(1.4)
                hfn = lambda k: hv[:, k, :]
                pst = actv[:, 0:8, :]
                PSTR = [("act", j) for j in range(8)]
                if kind != "h":
                    def ev_q(c, ps, pr):
                        V("act", lambda e: e.activation(out=pst[:, c, :], in_=ps, func=AF.Identity, scale=0.125), [pr], [("act", c)])
                    linear_fm(hfn, (lambda k: [("h", k)]), 16, lambda cb: win_v[:, :, cb * 512:(cb + 1) * 512], 2, ev_q)
                    P.dma(KSQ, lambda e: e.dma_start(out=qs.rearrange("(c p) t -> p c t", p=128)[:, :, g0:g0 + T], in_=pst), reads=PSTR, writes=[("qs", ti)], key="pst")
                    if ti == 0:
                        chk(1.5)
                if kind != "h" or ti == 6:
                    def ev_k(c, ps, pr):
                        if kind != "p":
                            V("act", lambda e: e.activation(out=pst[:, c, :], in_=ps, func=AF.Copy), [pr], [("act", c)])
                        else:
                            kf, kr = tmp()
                            V("dve", lambda e: e.tensor_copy(out=kf, in_=ps), [pr], [kr])
                            V("act", lambda e: e.activation(out=pst[:, c, :], in_=kf, func=AF.Copy), [kr], [("act", c)])
                            P.dma("sp", lambda e: e.dma_start(out=nk[c * 128:(c + 1) * 128, g0:g0 + T], in_=kf), reads=[kr], writes=["nk"], key=kr)
                    linear_fm(hfn, (lambda k: [("h", k)]), 16, lambda cb: win_v[:, :, 1024 + cb * 512:1024 + (cb + 1) * 512], 2, ev_k)
                    if ti == 0:
                        chk(1.55)
                    P.dma(KSQ, lambda e: e.dma_start(out=ks.rearrange("(c p) t -> p c t", p=128)[:, :, g0:g0 + T], in_=pst), reads=PSTR, writes=[("ks", ti)], key="pst")
                    if ti == 0:
                        chk(1.6)
                    vst = actv[:, 8:17, :].rearrange("p a t -> p (a t)")[:, 0:4 * 1040].rearrange("p (b h e) -> p b h e", b=4, h=16)
                    VSTR = [("act", j) for j in range(8, 17)]
                    V("dve", lambda e: e.memset(vst[:, :, :, 64:65], 1.0), [], VSTR)
                    for cb in range(2):
                        wv, wr = wslot()
                        wv3 = wv.rearrange("p (k n) -> p k n", k=16)
                        wdma(wv3, win_v[:, :, 2048 + cb * 512:2048 + (cb + 1) * 512], wr)
                        for tb in range(4):
                            b, ps, pr = bank()
                            for k in range(16):
                                V("pe", lambda e, ps=ps, k=k, tb=tb, wv3=wv3: e.matmul(ps, lhsT=hv[:, k, tb * 128:(tb + 1) * 128], rhs=wv3[:, k, :], start=(k == 0), stop=(k == 15)),
                                  [wr, ("h", k)], [pr])
                            if kind != "p":
                                V("act", lambda e, ps=ps, tb=tb, cb=cb: e.activation(out=vst[:, tb, cb * 8:(cb + 1) * 8, 0:64], in_=ps.rearrange("p (h d) -> p h d", h=8), func=AF.Copy), [pr], VSTR)
                            else:
                                vf, vr = tmp()
                                V("dve", lambda e, vf=vf, ps=ps: e.tensor_copy(out=vf, in_=ps), [pr], [vr])
                                V("act", lambda e, vf=vf, tb=tb, cb=cb: e.activation(out=vst[:, tb, cb * 8:(cb + 1) * 8, 0:64], in_=vf.rearrange("p (h d) -> p h d", h=8), func=AF.Copy), [vr], VSTR)
                                P.dma("sp", lambda e, vf=vf, tb=tb, cb=cb: e.dma_start(out=nv[g0 + tb * 128:g0 + (tb + 1) * 128, cb * 512:(cb + 1) * 512], in_=vf), reads=[vr], writes=["nv"], key=vr)
                    P.dma(KSQ, lambda e: e.dma_start(out=vs.rearrange("(b p) e -> p b e", p=128)[:, g0 // 128:g0 // 128 + 4, :], in_=vst.rearrange("p b h e -> p b (h e)")), reads=VSTR, writes=[("vs", ti)], key="vst")

                if ti == 0:
                    chk(1.7)

                def ev_u(c, ps, pr):
                    V("act", lambda e: e.activation(out=pst[:, c, :], in_=ps, func=AF.Copy), [pr], [("act", c)])
                linear_fm(hfn, (lambda k: [("h", k)]), 16, lambda cb: win_v[:, :, 3072 + cb * 512:3072 + (cb + 1) * 512], 2, ev_u)
                P.dma(KSQ, lambda e: e.dma_start(out=us.rearrange("(c p) t -> p c t", p=128)[:, :, g0:g0 + T], in_=pst), reads=PSTR, writes=[("us", ti)], key="pst")

            P.barrier()
            xv, hv, rs = alloc_common()
            actv = A.bf(44 * T).rearrange("p (j t) -> p j t", j=44)
            for ti in range(10):
                phaseA(ti)
                chk(2 if ti == 0 else (3 if ti == 9 else -1))

            P.barrier()
            A.reset(PERS)
            bbr = A.bf(64 * 128).rearrange("p (c n) -> p c n", c=64)
            bbi = A.bf(64 * 128).rearrange("p (c n) -> p c n", c=64)
            ctr_ = A.bf(64 * 128).rearrange("p (c n) -> p c n", c=64)
            cti = A.bf(64 * 128).rearrange("p (c n) -> p c n", c=64)
            S5BASE = A.off
            sm = {}
            for nm in ["lre", "lim", "dt", "ldr", "ldi", "c", "s", "t", "are", "aim", "mag", "fre", "fim", "nfi", "u1", "u2"]:
                sm[nm] = A.f32(64)
            h0s = A.f32(128)
            P.dma("sp", lambda e: e.dma_start(out=sm["lre"], in_=lamre), writes=["lre"], key="lre")
            P.dma("sp", lambda e: e.dma_start(out=sm["lim"], in_=lamim), writes=["lim"], key="lim")
            P.dma("sp", lambda e: e.dma_start(out=sm["dt"], in_=ldt), writes=["dt"], key="dt")
            P.dma("sp", lambda e: e.dma_start(out=h0s, in_=h0), writes=["h0s"], key="h0s")

            SMR = ["sm", "h0s", "lre", "lim", "dt"]

            def TT(o, a, b, op, eng="dve"):
                V(eng, lambda e: e.tensor_tensor(out=sm[o] if isinstance(o, str) else o, in0=sm[a] if isinstance(a, str) else a, in1=sm[b] if isinstance(b, str) else b, op=op), SMR, ["sm"])

            def TS(o, a, s1, s2, op0, op1=None):
                if op1 is None:
                    V("dve", lambda e: e.tensor_scalar(out=sm[o] if isinstance(o, str) else o, in0=sm[a] if isinstance(a, str) else a, scalar1=s1, scalar2=None, op0=op0), SMR, ["sm"])
                else:
                    V("dve", lambda e: e.tensor_scalar(out=sm[o] if isinstance(o, str) else o, in0=sm[a] if isinstance(a, str) else a, scalar1=s1, scalar2=s2, op0=op0, op1=op1), SMR, ["sm"])

            V("act", lambda e: e.activation(out=sm["dt"], in_=sm["dt"], func=AF.Exp), ["dt"], ["sm"])
            V("dve", lambda e: e.tensor_tensor(out=sm["ldr"], in0=sm["lre"], in1=sm["dt"], op=ALU.mult), ["lre", "sm"], ["sm"])
            V("dve", lambda e: e.tensor_tensor(out=sm["ldi"], in0=sm["lim"], in1=sm["dt"], op=ALU.mult), ["lim", "sm"], ["sm"])
            V("act", lambda e: e.activation(out=rmag, in_=sm["ldr"], func=AF.Exp), ["sm"], ["sm"])
            V("act", lambda e: e.activation(out=sm["s"], in_=sm["ldi"], func=AF.Sin, scale=1.0 / 32), ["sm"], ["sm"])
            V("act", lambda e: e.activation(out=sm["c"], in_=sm["ldi"], func=AF.Sin, bias=hpib[:, 0:1], scale=1.0 / 32), ["sm", "hpib"], ["sm"])

            def dbl(co, so, cin, sin_):
                TT("t", sin_, sin_, ALU.mult)
                V("dve", lambda e: e.scalar_tensor_tensor(out=so, in0=sin_ if not isinstance(sin_, str) else sm[sin_], scalar=2.0, in1=cin if not isinstance(cin, str) else sm[cin], op0=ALU.mult, op1=ALU.mult), ["sm"], ["sm"])
                TS(co, "t", -2.0, 1.0, ALU.mult, ALU.add)

            for i in range(5):
                if i < 4:
                    dbl(sm["u1"], sm["u2"], "c", "s")
                    TT("c", "u1", "u1", ALU.max)
                    TT("s", "u2", "u2", ALU.max)
                else:
                    dbl(CL[:, 0, :], SL[:, 0, :], "c", "s")
            for k in range(1, 10):
                dbl(CL[:, k, :], SL[:, k, :], CL[:, k - 1, :], SL[:, k - 1, :])
            TT("u1", CL[:, 8, :], CL[:, 0, :], ALU.mult); TT("u2", SL[:, 8, :], SL[:, 0, :], ALU.mult); TT(c255, "u1", "u2", ALU.add)
            TT("u1", SL[:, 8, :], CL[:, 0, :], ALU.mult); TT("u2", CL[:, 8, :], SL[:, 0, :], ALU.mult); TT(s255, "u1", "u2", ALU.subtract)
            TT("are", rmag, CL[:, 0, :], ALU.mult); TT("aim", rmag, SL[:, 0, :], ALU.mult)
            TT("u1", "lre", "lre", ALU.mult); TT("u2", "lim", "lim", ALU.mult); TT("mag", "u1", "u2", ALU.add)
            V("dve", lambda e: e.reciprocal(out=sm["mag"], in_=sm["mag"]), ["sm"], ["sm"])
            TS("are", "are", -1.0, None, ALU.add)
            TT("u1", "are", "lre", ALU.mult); TT("u2", "aim", "lim", ALU.mult); TT("fre", "u1", "u2", ALU.add); TT("fre", "fre", "mag", ALU.mult)
            TT("u1", "aim", "lre", ALU.mult); TT("u2", "are", "lim", ALU.mult); TT("fim", "u1", "u2", ALU.subtract); TT("fim", "fim", "mag", ALU.mult)
            TS("nfi", "fim", -1.0, None, ALU.mult)
            h04 = h0s.rearrange("p (d r c) -> p d r c", d=2, r=2)
            for d_ in range(2):
                cs = CL[:, 0, d_ * 32:(d_ + 1) * 32]; ss = SL[:, 0, d_ * 32:(d_ + 1) * 32]
                TT(sm["u1"][:, 0:32], h04[:, d_, 0, :], cs, ALU.mult); TT(sm["u2"][:, 0:32], h04[:, d_, 1, :], ss, ALU.mult)
                TT(carry[:, d_, :, 0], sm["u1"][:, 0:32], sm["u2"][:, 0:32], ALU.subtract)
                TT(sm["u1"][:, 0:32], h04[:, d_, 0, :], ss, ALU.mult); TT(sm["u2"][:, 0:32], h04[:, d_, 1, :], cs, ALU.mult)
                TT(carry[:, d_, :, 1], sm["u1"][:, 0:32], sm["u2"][:, 0:32], ALU.add)
            V("dve", lambda e: e.memset(fin.rearrange("p s d r c -> p (s d r c)"), 0.0), ["sm"], ["fin", "sm"])
            chk(4)
            PRO2 = A.off
            EC = A.f32(8 * T).rearrange("p (c t) -> p c t", c=8)
            ES = A.f32(8 * T).rearrange("p (c t) -> p c t", c=8)
            T1 = A.f32(8 * 256).rearrange("p (c t) -> p c t", c=8)
            T2 = A.f32(8 * 256).rearrange("p (c t) -> p c t", c=8)
            ESn = A.f32(8 * T).rearrange("p (c t) -> p c t", c=8)
            for q in range(8):
                V("dve", lambda e: e.memset(EC[:, :, 0:1], 1.0), ["ec"], ["ec"])
                V("dve", lambda e: e.memset(ES[:, :, 0:1], 0.0), ["ec"], ["ec"])
                for k in range(9):
                    m = 1 << k
                    cm = CL[:, k, q * 8:(q + 1) * 8].unsqueeze(2).to_broadcast([128, 8, m])
                    smm = SL[:, k, q * 8:(q + 1) * 8].unsqueeze(2).to_broadcast([128, 8, m])
                    c_, s_ = EC[:, :, 0:m], ES[:, :, 0:m]
                    t1, t2 = T1[:, :, 0:m], T2[:, :, 0:m]
                    for (a_, b_, c2, d2, op, dst) in ((c_, cm, s_, smm, ALU.subtract, EC[:, :, m:2 * m]), (s_, cm, c_, smm, ALU.add, ES[:, :, m:2 * m])):
                        V("dve", lambda e, a_=a_, b_=b_, t1=t1: e.tensor_tensor(out=t1, in0=a_, in1=b_, op=ALU.mult), ["ec", "sm"], ["t1"])
                        V("dve", lambda e, c2=c2, d2=d2, t2=t2: e.tensor_tensor(out=t2, in0=c2, in1=d2, op=ALU.mult), ["ec", "sm"], ["t2"])
                        V("dve", lambda e, dst=dst, t1=t1, t2=t2, op=op: e.tensor_tensor(out=dst, in0=t1, in1=t2, op=op), ["t1", "t2"], ["ec"])
                rv = rot.rearrange("c p t -> p c t")
                P.dma("sp", lambda e, q=q: e.dma_start(out=rv[:, q * 8:(q + 1) * 8, 0:T], in_=EC), reads=["ec"], writes=["rot"], key="ecst")
                P.dma("sp", lambda e, q=q: e.dma_start(out=rv[:, q * 8:(q + 1) * 8, T:2 * T], in_=ES), reads=["ec"], writes=["rot"], key="ecst")
                V("dve", lambda e: e.tensor_scalar(out=ESn, in0=ES, scalar1=-1.0, scalar2=None, op0=ALU.mult), ["ec"], ["esn"])
                P.dma("sp", lambda e, q=q: e.dma_start(out=rv[:, q * 8:(q + 1) * 8, 2 * T:3 * T], in_=ESn), reads=["esn"], writes=["rot"], key="esnst")
            P.barrier()
            A.reset(PRO2)
            bl = [A.f32(8 * 128).rearrange("p (c n) -> p c n", c=8) for _ in range(2)]
            Dm = [A.f32(128) for _ in range(3)]
            for g8 in range(8):
                P.dma("sp", lambda e, g8=g8: e.dma_start(out=bl[0], in_=bpr[g8 * 8:(g8 + 1) * 8].rearrange("c p n -> p c n")), writes=["bl0"], key="bl0")
                P.dma("sp", lambda e, g8=g8: e.dma_start(out=bl[1], in_=bpi[g8 * 8:(g8 + 1) * 8].rearrange("c p n -> p c n")), writes=["bl1"], key="bl1")
                for i in range(8):
                    dc = g8 * 8 + i
                    for di, nm in enumerate(("fre", "fim", "nfi")):
                        V("dve", lambda e, di=di, nm=nm, dc=dc: e.tensor_scalar(out=Dm[di], in0=csts[:, 0:128], scalar1=sm[nm][:, dc:dc + 1], scalar2=None, op0=ALU.mult), ["csts", "sm"], [("Dm", di)])
                    ba, pa, ra = bank()
                    bb_, pb_, rb = bank()
                    V("pe", lambda e, pa=pa, i=i: e.matmul(pa[:, 0:128], lhsT=bl[0][:, i, :], rhs=Dm[0], start=True, stop=False), ["bl0", ("Dm", 0)], [ra])
                    V("pe", lambda e, pa=pa, i=i: e.matmul(pa[:, 0:128], lhsT=bl[1][:, i, :], rhs=Dm[2], start=False, stop=True), ["bl1", ("Dm", 2)], [ra])
                    V("pe", lambda e, pb_=pb_, i=i: e.matmul(pb_[:, 0:128], lhsT=bl[1][:, i, :], rhs=Dm[0], start=True, stop=False), ["bl1", ("Dm", 0)], [rb])
                    V("pe", lambda e, pb_=pb_, i=i: e.matmul(pb_[:, 0:128], lhsT=bl[0][:, i, :], rhs=Dm[1], start=False, stop=True), ["bl0", ("Dm", 1)], [rb])
                    V("act", lambda e, pa=pa, dc=dc: e.activation(out=bbr[:, dc, :], in_=pa[:, 0:128], func=AF.Copy), [ra], ["bbt"])
                    V("act", lambda e, pb_=pb_, dc=dc: e.activation(out=bbi[:, dc, :], in_=pb_[:, 0:128], func=AF.Copy), [rb], ["bbt"])
            for g8 in range(8):
                P.dma("sp", lambda e, g8=g8: e.dma_start(out=bl[0], in_=cpr[g8 * 8:(g8 + 1) * 8].rearrange("c p n -> p c n")), writes=["bl0"], key="bl0")
                P.dma("sp", lambda e, g8=g8: e.dma_start(out=bl[1], in_=cpi[g8 * 8:(g8 + 1) * 8].rearrange("c p n -> p c n")), writes=["bl1"], key="bl1")
                V("act", lambda e, g8=g8: e.activation(out=ctr_[:, g8 * 8:(g8 + 1) * 8, :], in_=bl[0], func=AF.Copy), ["bl0"], ["bbt"])
                V("act", lambda e, g8=g8: e.activation(out=cti[:, g8 * 8:(g8 + 1) * 8, :], in_=bl[1], func=AF.Identity, scale=-1.0), ["bl1"], ["bbt"])

            chk(5)
            P.barrier()
            A.reset(S5BASE + 16 * 64 + 128)
            ut = A.bf(8 * T).rearrange("p (k t) -> p k t", k=8)
            rts = [A.f32(3 * T) for _ in range(4)]
            tbs = [[A.f32(T) for _ in range(6)] for _ in range(3)]
            prod = [A.bf(16 * T).rearrange("p (j q t) -> p j q t", j=4, q=4) for _ in range(2)]
            ybf = [A.f32(T) for _ in range(2)]
            ybb = [A.bf(T) for _ in range(2)]
            zl = A.f32(128)
            gt = [A.f32(T) for _ in range(3)]
            rtc = [0]

            def s5_tile(g0, dirn, rev, nseg, use_carry, mode, yoff=None, fin_seq=None):
                ln = T // nseg
                P.dma("sp", lambda e: e.dma_start(out=ut, in_=us.rearrange("(c p) t -> p c t", p=128)[:, :, g0:g0 + T]), writes=["ut"], key="ut")
                yasv = yas.rearrange("(c p) t -> p c t", p=128)
                ybsv = ybs.rearrange("(c p) t -> p c t", p=128)
                zl4 = zl[:, 0:32 * nseg * 2].rearrange("p (c s r) -> p c s r", c=32, r=2)

                def seg3(ap):
                    return ap.rearrange("p (s l) -> p s l", s=nseg)

                items = [(uc, j) for uc in range(8) for j in range(4)]
                ctxs = {}

                def st_a0(i):
                    uc, j = items[i]
                    c = 4 * uc + j
                    dc = dirn * 32 + c
                    ri = rtc[0] % 4
                    rtc[0] += 1
                    rt = rts[ri]
                    rr = ("rt", ri)
                    P.dma("sp", lambda e: e.dma_start(out=rt, in_=rot[dc]), reads=["rot"], writes=[rr], key=rr)
                    Cs = rt[:, 0:ln]; Ss = rt[:, T:T + ln]
                    if rev:
                        Cs = Cs[:, ::-1]; Ss = Ss[:, ::-1]
                    Sn = rt[:, 2 * T:2 * T + ln]
                    if rev:
                        Sn = Sn[:, ::-1]
                    C3 = Cs.unsqueeze(1).to_broadcast([128, nseg, ln]); S3 = Ss.unsqueeze(1).to_broadcast([128, nseg, ln])
                    N3 = Sn.unsqueeze(1).to_broadcast([128, nseg, ln])
                    if mode == "B" and j == 0:
                        ysl = uc % 2
                        P.dma("sp", lambda e: e.dma_start(out=ybf[ysl], in_=yasv[:, uc, yoff:yoff + T]), reads=["yas"], writes=[("ybf", ysl)], key=("ybf", ysl))
                    b1, p1, r1 = bank()
                    b2, p2, r2 = bank()
                    V("pe", lambda e: e.matmul(p1, lhsT=bbr[:, dc, :], rhs=ut[:, uc, :], start=True, stop=True), ["bbt", "ut"], [r1])
                    V("pe", lambda e: e.matmul(p2, lhsT=bbi[:, dc, :], rhs=ut[:, uc, :], start=True, stop=True), ["bbt", "ut"], [r2])
                    si = i % 3
                    ctxs[i] = dict(uc=uc, j=j, c=c, dc=dc, rr=rr, C3=C3, S3=S3, N3=N3, t=tbs[si], sx="_%d" % si, p=(p1, r1, p2, r2))

                def st_a1(i):
                    cx = ctxs[i]
                    rr, C3, S3, N3, sx = cx["rr"], cx["C3"], cx["S3"], cx["N3"], cx["sx"]
                    p1, r1, p2, r2 = cx["p"]
                    t1, t2, t3, t4, zsr, zsi = cx["t"]
                    V("dve", lambda e: e.tensor_tensor(out=seg3(t1), in0=seg3(p1), in1=C3, op=ALU.mult), [r1, rr], ["t1" + sx])
                    V("dve", lambda e: e.tensor_tensor(out=seg3(t2), in0=seg3(p2), in1=S3, op=ALU.mult), [r2, rr], ["t2" + sx])
                    V("dve", lambda e: e.tensor_tensor(out=seg3(t3), in0=seg3(p2), in1=C3, op=ALU.mult), [r2, rr], ["t3" + sx])
                    V("dve", lambda e: e.tensor_tensor(out=seg3(t4), in0=seg3(p1), in1=N3, op=ALU.mult), [r1, rr], ["t4" + sx])
                    identf = csts[:, 0:128]
                    bz1, pz1, rz1 = bank()
                    bz2, pz2, rz2 = bank()
                    V("pe", lambda e: e.matmul(pz1, lhsT=identf, rhs=t1, start=True, stop=False), ["csts", "t1" + sx], [rz1])
                    V("pe", lambda e: e.matmul(pz1, lhsT=identf, rhs=t2, start=False, stop=True), ["csts", "t2" + sx], [rz1])
                    V("pe", lambda e: e.matmul(pz2, lhsT=identf, rhs=t3, start=True, stop=False), ["csts", "t3" + sx], [rz2])
                    V("pe", lambda e: e.matmul(pz2, lhsT=identf, rhs=t4, start=False, stop=True), ["csts", "t4" + sx], [rz2])
                    cx["z"] = (pz1, rz1, pz2, rz2)

                def st_b(i):
                    cx = ctxs[i]
                    c, dc, rr, C3, S3, sx = cx["c"], cx["dc"], cx["rr"], cx["C3"], cx["S3"], cx["sx"]
                    t1, t2, t3, t4, zsr, zsi = cx["t"]
                    rbc = rmag[:, dc:dc + 1].to_broadcast([128, ln])
                    for s_ in range(nseg):
                        sl = slice(s_ * ln, (s_ + 1) * ln)
                        pz1, rz1, pz2, rz2 = cx["z"]
                        for (zo, zin, rix, zres, ores) in ((zsr, pz1, 0, rz1, "zsr" + sx), (zsi, pz2, 1, rz2, "zsi" + sx)):
                            o_ = zo[:, sl]; i_ = zin[:, sl]
                            if rev:
                                o_ = o_[:, ::-1]; i_ = i_[:, ::-1]
                            init = carry[:, dirn, c, rix:rix + 1] if use_carry else 0.0
                            V("dve", lambda e, o_=o_, i_=i_, init=init: e.tensor_tensor_scan(out=o_, data0=rbc, data1=i_, initial=init, op0=ALU.mult, op1=ALU.add),
                              [zres, "carry", "sm"], [ores])
                    pos = 0 if rev else ln - 1
                    V("act", lambda e: e.activation(out=zl4[:, c, :, 0], in_=seg3(zsr)[:, :, pos], func=AF.Copy), ["zsr" + sx], ["zl"])
                    V("act", lambda e: e.activation(out=zl4[:, c, :, 1], in_=seg3(zsi)[:, :, pos], func=AF.Copy), ["zsi" + sx], ["zl"])
                    if mode != "N":
                        N3 = cx["N3"]
                        par_, j_ = cx["uc"] % 2, cx["j"]
                        pr_ = ("prod", par_)
                        pd = prod[par_]
                        V("pool", lambda e: e.tensor_tensor(out=seg3(pd[:, j_, 0, :]), in0=seg3(zsr), in1=C3, op=ALU.mult), ["zsr" + sx, rr], [pr_])
                        V("pool", lambda e: e.tensor_tensor(out=seg3(pd[:, j_, 1, :]), in0=seg3(zsi), in1=N3, op=ALU.mult), ["zsi" + sx, rr], [pr_])
                        V("pool", lambda e: e.tensor_tensor(out=seg3(pd[:, j_, 2, :]), in0=seg3(zsr), in1=S3, op=ALU.mult), ["zsr" + sx, rr], [pr_])
                        V("pool", lambda e: e.tensor_tensor(out=seg3(pd[:, j_, 3, :]), in0=seg3(zsi), in1=C3, op=ALU.mult), ["zsi" + sx, rr], [pr_])

                def st_c(i):
                    cx = ctxs.pop(i)
                    uc, j = cx["uc"], cx["j"]
                    par = uc % 2
                    if mode == "N" or j != 3:
                        return
                    pr_ = ("prod", par)
                    pd = prod[par]
                    by, py, ry = bank()
                    n = 0
                    for jj in range(4):
                        dcc = dirn * 32 + 4 * uc + jj
                        for q in range(4):
                            tab = ctr_ if q < 2 else cti
                            V("pe", lambda e, dcc=dcc, jj=jj, q=q, tab=tab, n=n: e.matmul(py, lhsT=tab[:, dcc, :], rhs=pd[:, jj, q, :], start=(n == 0), stop=(n == 15)), ["bbt", pr_], [ry])
                            n += 1
                    ysl = uc % 2
                    if mode == "A":
                        V("act", lambda e: e.activation(out=ybf[ysl], in_=py, func=AF.Copy), [ry], [("ybf", ysl)])
                        P.dma("sp", lambda e: e.dma_start(out=yasv[:, uc, yoff:yoff + T], in_=ybf[ysl]), reads=[("ybf", ysl)], writes=["yas"], key=("ybf", ysl))
                    else:
                        g1, g2, g3 = gt
                        V("dve", lambda e: e.tensor_tensor(out=g1, in0=py, in1=ybf[ysl], op=ALU.add), [ry, ("ybf", ysl)], ["g1"])
                        V("dve", lambda e: e.scalar_tensor_tensor(out=g1, in0=ut[:, uc, :], scalar=dsks[:, uc:uc + 1], in1=g1, op0=ALU.mult, op1=ALU.add), ["ut", "g1", "dsks"], ["g1"])
                        V("act", lambda e: e.activation(out=g2, in_=g1, func=AF.Square), ["g1"], ["g2"])
                        V("dve", lambda e: e.tensor_scalar(out=g2, in0=g2, scalar1=0.044715, scalar2=1.0, op0=ALU.mult, op1=ALU.add), ["g2"], ["g2"])
                        V("pool", lambda e: e.tensor_tensor(out=g2, in0=g2, in1=g1, op=ALU.mult), ["g2", "g1"], ["g2"])
                        V("act", lambda e: e.activation(out=g3, in_=g2, func=AF.Sigmoid, scale=1.5957691216057308), ["g2"], ["g3"])
                        V("pool", lambda e: e.tensor_tensor(out=ybb[ysl], in0=g1, in1=g3, op=ALU.mult), ["g1", "g3"], [("ybb", ysl)])
                        P.dma("sp", lambda e: e.dma_start(out=ybsv[:, uc, yoff:yoff + T], in_=ybb[ysl]), reads=[("ybb", ysl)], writes=["ybs"], key=("ybb", ysl))

                n_it = len(items)
                for i in range(n_it + 3):
                    if i < n_it:
                        st_a0(i)
                    if 0 <= i - 1 < n_it:
                        st_a1(i - 1)
                    if 0 <= i - 2 < n_it:
                        st_b(i - 2)
                    if 0 <= i - 3 < n_it:
                        st_c(i - 3)
                u1 = gt[0][:, 0:64].rearrange("p (c s) -> p c s", c=32)[:, :, 0:nseg]
                u2 = gt[1][:, 0:64].rearrange("p (c s) -> p c s", c=32)[:, :, 0:nseg]
                if use_carry:
                    cc = CL[:, 9, dirn * 32:(dirn + 1) * 32]; ss = SL[:, 9, dirn * 32:(dirn + 1) * 32]
                    outs = (carry[:, dirn, :, 0], carry[:, dirn, :, 1])
                    zre_, zim_ = zl4[:, :, 0, 0], zl4[:, :, 0, 1]
                    u1_, u2_ = gt[0][:, 0:32], gt[1][:, 0:32]
                else:
                    cc = c255[:, dirn * 32:(dirn + 1) * 32].unsqueeze(2).to_broadcast([128, 32, nseg])
                    ss = s255[:, dirn * 32:(dirn + 1) * 32].unsqueeze(2).to_broadcast([128, 32, nseg])
                    fv = fin[:, fin_seq:fin_seq + nseg, dirn]
                    outs = (fv[:, :, 0, :].rearrange("p s c -> p c s"), fv[:, :, 1, :].rearrange("p s c -> p c s"))
                    zre_, zim_ = zl4[:, :, :, 0], zl4[:, :, :, 1]
                    u1_, u2_ = u1, u2
                V("dve", lambda e: e.tensor_tensor(out=u1_, in0=zre_, in1=cc, op=ALU.mult), ["zl", "sm"], ["g1"])
                V("dve", lambda e: e.tensor_tensor(out=u2_, in0=zim_, in1=ss, op=ALU.mult), ["zl", "sm"], ["g2"])
                V("dve", lambda e: e.tensor_tensor(out=outs[0], in0=u1_, in1=u2_, op=ALU.subtract), ["g1", "g2"], ["carry", "fin"])
                V("dve", lambda e: e.tensor_tensor(out=u1_, in0=zre_, in1=ss, op=ALU.mult), ["zl", "sm"], ["g1"])
                V("dve", lambda e: e.tensor_tensor(out=u2_, in0=zim_, in1=cc, op=ALU.mult), ["zl", "sm"], ["g2"])
                V("dve", lambda e: e.tensor_tensor(out=outs[1], in0=u1_, in1=u2_, op=ALU.add), ["g1", "g2"], ["carry", "fin"])

            for pt in range(2):
                s5_tile(pt * T, 0, False, 2, False, "A", yoff=pt * T, fin_seq=2 * pt)
                s5_tile(pt * T, 1, True, 2, False, "B", yoff=pt * T, fin_seq=2 * pt)
            for i in range(4):
                s5_tile(1024 + i * T, 0, False, 1, True, "A", yoff=1024 + i * T)
            for k in (3, 2, 1, 0):
                s5_tile(3072 + k * T, 1, True, 1, True, "N")
            for i in (3, 2, 1, 0):
                s5_tile(1024 + i * T, 1, True, 1, True, "B", yoff=1024 + i * T)
            P.dma("sp", lambda e: e.dma_start(out=ns, in_=fin.rearrange("p s d r c -> p (s d r c)")), reads=["fin"], writes=["ns"], key="nsst")

            chk(6)
            def attention(ti):
                g0 = ti * T
                prompt = ti < 2
                A.reset(PERS)
                Qt = A.bf(8 * T).rearrange("p (k t) -> p k t", k=8)
                nkw = T if prompt else 1024
                Kw = A.bf(8 * nkw).rearrange("p (k t) -> p k t", k=8)
                nvb = nkw // 128
                Vw = A.bf(nvb * 1040).rearrange("p (b h e) -> p b h e", b=nvb, h=16)
                npart = 128 if prompt else 64
                Osb = A.bf(8 * 1024).rearrange("p (b f) -> p b f", b=8)
                pTs = [A.bf(4 * T).rearrange("p (b t) -> p b t", b=4) for _ in range(2)]
                rcs = [A.f32(4) for _ in range(4)]
                rcc = [0]

                def tmp():
                    i = rcc[0] % 4
                    rcc[0] += 1
                    return rcs[i], ("rc", i)
                qv = qs.rearrange("(c p) t -> p c t", p=128)
                kv = ks.rearrange("(c p) t -> p c t", p=128)
                vv = vs.rearrange("(b p) e -> p b e", p=128)
                P.dma("sp", lambda e: e.dma_start(out=Qt, in_=qv[:, :, g0:g0 + T]), writes=["Qt"], key="Qt")
                if prompt:
                    k0 = g0
                else:
                    r0 = 8 * (ti - 2)
                    wr0 = max(r0 - 4, 0)
                    k0 = 1024 + wr0 * 64
                P.dma("sp", lambda e: e.dma_start(out=Kw, in_=kv[:, :, k0:k0 + nkw]), writes=["Kw"], key="Kw")
                P.dma("sp", lambda e: e.dma_start(out=Vw.rearrange("p b h e -> p b (h e)"), in_=vv[:, k0 // 128:k0 // 128 + nvb, :]), writes=["Vw"], key="Vw")
                if not prompt:
                    cK = A.bf(8 * 512).rearrange("p (k t) -> p k t", k=8)
                    cV = A.bf(4 * 1040).rearrange("p (b h e) -> p b h e", b=4, h=16)
                    _CACHE.setdefault("offs", {})[("Tb", ti)] = A.off
                    Tb = A.bf(16 * NSLOT * 64).rearrange("p (h s q) -> p h s q", h=16, s=NSLOT)
                    pTl = [A.bf(320) for _ in range(2)]
                    P.dma("pool", lambda e: e.dma_start(out=cK.rearrange("p k t -> p (k t)").rearrange("p (a b) -> p a b", a=4), in_=ckT.rearrange("p (a b) -> p a b", a=4)), writes=["cK"], key="cK")
                    P.dma("pool", lambda e: e.dma_start(out=Tb[0:64].rearrange("p h s q -> p h (s q)"), in_=rpbx.rearrange("p (h x) -> p h x", h=16)), writes=["Tb"], key="Tb")
                    P.dma("pool", lambda e: e.dma_start(out=Tb[64:128].rearrange("p h s q -> p h (s q)"), in_=rpbx.rearrange("p (h x) -> p h x", h=16)), writes=["Tb"], key="Tb")
                    V("dve", lambda e: e.memset(cV[:, :, :, 64:65], 1.0), [], ["cV"])
                    P.dma("pool", lambda e: e.dma_start(out=cV[:, :, :, 0:64], in_=cvt.rearrange("p (b h d) -> p b h d", b=4, h=16)), writes=["cV"], key="cV")
                if ti == 2:
                    chk(8.01)
                bfv = lambda ps: ps.bitcast(BF16)
                pc = [0]
                if prompt:
                    for hg in range(4):
                        for s_ in range(2):
                            pts = []
                            for hh in range(4):
                                h = hg * 4 + hh
                                ch, pb = h // 2, 64 * (h % 2)
                                b, ps, pr = bank()
                                for kb in range(2):
                                    V("pe", lambda e, ps=ps, kb=kb, ch=ch, pb=pb, s_=s_: e.matmul(ps[:, kb * 256:(kb + 1) * 256], lhsT=Kw[pb:pb + 64, ch, s_ * 256 + kb * 128:s_ * 256 + (kb + 1) * 128],
                                                                                                 rhs=Qt[pb:pb + 64, ch, s_ * 256:(s_ + 1) * 256], start=True, stop=True), ["Kw", "Qt"], [pr])
                                pi = pc[0] % 8
                                pc[0] += 1
                                pT = pTs[pi // 4][:, pi % 4, :]
                                V("act", lambda e, pT=pT, ps=ps: e.activation(out=pT, in_=ps, func=AF.Exp), [pr], [("pT", pi)])
                                pts.append((pT, ("pT", pi)))
                            for qb in range(2):
                                b, ps, pr = bank()
                                o4 = ps.rearrange("p (h e) -> p h e", h=4)
                                for hh in range(4):
                                    h = hg * 4 + hh
                                    pT, ptr = pts[hh]
                                    for kb in range(2):
                                        V("pe", lambda e, o4=o4, hh=hh, pT=pT, kb=kb, qb=qb, h=h, s_=s_: e.matmul(o4[:, hh, 0:65], lhsT=pT[:, kb * 256 + qb * 128:kb * 256 + (qb + 1) * 128],
                                                                                                                  rhs=Vw[:, s_ * 2 + kb, h, :], start=(kb == 0), stop=(kb == 1)), [ptr, "Vw"], [pr])
                                rc, rcr = tmp()
                                rc4 = rc[:, 0:4].unsqueeze(2)
                                V("dve", lambda e, rc4=rc4, o4=o4: e.reciprocal(out=rc4, in_=o4[:, :, 64:65]), [pr], [rcr])
                                V("dve", lambda e, rc4=rc4, o4=o4, s_=s_, qb=qb, hg=hg: e.tensor_tensor(out=Osb[:, s_ * 2 + qb, hg * 256:(hg + 1) * 256].rearrange("p (h d) -> p h d", h=4), in0=o4[:, :, 0:64],
                                                                                                   in1=rc4.to_broadcast([128, 4, 64]), op=ALU.mult), [pr, rcr], ["Osb"])
                    for fc in range(8):
                        b, ps, pr = bank()
                        pbv = bfv(ps)
                        for blk in range(4):
                            V("pe", lambda e, pbv=pbv, blk=blk, fc=fc: e.transpose(out=pbv[:, blk * 128:(blk + 1) * 128], in_=Osb[:, blk, fc * 128:(fc + 1) * 128], identity=identb), ["Osb", "identb"], [pr])
                        V("act", lambda e, pbv=pbv, fc=fc: e.activation(out=ATv[:, fc, :], in_=pbv[:, 0:T], func=AF.Copy), [pr], ["AT"])
                else:
                    sbc = [0]
                    obc = [0]

                    def sbank():
                        bi = sbc[0] % 6
                        sbc[0] += 1
                        return bi, psb[bi][:], ("ps", bi)

                    def obank():
                        bi = 6 + obc[0] % 2
                        obc[0] += 1
                        return bi, psb[bi][:], ("ps", bi)

                    jobs = [(h, half, r4) for h in range(16) for half in range(2) for r4 in range(4)]
                    jctx = {}

                    def S1(n):
                        h, half, r4 = jobs[n]
                        ch, pb = h // 2, 64 * (h % 2)
                        pTc = pTs[h % 2]
                        if half == 0 and r4 == 0:
                            for blk in range(4):
                                b_, ps_, pr_ = sbank()
                                V("pe", lambda e, ps_=ps_, blk=blk: e.matmul(ps_, lhsT=cK[pb:pb + 64, ch, blk * 128:(blk + 1) * 128], rhs=Qt[pb:pb + 64, ch, :], start=True, stop=True), ["cK", "Qt"], [pr_])
                                V("act", lambda e, ps_=ps_, blk=blk: e.activation(out=pTc[:, blk, :], in_=ps_, func=AF.Exp), [pr_], [("pTc", h % 2)])
                        rr_ = half * 4 + r4
                        r_p = r0 + rr_
                        if r_p < 4:
                            sr, npair, edge = 0, 4, True
                        else:
                            sr, npair, edge = (r_p - 4) & ~1, 5, False
                        b, ps, pr = sbank()
                        for pp in range(npair):
                            kt = (sr - wr0 + 2 * pp) * 64
                            V("pe", lambda e, pp=pp, kt=kt: e.matmul(ps[:, pp * 64:(pp + 1) * 64], lhsT=Kw[pb:pb + 64, ch, kt:kt + 128], rhs=Qt[pb:pb + 64, ch, rr_ * 64:(rr_ + 1) * 64],
                                                                     start=True, stop=False), ["Kw", "Qt"], [pr])
                            for jj in range(2):
                                off = sr + 2 * pp + jj - r_p
                                slot = (11 + off + 3) if edge else (off + 5)
                                assert 0 <= slot < NSLOT
                                V("pe", lambda e, pp=pp, jj=jj, slot=slot: e.matmul(ps[:, pp * 64:(pp + 1) * 64], lhsT=selb[pb:pb + 64, jj * 128:(jj + 1) * 128], rhs=Tb[pb:pb + 64, h, slot, :],
                                                                                 start=False, stop=(jj == 1)), ["selb", "Tb"], [pr])
                        li = pc[0] % 2
                        pc[0] += 1
                        pl = pTl[li]
                        V("act", lambda e: e.activation(out=pl[:, 0:npair * 64], in_=ps[:, 0:npair * 64], func=AF.Exp), [pr], [("pTl", li)])
                        jctx[n] = (pl, li, sr, npair, rr_, pTc)

                    ocur = [None]

                    def S2(n):
                        h, half, r4 = jobs[n]
                        pl, li, sr, npair, rr_, pTc = jctx.pop(n)
                        if r4 == 0:
                            bo, pso, pro = obank()
                            ocur[0] = (pso.rearrange("p (r e) -> p r e", r=4), pro)
                        o4, pro = ocur[0]
                        for pp in range(npair):
                            V("pe", lambda e, pp=pp: e.matmul(o4[0:64, r4, 0:65], lhsT=pl[:, pp * 64:(pp + 1) * 64], rhs=Vw[:, (sr - wr0) // 2 + pp, h, :], start=(pp == 0), stop=False),
                              [("pTl", li), "Vw"], [pro])
                        for blk in range(4):
                            V("pe", lambda e, blk=blk: e.matmul(o4[0:64, r4, 0:65], lhsT=pTc[:, blk, rr_ * 64:(rr_ + 1) * 64], rhs=cV[:, blk, h, :], start=False, stop=(blk == 3)),
                              [("pTc", h % 2), "cV"], [pro])
                        if r4 == 3:
                            rc, rcr = tmp()
                            rc4 = rc[0:64, 0:4].unsqueeze(2)
                            V("dve", lambda e: e.reciprocal(out=rc4, in_=o4[0:64, :, 64:65]), [pro], [rcr])
                            V("dve", lambda e: e.tensor_tensor(out=Osb[0:64, half * 4:(half + 1) * 4, h * 64:(h + 1) * 64], in0=o4[0:64, :, 0:64], in1=rc4.to_broadcast([64, 4, 64]), op=ALU.mult),
                              [pro, rcr], ["Osb"])

                    S1(0)
                    for n in range(len(jobs)):
                        if n + 1 < len(jobs):
                            S1(n + 1)
                        S2(n)
                    if ti == 2:
                        chk(8.04)
                    for fc in range(8):
                        b, ps, pr = bank()
                        pbv = bfv(ps)
                        for rr_ in range(8):
                            V("pe", lambda e, pbv=pbv, rr_=rr_, fc=fc: e.transpose(out=pbv[:, rr_ * 64:(rr_ + 1) * 64], in_=Osb[0:64, rr_, fc * 128:(fc + 1) * 128], identity=identb[0:64, 0:64]), ["Osb", "identb"], [pr])
                        V("act", lambda e, pbv=pbv, fc=fc: e.activation(out=ATv[:, fc, :], in_=pbv[:, 0:T], func=AF.Copy), [pr], ["AT"])

            def phaseB(ti):
                g0 = ti * T
                ci = 0 if ti < 2 else 1
                P.barrier()
                attention(ti)
                if ti == 0:
                    chk(6.1)
                if ti == 2:
                    chk(8.1)
                P.barrier()
                xv, hv, rs = alloc_common(5)
                yb = A.bf(8 * T).rearrange("p (k t) -> p k t", k=8)
                yb2 = A.bf(8 * T).rearrange("p (k t) -> p k t", k=8)
                mg_ = A.bf(16 * T).rearrange("p (k t) -> p k t", k=16)
                P.dma("sp", lambda e: e.dma_start(out=yb, in_=ybs.rearrange("(c p) t -> p c t", p=128)[:, :, g0:g0 + T]), writes=["yb"], key="yb")
                P.dma("sp", lambda e: e.dma_start(out=xv, in_=x1s.rearrange("(k p) t -> p k t", p=128)[:, :, g0:g0 + T]), writes=XR, key="xld")
                wv, wr = wslot()
                wv3 = wv.rearrange("p (k n) -> p k n", k=8)
                wdma(wv3, w_glu.rearrange("(k p) n -> p k n", p=128), wr)
                for m in range(8):
                    b, ps, pr = bank()
                    for k in range(8):
                        V("pe", lambda e, ps=ps, k=k, m=m, wv3=wv3: e.matmul(ps, lhsT=wv3[:, k, m * 128:(m + 1) * 128], rhs=yb[:, k, :], start=(k == 0), stop=(k == 7)), [wr, "yb"], [pr])
                    sg, sr_ = tmp()
                    V("act", lambda e, sg=sg, ps=ps: e.activation(out=sg, in_=ps, func=AF.Sigmoid), [pr], [sr_])
                    V("dve", lambda e, sg=sg, m=m: e.tensor_tensor(out=yb2[:, m, :], in0=sg, in1=yb[:, m, :], op=ALU.mult), [sr_, "yb"], ["yb2"])
                norm_mod(xv, hv, rs, 1, ci)
                if ti == 0:
                    chk(6.3)
                upa = w_up_a.rearrange("(k p) n -> p k n", p=128)
                upb = w_up_b.rearrange("(k p) n -> p k n", p=128)
                for mg in range(8):
                    wg, wrg = wslot()
                    wg4 = wg.rearrange("p (g k n) -> p g k n", g=2, k=16)
                    wdma(wg4[:, 0], win_v[:, :, 4096 + mg * 256:4096 + (mg + 1) * 256], wrg)
                    wdma(wg4[:, 1], win_v[:, :, 6144 + mg * 256:6144 + (mg + 1) * 256], wrg)
                    wu, wru = wslot()
                    wu4 = wu[:, 0:4096].rearrange("p (g k n) -> p g k n", g=2, k=8)
                    wdma(wu4[:, 0], upa[:, :, mg * 256:(mg + 1) * 256], wru)
                    wdma(wu4[:, 1], upb[:, :, mg * 256:(mg + 1) * 256], wru)
                    for mm in range(2):
                        m = mg * 2 + mm
                        b1, p1, r1 = bank(); b2, p2, r2 = bank(); b3, p3, r3 = bank(); b4, p4, r4 = bank()
                        for k in range(16):
                            V("pe", lambda e, p1=p1, k=k, mm=mm, wg4=wg4: e.matmul(p1, lhsT=wg4[:, 0, k, mm * 128:(mm + 1) * 128], rhs=hv[:, k, :], start=(k == 0), stop=(k == 15)), [wrg, ("h", k)], [r1])
                        for k in range(8):
                            V("pe", lambda e, p2=p2, k=k, mm=mm, wu4=wu4: e.matmul(p2, lhsT=wu4[:, 0, k, mm * 128:(mm + 1) * 128], rhs=ATv[:, k, :], start=(k == 0), stop=(k == 7)), [wru, "AT"], [r2])
                        for k in range(16):
                            V("pe", lambda e, p3=p3, k=k, mm=mm, wg4=wg4: e.matmul(p3, lhsT=wg4[:, 1, k, mm * 128:(mm + 1) * 128], rhs=hv[:, k, :], start=(k == 0), stop=(k == 15)), [wrg, ("h", k)], [r3])
                        for k in range(8):
                            V("pe", lambda e, p4=p4, k=k, mm=mm, wu4=wu4: e.matmul(p4, lhsT=wu4[:, 1, k, mm * 128:(mm + 1) * 128], rhs=yb2[:, k, :], start=(k == 0), stop=(k == 7)), [wru, "yb2"], [r4])
                        s1, s1r = tmp(); s2, s2r = tmp()
                        V("act", lambda e, s1=s1, p1=p1: e.activation(out=s1, in_=p1, func=AF.Sigmoid), [r1], [s1r])
                        V("dve", lambda e, s1=s1, p2=p2: e.tensor_tensor(out=s1, in0=s1, in1=p2, op=ALU.mult), [s1r, r2], [s1r])
                        V("act", lambda e, s2=s2, p3=p3: e.activation(out=s2, in_=p3, func=AF.Sigmoid), [r3], [s2r])
                        V("dve", lambda e, s2=s2, p4=p4: e.tensor_tensor(out=s2, in0=s2, in1=p4, op=ALU.mult), [s2r, r4], [s2r])
                        V("pool", lambda e, s1=s1, s2=s2, m=m: e.tensor_tensor(out=mg_[:, m, :], in0=s1, in1=s2, op=ALU.add), [s1r, s2r], [("mg", m)])

                if ti == 0:
                    chk(6.5)

                def ev_o(c, ps, pr):
                    V("dve", lambda e: e.scalar_tensor_tensor(out=xv[:, c, :], in0=ps, scalar=mods[:, 80 + c, ci:ci + 1], in1=xv[:, c, :], op0=ALU.mult, op1=ALU.add), [pr, ("x", c), "mods"], [("x", c)])
                linear_fm(lambda k: mg_[:, k, :], (lambda k: [("mg", k)]), 16, lambda cb: w_out.rearrange("(k p) n -> p k n", p=128)[:, :, cb * 512:(cb + 1) * 512], 4, ev_o)
                P.barrier()
                if ti == 0:
                    chk(6.7)
                xv2, hv2, rs2 = alloc_common()
                actv2 = A.bf(44 * T).rearrange("p (j t) -> p j t", j=44)
                norm_mod(xv2, hv2, rs2, 2, ci)
                ffn(1, xv2, hv2, actv2, ci)
                rstd_of(xv2, rs2)
                for k in range(16):
                    V("dve", lambda e, k=k: e.scalar_tensor_tensor(out=xv2[:, k, :], in0=xv2[:, k, :], scalar=fgs[:, k:k + 1], in1=rs2, op0=ALU.mult, op1=ALU.mult), [("x", k), "rs", "fgs"], [("x", k)])
                dst = yp if ti < 2 else ys
                t0 = g0 if ti < 2 else g0 - 1024
                P.dma("sp", lambda e: e.dma_start(out=dst.rearrange("(k p) t -> p k t", p=128)[:, :, t0:t0 + T], in_=xv2), reads=XR, writes=[("y", ti)], key="xst")

            for ti in range(6):
                phaseB(ti)
                chk(7 + ti)
        except _Stop:
            pass
        P.emit()
    return nc, P.stats


def _bias_table(rpb, flip):
    tb = np.full((64, 16, NSLOT, 64), NEG, np.float32)
    kc = np.arange(64)[:, None]
    qc = np.arange(64)[None, :]
    if not flip:
        cok = (kc >= np.clip(qc - 8, 0, 48)) & (kc <= np.clip(qc - 8, 0, 48) + 15)
        dc = kc - qc + 15
    else:
        cok = (kc >= np.clip(qc - 7, 0, 48)) & (kc <= np.clip(qc - 7, 0, 48) + 15)
        dc = qc - kc + 15
    dcc = np.clip(dc, 0, 30)
    for slot in range(NSLOT):
        edge = slot >= 11
        off = (slot - 11 - 3) if edge else (slot - 5)
        dr = (7 - off) if flip else (off + 7)
        if edge:
            valid = 0 <= dr <= 14
        else:
            valid = (-3 <= off <= 4) if flip else (-4 <= off <= 3)
        if not valid:
            continue
        g = rpb[:, dr, :][:, dcc]
        g = np.where(cok[None], g, np.float32(NEG))
        tb[:, :, slot, :] = np.transpose(g, (1, 0, 2))
    return np.ascontiguousarray(tb.reshape(64, -1))


def _pad_bc(w, c_layout):
    out = np.zeros((2, 32, 2, 64, 4, 2, 16), np.float32)
    for c in range(32):
        for gj in range(2):
            g = 2 * c + gj
            if c_layout:
                blk = np.transpose(w[:, g], (0, 2, 1))
            else:
                blk = w[:, g]
            out[:, c, gj, :, c % 4, gj, :] = blk
    return np.ascontiguousarray(out.reshape(64, 128, 128))


_CACHE = {}


def kernel(x_prompt, x_sample, cache_k, cache_v, state_ssm, c, c_ctx, w_ada, b_ada, norm_g,
           ffn_in, ffn_out, w_in, rpb, s5_lam_re, s5_lam_im, s5_log_dt, s5_b_re, s5_b_im,
           s5_c_re, s5_c_im, s5_d, w_glu, w_up_a, w_up_b, w_out, final_g):
    f = np.float32
    A_ = lambda a: np.ascontiguousarray(np.asarray(a, dtype=f))
    x_prompt, x_sample = A_(x_prompt), A_(x_sample)
    if "nc" not in _CACHE:
        _CACHE["nc"] = build_program()
    nc, stats = _CACHE["nc"]
    print("program stats (ops, eng counts, nsems, max dma sem):", stats, flush=True)

    def chunked(v, nchunk):
        return np.ascontiguousarray(np.asarray(v, f).reshape(nchunk, 128).T)

    shared = dict(
        w_ada=A_(w_ada[0]), b_ada=chunked(b_ada[0], 144),
        ng=np.ascontiguousarray(np.concatenate([chunked(norm_g[0, n], 16) for n in range(3)], axis=1)),
        fg=chunked(final_g, 16), ffn_in=A_(ffn_in[0]), ffn_out=A_(ffn_out[0]), w_in=A_(w_in[0]),
        w_glu=A_(w_glu[0]), w_up_a=A_(w_up_a[0]), w_up_b=A_(w_up_b[0]), w_out=A_(w_out[0]),
        dsk=chunked(s5_d[0], 8),
    )
    cstm = np.zeros((128, 384), f)
    cstm[:, 0:128] = np.eye(128, dtype=f)
    cstm[0:64, 128:192] = np.eye(64, dtype=f)
    cstm[0:64, 256 + 64:256 + 128] = np.eye(64, dtype=f)
    cstm[64:128, 128:192] = np.eye(64, dtype=f)
    cstm[64:128, 256 + 64:256 + 128] = np.eye(64, dtype=f)
    shared["cst"] = cstm

    def chan(a):
        a = np.asarray(a, f).reshape(2, 32, 2, 64)
        return np.ascontiguousarray(np.transpose(a, (2, 3, 0, 1)).reshape(128, 64))

    per_orient = {}
    for flip in (False, True):
        sw = (lambda a: np.asarray(a, f)[::-1]) if flip else (lambda a: np.asarray(a, f))
        ldt_full = np.broadcast_to(np.asarray(s5_log_dt[0], f)[:, :, None], (2, 64, 64))
        per_orient[flip] = dict(
            lamre=chan(sw(s5_lam_re[0])), lamim=chan(sw(s5_lam_im[0])), ldt=chan(sw(ldt_full)),
            bpr=_pad_bc(sw(s5_b_re[0]), False), bpi=_pad_bc(sw(s5_b_im[0]), False),
            cpr=_pad_bc(sw(s5_c_re[0]), True), cpi=_pad_bc(sw(s5_c_im[0]), True),
            rpbx=_bias_table(np.asarray(rpb[0], f), flip),
        )

    in_maps = []
    for core in range(8):
        flip = (core % 2 == 1)
        b = core // 2
        xpT = x_prompt[4 * core:4 * core + 4]
        xsq = x_sample[b]
        if flip:
            xpT = xpT[:, ::-1]
            xsq = xsq[::-1]
        m = dict(shared)
        m.update(per_orient[flip])
        m["xp"] = np.ascontiguousarray(xpT.reshape(1024, D).T)
        m["xo"] = np.ascontiguousarray(xsq[0:2048].T)
        m["xh"] = np.ascontiguousarray(xsq[2048:4096].T)
        cc = np.stack([np.asarray(c_ctx, f), np.asarray(c[b], f)], axis=1)
        m["cond"] = np.ascontiguousarray(cc.reshape(16, 128, 2).transpose(1, 0, 2).reshape(128, 32))
        st_ = np.asarray(state_ssm[b, 0], f)
        if flip:
            st_ = st_[::-1]
        st_ = st_.reshape(2, 2, 32, 2, 64)
        m["h0"] = np.ascontiguousarray(np.transpose(st_, (3, 4, 0, 1, 2)).reshape(128, 128))
        ck = np.asarray(cache_k[b, 0], f)
        m["ckT"] = np.ascontiguousarray(np.transpose(ck.reshape(8, 2, 512, 64), (1, 3, 0, 2)).reshape(128, 4096))
        cv = np.asarray(cache_v[b, 0], f)
        m["cvt"] = np.ascontiguousarray(np.transpose(cv.reshape(16, 4, 128, 64), (2, 1, 0, 3)).reshape(128, 4096))
        in_maps.append(m)

    if _CACHE.get("sim_hook") is not None:
        R = _CACHE["sim_hook"](nc, in_maps)
    else:
        res = run_bass_kernel_spmd(nc, in_maps, core_ids=list(range(8)))
        R = res.results
    y_prompt = np.zeros((32, 256, D), f); y_sample = np.zeros((4, 4096, D), f)
    nck = np.zeros((32, 1, 16, 256, 64), f); ncv = np.zeros((32, 1, 16, 256, 64), f)
    nss = np.zeros((32, 1, 2, 2, 64, 64), f)
    for core in range(8):
        flip = (core % 2 == 1)
        b = core // 2
        r = R[core]
        ypc = np.asarray(r["yp"]).T.reshape(4, 256, D)
        ysc = np.asarray(r["ys"]).T
        kk = np.asarray(r["nk"]).reshape(16, 64, 4, 256)
        kk = np.transpose(kk, (2, 0, 3, 1))
        vv = np.asarray(r["nv"]).reshape(4, 256, 16, 64)
        vv = np.transpose(vv, (0, 2, 1, 3))
        s_ = np.asarray(r["ns"]).reshape(2, 64, 4, 2, 2, 32)
        s_ = np.transpose(s_, (2, 3, 4, 5, 0, 1)).reshape(4, 2, 2, 64, 64)
        if flip:
            ypc = ypc[:, ::-1]
            kk = kk[:, :, ::-1]
            vv = vv[:, :, ::-1]
            s_ = s_[:, ::-1]
            y_sample[b, 2048:4096] = ysc[::-1]
        else:
            y_sample[b, 0:2048] = ysc
        y_prompt[4 * core:4 * core + 4] = ypc
        nck[4 * core:4 * core + 4, 0] = kk
        ncv[4 * core:4 * core + 4, 0] = vv
        nss[4 * core:4 * core + 4, 0] = s_
    return (y_prompt, y_sample, nck, ncv, nss)
```

```python
import contextlib
import math
import numpy as np
import concourse.bass as bass
import concourse.mybir as mybir
from concourse.bass_utils import run_bass_kernel_spmd

F32 = mybir.dt.float32
BF16 = mybir.dt.bfloat16
ALU = mybir.AluOpType
AF = mybir.ActivationFunctionType

T = 512
NEG = -30000.0
EPS = 1e-6
D = 2048
DFF = 5632
NSLOT = 22
ARENA = 52800
STAGE = 99
KSQ = 'act'


class Prog:
    ENGS = ("pe", "dve", "act", "pool", "sp")

    def __init__(self, nc):
        self.nc = nc
        self.ops = []
        self.last_w = {}
        self.readers = {}
        self.pending = {}
        self.last_eng = {}
        self.dma_since = []

    def _add(self, eng, fn, reads, writes, dma_key=None):
        idx = len(self.ops)
        deps = set()
        for r in reads:
            w = self.last_w.get(r)
            if w is not None:
                deps.add(w)
        for r in writes:
            w = self.last_w.get(r)
            if w is not None:
                deps.add(w)
            rl = self.readers.get(r)
            if rl:
                deps.update(rl.values())
        rkey = eng if dma_key is None else ("dma", idx)
        for r in reads:
            self.readers.setdefault(r, {})[rkey] = idx
        for r in writes:
            self.last_w[r] = idx
            self.readers[r] = {}
        pb = self.pending.pop(eng, None)
        if pb:
            deps.update(pb)
        deps.discard(idx)
        self.ops.append(dict(eng=eng, fn=fn, deps=deps, dma=dma_key))
        self.last_eng[eng] = idx
        if dma_key is not None:
            self.dma_since.append(idx)
        return idx

    def op(self, eng, fn, reads=(), writes=()):
        return self._add(eng, fn, tuple(reads), tuple(writes))

    def dma(self, queue, fn, reads=(), writes=(), key=None):
        return self._add(queue, fn, tuple(reads), tuple(writes), dma_key=key)

    def barrier(self):
        b = set(self.last_eng.values()) | set(self.dma_since)
        self.dma_since = []
        for e in self.ENGS:
            self.pending[e] = set(b) | self.pending.get(e, set())

    def emit(self):
        nc = self.nc
        ops = self.ops
        needed = set()
        for i, o in enumerate(ops):
            nd = set()
            for d in o["deps"]:
                od = ops[d]
                if od["dma"] is None and od["eng"] == "pe" and o["eng"] == "pe" and o["dma"] is None:
                    continue
                nd.add(d)
            o["deps"] = nd
            needed |= nd
        cnt = {e: 0 for e in self.ENGS}
        dcnt = {}
        for i, o in enumerate(ops):
            if o["dma"] is not None:
                k = o["dma"]
                dcnt[k] = dcnt.get(k, 0) + 16
                o["sig"] = ("d:" + str(k), dcnt[k])
            elif i in needed:
                cnt[o["eng"]] += 1
                o["sig"] = ("e:" + o["eng"], cnt[o["eng"]])
            else:
                o["sig"] = None
        semnames = ["e:" + e for e in self.ENGS] + ["d:" + str(k) for k in dcnt]
        self.stats = (len(ops), dict(cnt), len(semnames), max(dcnt.values()) if dcnt else 0)
        per_eng = {e: [] for e in self.ENGS}
        for i, o in enumerate(ops):
            per_eng[o["eng"]].append(i)
        with contextlib.ExitStack() as st:
            sems = {}
            for j, n in enumerate(semnames):
                sems[n] = st.enter_context(nc.semaphore("s%d" % j))
            block = st.enter_context(nc.Block())

            def make(e):
                def body(eng):
                    known = {}
                    for i in per_eng[e]:
                        o = ops[i]
                        w = {}
                        for d in o["deps"]:
                            s, v = ops[d]["sig"]
                            if v > w.get(s, 0):
                                w[s] = v
                        for s, v in w.items():
                            if known.get(s, 0) >= v:
                                continue
                            known[s] = v
                            eng.wait_ge(sems[s], v)
                        ins = o["fn"](eng)
                        if o["sig"] is not None:
                            s, v = o["sig"]
                            ins.then_inc(sems[s], 16 if o["dma"] is not None else 1)
                    if e == "sp":
                        for k, v in dcnt.items():
                            if known.get("d:" + str(k), 0) < v:
                                eng.wait_ge(sems["d:" + str(k)], v)
                return body

            block.tensor(make("pe"))
            block.vector(make("dve"))
            block.scalar(make("act"))
            block.gpsimd(make("pool"))
            block.sync(make("sp"))


class Arena:
    def __init__(self, ap, size):
        self.ap, self.size, self.off = ap, size, 0

    def f32(self, n):
        v = self.ap[:, self.off:self.off + n]
        self.off += n
        assert self.off <= self.size, ("arena overflow", self.off)
        return v

    def bf(self, n):
        nf = (n + 1) // 2
        return self.f32(nf).bitcast(BF16)

    def reset(self, to):
        self.off = to


def build_program():
    nc = bass.Bass("TRN2", target_bir_lowering=False)

    def din(name, shape, dt=F32):
        return nc.dram_tensor(name, list(shape), dt, kind="ExternalInput").ap()

    def dout(name, shape, dt=F32):
        return nc.dram_tensor(name, list(shape), dt, kind="ExternalOutput").ap()

    def dscr(name, shape, dt=F32):
        return nc.dram_tensor(name, list(shape), dt, kind="Internal").ap()

    xp = din("xp", [D, 1024]); xo = din("xo", [D, 2048]); xh = din("xh", [D, 2048])
    cond = din("cond", [128, 32])
    w_ada = din("w_ada", [D, 9 * D]); b_ada = din("b_ada", [128, 144])
    ng = din("ng", [128, 48]); fg = din("fg", [128, 16])
    ffn_in = din("ffn_in", [2, D, 2 * DFF]); ffn_out = din("ffn_out", [2, DFF, D])
    w_in = din("w_in", [D, 8192]); w_glu = din("w_glu", [1024, 1024])
    w_up_a = din("w_up_a", [1024, D]); w_up_b = din("w_up_b", [1024, D]); w_out = din("w_out", [D, D])
    lamre = din("lamre", [128, 64]); lamim = din("lamim", [128, 64]); ldt = din("ldt", [128, 64])
    bpr = din("bpr", [64, 128, 128]); bpi = din("bpi", [64, 128, 128])
    cpr = din("cpr", [64, 128, 128]); cpi = din("cpi", [64, 128, 128])
    dsk = din("dsk", [128, 8]); h0 = din("h0", [128, 128])
    ckT = din("ckT", [128, 4096]); cvt = din("cvt", [128, 4096])
    rpbx = din("rpbx", [64, 16 * NSLOT * 64])
    cst = din("cst", [128, 128 + 256])

    yp = dout("yp", [D, 1024]); ys = dout("ys", [D, 2048])
    nk = dout("nk", [1024, 1024]); nv = dout("nv", [1024, 1024]); ns = dout("ns", [128, 512])

    x1s = dscr("x1s", [D, 3072]); qs = dscr("qs", [1024, 3072], BF16); ks = dscr("ks", [1024, 3584], BF16)
    vs = dscr("vs", [3584, 1040], BF16); us = dscr("us", [1024, 5120], BF16)
    yas = dscr("yas", [1024, 3072]); ybs = dscr("ybs", [1024, 3072], BF16)
    rot = dscr("rot", [64, 128, 1536])

    P = Prog(nc)
    with contextlib.ExitStack() as st:
        E = st.enter_context
        arena_t = E(nc.sbuf_tensor("arena", [128, ARENA], F32))
        psb = [E(nc.psum_tensor("ps%d" % i, [128, 512], F32)) for i in range(8)]
        A = Arena(arena_t[:], ARENA)
        bankc = [0]

        def bank():
            b = bankc[0] % 8
            bankc[0] += 1
            return b, psb[b][:], ("ps", b)

        mods = A.f32(288).rearrange("p (m c) -> p m c", c=2)
        der = A.f32(160).rearrange("p (n k c) -> p n k c", n=5, k=16)
        ngs = A.f32(48); fgs = A.f32(16)
        onesf = A.bf(128)
        csts = A.f32(384)
        identb = A.bf(128)
        selb = A.bf(256)
        epsb = A.f32(1); hpib = A.f32(1)
        rmag = A.f32(64)
        CL = A.f32(640).rearrange("p (k c) -> p k c", k=10)
        SL = A.f32(640).rearrange("p (k c) -> p k c", k=10)
        c255 = A.f32(64); s255 = A.f32(64)
        dsks = A.f32(8)
        carry = A.f32(128).rearrange("p (d c r) -> p d c r", d=2, r=2)
        fin = A.f32(512).rearrange("p (s d r c) -> p s d r c", s=4, d=2, r=2)
        ATv = A.bf(8 * T).rearrange("p (k t) -> p k t", k=8)
        PERS = A.off

        def V(eng, f, reads, writes):
            return P.op(eng, f, reads, writes)

        class _Stop(Exception):
            pass

        def chk(n):
            if STAGE == n:
                raise _Stop()

        try:
            P.dma("sp", lambda e: e.dma_start(out=csts, in_=cst), writes=["csts"], key="csts")
            P.dma("sp", lambda e: e.dma_start(out=ngs, in_=ng), writes=["ngs"], key="ngs")
            P.dma("sp", lambda e: e.dma_start(out=fgs, in_=fg), writes=["fgs"], key="fgs")
            P.dma("sp", lambda e: e.dma_start(out=dsks, in_=dsk), writes=["dsks"], key="dsks")
            V("dve", lambda e: e.memset(onesf, 1.0), [], ["onesf"])
            V("dve", lambda e: e.memset(epsb, EPS), [], ["epsb"])
            V("dve", lambda e: e.memset(hpib, math.pi / 2), [], ["hpib"])
            V("dve", lambda e: e.tensor_copy(out=identb, in_=csts[:, 0:128]), ["csts"], ["identb"])
            V("dve", lambda e: e.tensor_copy(out=selb, in_=csts[:, 128:384]), ["csts"], ["selb"])

            chk(0.1)
            wctr = [0]
            wslots = [None] * 5
            nws = [4]

            def wslot():
                s = wctr[0] % nws[0]
                wctr[0] += 1
                return wslots[s], ("ws", s)

            def wdma(dst, src, res):
                P.dma("pool", lambda e: e.dma_start(out=dst, in_=src), writes=[res], key=res)

            tmpc = [0]
            tmps = [None] * 8

            def tmp():
                i = tmpc[0] % 8
                tmpc[0] += 1
                return tmps[i], ("tmp", i)

            def alloc_common(nslots=4):
                A.reset(PERS)
                nws[0] = nslots
                xv = A.f32(16 * T).rearrange("p (k t) -> p k t", k=16)
                hv = A.bf(16 * T).rearrange("p (k t) -> p k t", k=16)
                for i in range(nslots):
                    wslots[i] = A.bf(8192)
                for i in range(8):
                    tmps[i] = A.f32(T)
                rs = A.f32(T)
                return xv, hv, rs

            xv, hv, rs = alloc_common()
            cnd = A.f32(32); scb = A.bf(32); bad = A.f32(144)
            P.dma("sp", lambda e: e.dma_start(out=cnd, in_=cond), writes=["cnd"], key="cnd")
            P.dma("sp", lambda e: e.dma_start(out=bad, in_=b_ada), writes=["bad"], key="bad")
            V("act", lambda e: e.activation(out=scb, in_=cnd, func=AF.Silu), ["cnd"], ["scb"])
            chk(0.2)
            scb3 = scb.rearrange("p (k c) -> p k c", c=2)
            wav = w_ada.rearrange("(k p) n -> p k n", p=128)
            for blk in range(36):
                wv, wr = wslot()
                wv3 = wv.rearrange("p (k n) -> p k n", k=16)
                wdma(wv3, wav[:, :, blk * 512:(blk + 1) * 512], wr)
                for mc in range(4):
                    m = blk * 4 + mc
                    b, ps, pr = bank()
                    for k in range(16):
                        V("pe", lambda e, ps=ps, wv3=wv3, k=k, mc=mc: e.matmul(ps[:, 0:2], lhsT=wv3[:, k, mc * 128:(mc + 1) * 128], rhs=scb3[:, k, :], start=(k == 0), stop=(k == 15)),
                          [wr, "scb"], [pr])
                    V("dve", lambda e, ps=ps, m=m: e.tensor_scalar(out=mods[:, m, :], in0=ps[:, 0:2], scalar1=bad[:, m:m + 1], scalar2=None, op0=ALU.add),
                      [pr, "bad"], ["mods"])
                chk(0.3 if blk == 0 else -1)
            chk(0.5)
            for n in range(3):
                V("dve", lambda e, n=n: e.tensor_scalar(out=der[:, n], in0=mods[:, 16 * (3 * n + 1):16 * (3 * n + 2), :], scalar1=1.0, scalar2=None, op0=ALU.add),
                  ["mods"], ["der"])
                V("dve", lambda e, n=n: e.tensor_tensor(out=der[:, n], in0=der[:, n], in1=ngs[:, n * 16:(n + 1) * 16].unsqueeze(2).to_broadcast([128, 16, 2]), op=ALU.mult),
                  ["der", "ngs"], ["der"])
            V("dve", lambda e: e.tensor_scalar(out=der[:, 3], in0=mods[:, 32:48, :], scalar1=0.5, scalar2=None, op0=ALU.mult), ["mods"], ["der"])
            V("dve", lambda e: e.tensor_scalar(out=der[:, 4], in0=mods[:, 128:144, :], scalar1=0.5, scalar2=None, op0=ALU.mult), ["mods"], ["der"])

            chk(1)
            XR = [("x", k) for k in range(16)]

            def rstd_of(xv, rs):
                b, ps, pr = bank()
                for k in range(16):
                    sq, sr = tmp()
                    sqb = sq.bitcast(BF16)[:, 0:T]
                    V("act", lambda e, sqb=sqb, k=k: e.activation(out=sqb, in_=xv[:, k, :], func=AF.Square), [("x", k)], [sr])
                    V("pe", lambda e, ps=ps, sqb=sqb, k=k: e.matmul(ps, lhsT=onesf, rhs=sqb, start=(k == 0), stop=(k == 15)), [sr, "onesf"], [pr])
                V("act", lambda e, ps=ps: e.activation(out=rs, in_=ps, func=AF.Sqrt, bias=epsb[:, 0:1], scale=1.0 / D), [pr, "epsb"], ["rs"])
                V("dve", lambda e: e.reciprocal(out=rs, in_=rs), ["rs"], ["rs"])

            def norm_mod(xv, hv, rs, n, ci):
                rstd_of(xv, rs)
                for k in range(16):
                    tk, tr = tmp()
                    V("dve", lambda e, tk=tk, k=k: e.scalar_tensor_tensor(out=tk, in0=xv[:, k, :], scalar=der[:, n, k, ci:ci + 1], in1=rs, op0=ALU.mult, op1=ALU.mult),
                      [("x", k), "rs", "der"], [tr])
                    V("act", lambda e, tk=tk, k=k: e.activation(out=hv[:, k, :], in_=tk, func=AF.Identity, bias=mods[:, 48 * n + k, ci:ci + 1], scale=1.0),
                      [tr, "mods"], [("h", k)])

            fin_v = ffn_in.rearrange("f (k p) n -> f p k n", p=128)
            fout_v = ffn_out.rearrange("f (j p) n -> f p j n", p=128)

            def ffn(f, xv, hv, actv, ci):
                for jb in range(22):
                    wv, wr = wslot()
                    wv4 = wv.rearrange("p (g k n) -> p g k n", g=2, k=16)
                    wdma(wv4[:, 0], fin_v[f, :, :, jb * 256:(jb + 1) * 256], wr)
                    wdma(wv4[:, 1], fin_v[f, :, :, DFF + jb * 256:DFF + (jb + 1) * 256], wr)
                    for jj in range(2):
                        j = 2 * jb + jj
                        bg, psg, prg = bank()
                        bu, psu, pru = bank()
                        for k in range(16):
                            V("pe", lambda e, psg=psg, k=k, jj=jj, wv4=wv4: e.matmul(psg, lhsT=wv4[:, 0, k, jj * 128:(jj + 1) * 128], rhs=hv[:, k, :], start=(k == 0), stop=(k == 15)),
                              [wr, ("h", k)], [prg])
                        for k in range(16):
                            V("pe", lambda e, psu=psu, k=k, jj=jj, wv4=wv4: e.matmul(psu, lhsT=wv4[:, 1, k, jj * 128:(jj + 1) * 128], rhs=hv[:, k, :], start=(k == 0), stop=(k == 15)),
                              [wr, ("h", k)], [pru])
                        sg, sr = tmp()
                        V("act", lambda e, sg=sg, psg=psg: e.activation(out=sg, in_=psg, func=AF.Silu), [prg], [sr])
                        V("dve", lambda e, sg=sg, psu=psu, j=j: e.tensor_tensor(out=actv[:, j, :], in0=sg, in1=psu, op=ALU.mult), [sr, pru], [("act", j)])
                hgi = 3 if f == 0 else 4
                AR = [("act", j) for j in range(44)]
                for m in range(16):
                    wv, wr = wslot()
                    wv3 = wv[:, 0:44 * 128].rearrange("p (j n) -> p j n", j=44)
                    wdma(wv3, fout_v[f, :, :, m * 128:(m + 1) * 128], wr)
                    b, ps, pr = bank()
                    for j in range(44):
                        V("pe", lambda e, ps=ps, j=j, wv3=wv3: e.matmul(ps, lhsT=wv3[:, j, :], rhs=actv[:, j, :], start=(j == 0), stop=(j == 43)),
                          [wr, ("act", j)], [pr])
                    V("dve", lambda e, ps=ps, m=m: e.scalar_tensor_tensor(out=xv[:, m, :], in0=ps, scalar=der[:, hgi, m, ci:ci + 1], in1=xv[:, m, :], op0=ALU.mult, op1=ALU.add),
                      [pr, ("x", m), "der"], [("x", m)])

            def linear_fm(rhs_fn, rhs_res, KC, wsrc_fn, nblocks, evac):
                for cb in range(nblocks):
                    wv, wr = wslot()
                    wv3 = wv[:, 0:KC * 512].rearrange("p (k n) -> p k n", k=KC)
                    wdma(wv3, wsrc_fn(cb), wr)
                    for mc in range(4):
                        b, ps, pr = bank()
                        for k in range(KC):
                            V("pe", lambda e, ps=ps, k=k, mc=mc, wv3=wv3: e.matmul(ps, lhsT=wv3[:, k, mc * 128:(mc + 1) * 128], rhs=rhs_fn(k), start=(k == 0), stop=(k == KC - 1)),
                              [wr] + rhs_res(k), [pr])
                        evac(cb * 4 + mc, ps, pr)

            win_v = w_in.rearrange("(k p) n -> p k n", p=128)

            def phaseA(ti):
                g0 = ti * T
                kind = "p" if ti < 2 else ("o" if ti < 6 else "h")
                ci = 0 if kind == "p" else 1
                src = {"p": xp, "o": xo, "h": xh}[kind]
                t0 = {"p": g0, "o": g0 - 1024, "h": g0 - 3072}[kind]
                P.dma("sp", lambda e: e.dma_start(out=xv, in_=src.rearrange("(k p) t -> p k t", p=128)[:, :, t0:t0 + T]), writes=XR, key="xld")
                norm_mod(xv, hv, rs, 0, ci)
                if ti == 0:
                    chk(1.1)
                ffn(0, xv, hv, actv, ci)
                if ti == 0:
                    chk(1.3)
                if kind != "h":
                    P.dma("sp", lambda e: e.dma_start(out=x1s.rearrange("(k p) t -> p k t", p=128)[:, :, g0:g0 + T], in_=xv), reads=XR, writes=[("x1s", ti)], key="xst")
                norm_mod(xv, hv, rs, 1, ci)
                if ti == 0:
                    chk(1.4)
                hfn = lambda k: hv[:, k, :]
                pst = actv[:, 0:8, :]
                PSTR = [("act", j) for j in range(8)]
                if kind != "h":
                    def ev_q(c, ps, pr):
                        V("act", lambda e: e.activation(out=pst[:, c, :], in_=ps, func=AF.Identity, scale=0.125), [pr], [("act", c)])
                    linear_fm(hfn, (lambda k: [("h", k)]), 16, lambda cb: win_v[:, :, cb * 512:(cb + 1) * 512], 2, ev_q)
                    P.dma(KSQ, lambda e: e.dma_start(out=qs.rearrange("(c p) t -> p c t", p=128)[:, :, g0:g0 + T], in_=pst), reads=PSTR, writes=[("qs", ti)], key="pst")
                    if ti == 0:
                        chk(1.5)
                if kind != "h" or ti == 6:
                    def ev_k(c, ps, pr):
                        if kind != "p":
                            V("act", lambda e: e.activation(out=pst[:, c, :], in_=ps, func=AF.Copy), [pr], [("act", c)])
                        else:
                            kf, kr = tmp()
                            V("dve", lambda e: e.tensor_copy(out=kf, in_=ps), [pr], [kr])
                            V("act", lambda e: e.activation(out=pst[:, c, :], in_=kf, func=AF.Copy), [kr], [("act", c)])
                            P.dma("sp", lambda e: e.dma_start(out=nk[c * 128:(c + 1) * 128, g0:g0 + T], in_=kf), reads=[kr], writes=["nk"], key=kr)
                    linear_fm(hfn, (lambda k: [("h", k)]), 16, lambda cb: win_v[:, :, 1024 + cb * 512:1024 + (cb + 1) * 512], 2, ev_k)
                    if ti == 0:
                        chk(1.55)
                    P.dma(KSQ, lambda e: e.dma_start(out=ks.rearrange("(c p) t -> p c t", p=128)[:, :, g0:g0 + T], in_=pst), reads=PSTR, writes=[("ks", ti)], key="pst")
                    if ti == 0:
                        chk(1.6)
                    vst = actv[:, 8:17, :].rearrange("p a t -> p (a t)")[:, 0:4 * 1040].rearrange("p (b h e) -> p b h e", b=4, h=16)
                    VSTR = [("act", j) for j in range(8, 17)]
                    V("dve", lambda e: e.memset(vst[:, :, :, 64:65], 1.0), [], VSTR)
                    for cb in range(2):
                        wv, wr = wslot()
                        wv3 = wv.rearrange("p (k n) -> p k n", k=16)
                        wdma(wv3, win_v[:, :, 2048 + cb * 512:2048 + (cb + 1) * 512], wr)
                        for tb in range(4):
                            b, ps, pr = bank()
                            for k in range(16):
                                V("pe", lambda e, ps=ps, k=k, tb=tb, wv3=wv3: e.matmul(ps, lhsT=hv[:, k, tb * 128:(tb + 1) * 128], rhs=wv3[:, k, :], start=(k == 0), stop=(k == 15)),
                                  [wr, ("h", k)], [pr])
                            if kind != "p":
                                V("act", lambda e, ps=ps, tb=tb, cb=cb: e.activation(out=vst[:, tb, cb * 8:(cb + 1) * 8, 0:64], in_=ps.rearrange("p (h d) -> p h d", h=8), func=AF.Copy), [pr], VSTR)
                            else:
                                vf, vr = tmp()
                                V("dve", lambda e, vf=vf, ps=ps: e.tensor_copy(out=vf, in_=ps), [pr], [vr])
                                V("act", lambda e, vf=vf, tb=tb, cb=cb: e.activation(out=vst[:, tb, cb * 8:(cb + 1) * 8, 0:64], in_=vf.rearrange("p (h d) -> p h d", h=8), func=AF.Copy), [vr], VSTR)
                                P.dma("sp", lambda e, vf=vf, tb=tb, cb=cb: e.dma_start(out=nv[g0 + tb * 128:g0 + (tb + 1) * 128, cb * 512:(cb + 1) * 512], in_=vf), reads=[vr], writes=["nv"], key=vr)
                    P.dma(KSQ, lambda e: e.dma_start(out=vs.rearrange("(b p) e -> p b e", p=128)[:, g0 // 128:g0 // 128 + 4, :], in_=vst.rearrange("p b h e -> p b (h e)")), reads=VSTR, writes=[("vs", ti)], key="vst")

                if ti == 0:
                    chk(1.7)

                def ev_u(c, ps, pr):
                    V("act", lambda e: e.activation(out=pst[:, c, :], in_=ps, func=AF.Copy), [pr], [("act", c)])
                linear_fm(hfn, (lambda k: [("h", k)]), 16, lambda cb: win_v[:, :, 3072 + cb * 512:3072 + (cb + 1) * 512], 2, ev_u)
                P.dma(KSQ, lambda e: e.dma_start(out=us.rearrange("(c p) t -> p c t", p=128)[:, :, g0:g0 + T], in_=pst), reads=PSTR, writes=[("us", ti)], key="pst")

            P.barrier()
            xv, hv, rs = alloc_common()
            actv = A.bf(44 * T).rearrange("p (j t) -> p j t", j=44)
            for ti in range(10):
                phaseA(ti)
                chk(2 if ti == 0 else (3 if ti == 9 else -1))

            P.barrier()
            A.reset(PERS)
            bbr = A.bf(64 * 128).rearrange("p (c n) -> p c n", c=64)
            bbi = A.bf(64 * 128).rearrange("p (c n) -> p c n", c=64)
            ctr_ = A.bf(64 * 128).rearrange("p (c n) -> p c n", c=64)
            cti = A.bf(64 * 128).rearrange("p (c n) -> p c n", c=64)
            S5BASE = A.off
            sm = {}
            for nm in ["lre", "lim", "dt", "ldr", "ldi", "c", "s", "t", "are", "aim", "mag", "fre", "fim", "nfi", "u1", "u2"]:
                sm[nm] = A.f32(64)
            h0s = A.f32(128)
            P.dma("sp", lambda e: e.dma_start(out=sm["lre"], in_=lamre), writes=["lre"], key="lre")
            P.dma("sp", lambda e: e.dma_start(out=sm["lim"], in_=lamim), writes=["lim"], key="lim")
            P.dma("sp", lambda e: e.dma_start(out=sm["dt"], in_=ldt), writes=["dt"], key="dt")
            P.dma("sp", lambda e: e.dma_start(out=h0s, in_=h0), writes=["h0s"], key="h0s")

            SMR = ["sm", "h0s", "lre", "lim", "dt"]

            def TT(o, a, b, op, eng="dve"):
                V(eng, lambda e: e.tensor_tensor(out=sm[o] if isinstance(o, str) else o, in0=sm[a] if isinstance(a, str) else a, in1=sm[b] if isinstance(b, str) else b, op=op), SMR, ["sm"])

            def TS(o, a, s1, s2, op0, op1=None):
                if op1 is None:
                    V("dve", lambda e: e.tensor_scalar(out=sm[o] if isinstance(o, str) else o, in0=sm[a] if isinstance(a, str) else a, scalar1=s1, scalar2=None, op0=op0), SMR, ["sm"])
                else:
                    V("dve", lambda e: e.tensor_scalar(out=sm[o] if isinstance(o, str) else o, in0=sm[a] if isinstance(a, str) else a, scalar1=s1, scalar2=s2, op0=op0, op1=op1), SMR, ["sm"])

            V("act", lambda e: e.activation(out=sm["dt"], in_=sm["dt"], func=AF.Exp), ["dt"], ["sm"])
            V("dve", lambda e: e.tensor_tensor(out=sm["ldr"], in0=sm["lre"], in1=sm["dt"], op=ALU.mult), ["lre", "sm"], ["sm"])
            V("dve", lambda e: e.tensor_tensor(out=sm["ldi"], in0=sm["lim"], in1=sm["dt"], op=ALU.mult), ["lim", "sm"], ["sm"])
            V("act", lambda e: e.activation(out=rmag, in_=sm["ldr"], func=AF.Exp), ["sm"], ["sm"])
            V("act", lambda e: e.activation(out=sm["s"], in_=sm["ldi"], func=AF.Sin, scale=1.0 / 32), ["sm"], ["sm"])
            V("act", lambda e: e.activation(out=sm["c"], in_=sm["ldi"], func=AF.Sin, bias=hpib[:, 0:1], scale=1.0 / 32), ["sm", "hpib"], ["sm"])

            def dbl(co, so, cin, sin_):
                TT("t", sin_, sin_, ALU.mult)
                V("dve", lambda e: e.scalar_tensor_tensor(out=so, in0=sin_ if not isinstance(sin_, str) else sm[sin_], scalar=2.0, in1=cin if not isinstance(cin, str) else sm[cin], op0=ALU.mult, op1=ALU.mult), ["sm"], ["sm"])
                TS(co, "t", -2.0, 1.0, ALU.mult, ALU.add)

            for i in range(5):
                if i < 4:
                    dbl(sm["u1"], sm["u2"], "c", "s")
                    TT("c", "u1", "u1", ALU.max)
                    TT("s", "u2", "u2", ALU.max)
                else:
                    dbl(CL[:, 0, :], SL[:, 0, :], "c", "s")
            for k in range(1, 10):
                dbl(CL[:, k, :], SL[:, k, :], CL[:, k - 1, :], SL[:, k - 1, :])
            TT("u1", CL[:, 8, :], CL[:, 0, :], ALU.mult); TT("u2", SL[:, 8, :], SL[:, 0, :], ALU.mult); TT(c255, "u1", "u2", ALU.add)
            TT("u1", SL[:, 8, :], CL[:, 0, :], ALU.mult); TT("u2", CL[:, 8, :], SL[:, 0, :], ALU.mult); TT(s255, "u1", "u2", ALU.subtract)
            TT("are", rmag, CL[:, 0, :], ALU.mult); TT("aim", rmag, SL[:, 0, :], ALU.mult)
            TT("u1", "lre", "lre", ALU.mult); TT("u2", "lim", "lim", ALU.mult); TT("mag", "u1", "u2", ALU.add)
            V("dve", lambda e: e.reciprocal(out=sm["mag"], in_=sm["mag"]), ["sm"], ["sm"])
            TS("are", "are", -1.0, None, ALU.add)
            TT("u1", "are", "lre", ALU.mult); TT("u2", "aim", "lim", ALU.mult); TT("fre", "u1", "u2", ALU.add); TT("fre", "fre", "mag", ALU.mult)
            TT("u1", "aim", "lre", ALU.mult); TT("u2", "are", "lim", ALU.mult); TT("fim", "u1", "u2", ALU.subtract); TT("fim", "fim", "mag", ALU.mult)
            TS("nfi", "fim", -1.0, None, ALU.mult)
            h04 = h0s.rearrange("p (d r c) -> p d r c", d=2, r=2)
            for d_ in range(2):
                cs = CL[:, 0, d_ * 32:(d_ + 1) * 32]; ss = SL[:, 0, d_ * 32:(d_ + 1) * 32]
                TT(sm["u1"][:, 0:32], h04[:, d_, 0, :], cs, ALU.mult); TT(sm["u2"][:, 0:32], h04[:, d_, 1, :], ss, ALU.mult)
                TT(carry[:, d_, :, 0], sm["u1"][:, 0:32], sm["u2"][:, 0:32], ALU.subtract)
                TT(sm["u1"][:, 0:32], h04[:, d_, 0, :], ss, ALU.mult); TT(sm["u2"][:, 0:32], h04[:, d_, 1, :], cs, ALU.mult)
                TT(carry[:, d_, :, 1], sm["u1"][:, 0:32], sm["u2"][:, 0:32], ALU.add)
            V("dve", lambda e: e.memset(fin.rearrange("p s d r c -> p (s d r c)"), 0.0), ["sm"], ["fin", "sm"])
            chk(4)
            PRO2 = A.off
            EC = A.f32(8 * T).rearrange("p (c t) -> p c t", c=8)
            ES = A.f32(8 * T).rearrange("p (c t) -> p c t", c=8)
            T1 = A.f32(8 * 256).rearrange("p (c t) -> p c t", c=8)
            T2 = A.f32(8 * 256).rearrange("p (c t) -> p c t", c=8)
            ESn = A.f32(8 * T).rearrange("p (c t) -> p c t", c=8)
            for q in range(8):
                V("dve", lambda e: e.memset(EC[:, :, 0:1], 1.0), ["ec"], ["ec"])
                V("dve", lambda e: e.memset(ES[:, :, 0:1], 0.0), ["ec"], ["ec"])
                for k in range(9):
                    m = 1 << k
                    cm = CL[:, k, q * 8:(q + 1) * 8].unsqueeze(2).to_broadcast([128, 8, m])
                    smm = SL[:, k, q * 8:(q + 1) * 8].unsqueeze(2).to_broadcast([128, 8, m])
                    c_, s_ = EC[:, :, 0:m], ES[:, :, 0:m]
                    t1, t2 = T1[:, :, 0:m], T2[:, :, 0:m]
                    for (a_, b_, c2, d2, op, dst) in ((c_, cm, s_, smm, ALU.subtract, EC[:, :, m:2 * m]), (s_, cm, c_, smm, ALU.add, ES[:, :, m:2 * m])):
                        V("dve", lambda e, a_=a_, b_=b_, t1=t1: e.tensor_tensor(out=t1, in0=a_, in1=b_, op=ALU.mult), ["ec", "sm"], ["t1"])
                        V("dve", lambda e, c2=c2, d2=d2, t2=t2: e.tensor_tensor(out=t2, in0=c2, in1=d2, op=ALU.mult), ["ec", "sm"], ["t2"])
                        V("dve", lambda e, dst=dst, t1=t1, t2=t2, op=op: e.tensor_tensor(out=dst, in0=t1, in1=t2, op=op), ["t1", "t2"], ["ec"])
                rv = rot.rearrange("c p t -> p c t")
                P.dma("sp", lambda e, q=q: e.dma_start(out=rv[:, q * 8:(q + 1) * 8, 0:T], in_=EC), reads=["ec"], writes=["rot"], key="ecst")
                P.dma("sp", lambda e, q=q: e.dma_start(out=rv[:, q * 8:(q + 1) * 8, T:2 * T], in_=ES), reads=["ec"], writes=["rot"], key="ecst")
                V("dve", lambda e: e.tensor_scalar(out=ESn, in0=ES, scalar1=-1.0, scalar2=None, op0=ALU.mult), ["ec"], ["esn"])
                P.dma("sp", lambda e, q=q: e.dma_start(out=rv[:, q * 8:(q + 1) * 8, 2 * T:3 * T], in_=ESn), reads=["esn"], writes=["rot"], key="esnst")
            P.barrier()
            A.reset(PRO2)
            bl = [A.f32(8 * 128).rearrange("p (c n) -> p c n", c=8) for _ in range(2)]
            Dm = [A.f32(128) for _ in range(3)]
            for g8 in range(8):
                P.dma("sp", lambda e, g8=g8: e.dma_start(out=bl[0], in_=bpr[g8 * 8:(g8 + 1) * 8].rearrange("c p n -> p c n")), writes=["bl0"], key="bl0")
                P.dma("sp", lambda e, g8=g8: e.dma_start(out=bl[1], in_=bpi[g8 * 8:(g8 + 1) * 8].rearrange("c p n -> p c n")), writes=["bl1"], key="bl1")
                for i in range(8):
                    dc = g8 * 8 + i
                    for di, nm in enumerate(("fre", "fim", "nfi")):
                        V("dve", lambda e, di=di, nm=nm, dc=dc: e.tensor_scalar(out=Dm[di], in0=csts[:, 0:128], scalar1=sm[nm][:, dc:dc + 1], scalar2=None, op0=ALU.mult), ["csts", "sm"], [("Dm", di)])
                    ba, pa, ra = bank()
                    bb_, pb_, rb = bank()
                    V("pe", lambda e, pa=pa, i=i: e.matmul(pa[:, 0:128], lhsT=bl[0][:, i, :], rhs=Dm[0], start=True, stop=False), ["bl0", ("Dm", 0)], [ra])
                    V("pe", lambda e, pa=pa, i=i: e.matmul(pa[:, 0:128], lhsT=bl[1][:, i, :], rhs=Dm[2], start=False, stop=True), ["bl1", ("Dm", 2)], [ra])
                    V("pe", lambda e, pb_=pb_, i=i: e.matmul(pb_[:, 0:128], lhsT=bl[1][:, i, :], rhs=Dm[0], start=True, stop=False), ["bl1", ("Dm", 0)], [rb])
                    V("pe", lambda e, pb_=pb_, i=i: e.matmul(pb_[:, 0:128], lhsT=bl[0][:, i, :], rhs=Dm[1], start=False, stop=True), ["bl0", ("Dm", 1)], [rb])
                    V("act", lambda e, pa=pa, dc=dc: e.activation(out=bbr[:, dc, :], in_=pa[:, 0:128], func=AF.Copy), [ra], ["bbt"])
                    V("act", lambda e, pb_=pb_, dc=dc: e.activation(out=bbi[:, dc, :], in_=pb_[:, 0:128], func=AF.Copy), [rb], ["bbt"])
            for g8 in range(8):
                P.dma("sp", lambda e, g8=g8: e.dma_start(out=bl[0], in_=cpr[g8 * 8:(g8 + 1) * 8].rearrange("c p n -> p c n")), writes=["bl0"], key="bl0")
                P.dma("sp", lambda e, g8=g8: e.dma_start(out=bl[1], in_=cpi[g8 * 8:(g8 + 1) * 8].rearrange("c p n -> p c n")), writes=["bl1"], key="bl1")
                V("act", lambda e, g8=g8: e.activation(out=ctr_[:, g8 * 8:(g8 + 1) * 8, :], in_=bl[0], func=AF.Copy), ["bl0"], ["bbt"])
                V("act", lambda e, g8=g8: e.activation(out=cti[:, g8 * 8:(g8 + 1) * 8, :], in_=bl[1], func=AF.Identity, scale=-1.0), ["bl1"], ["bbt"])

            chk(5)
            P.barrier()
            A.reset(S5BASE + 16 * 64 + 128)
            ut = A.bf(8 * T).rearrange("p (k t) -> p k t", k=8)
            rts = [A.f32(3 * T) for _ in range(4)]
            tbs = [[A.f32(T) for _ in range(6)] for _ in range(3)]
            prod = [A.bf(16 * T).rearrange("p (j q t) -> p j q t", j=4, q=4) for _ in range(2)]
            ybf = [A.f32(T) for _ in range(2)]
            ybb = [A.bf(T) for _ in range(2)]
            zl = A.f32(128)
            gt = [A.f32(T) for _ in range(3)]
            rtc = [0]

            def s5_tile(g0, dirn, rev, nseg, use_carry, mode, yoff=None, fin_seq=None):
                ln = T // nseg
                P.dma("sp", lambda e: e.dma_start(out=ut, in_=us.rearrange("(c p) t -> p c t", p=128)[:, :, g0:g0 + T]), writes=["ut"], key="ut")
                yasv = yas.rearrange("(c p) t -> p c t", p=128)
                ybsv = ybs.rearrange("(c p) t -> p c t", p=128)
                zl4 = zl[:, 0:32 * nseg * 2].rearrange("p (c s r) -> p c s r", c=32, r=2)

                def seg3(ap):
                    return ap.rearrange("p (s l) -> p s l", s=nseg)

                items = [(uc, j) for uc in range(8) for j in range(4)]
                ctxs = {}

                def st_a0(i):
                    uc, j = items[i]
                    c = 4 * uc + j
                    dc = dirn * 32 + c
                    ri = rtc[0] % 4
                    rtc[0] += 1
                    rt = rts[ri]
                    rr = ("rt", ri)
                    P.dma("sp", lambda e: e.dma_start(out=rt, in_=rot[dc]), reads=["rot"], writes=[rr], key=rr)
                    Cs = rt[:, 0:ln]; Ss = rt[:, T:T + ln]
                    if rev:
                        Cs = Cs[:, ::-1]; Ss = Ss[:, ::-1]
                    Sn = rt[:, 2 * T:2 * T + ln]
                    if rev:
                        Sn = Sn[:, ::-1]
                    C3 = Cs.unsqueeze(1).to_broadcast([128, nseg, ln]); S3 = Ss.unsqueeze(1).to_broadcast([128, nseg, ln])
                    N3 = Sn.unsqueeze(1).to_broadcast([128, nseg, ln])
                    if mode == "B" and j == 0:
                        ysl = uc % 2
                        P.dma("sp", lambda e: e.dma_start(out=ybf[ysl], in_=yasv[:, uc, yoff:yoff + T]), reads=["yas"], writes=[("ybf", ysl)], key=("ybf", ysl))
                    b1, p1, r1 = bank()
                    b2, p2, r2 = bank()
                    V("pe", lambda e: e.matmul(p1, lhsT=bbr[:, dc, :], rhs=ut[:, uc, :], start=True, stop=True), ["bbt", "ut"], [r1])
                    V("pe", lambda e: e.matmul(p2, lhsT=bbi[:, dc, :], rhs=ut[:, uc, :], start=True, stop=True), ["bbt", "ut"], [r2])
                    si = i % 3
                    ctxs[i] = dict(uc=uc, j=j, c=c, dc=dc, rr=rr, C3=C3, S3=S3, N3=N3, t=tbs[si], sx="_%d" % si, p=(p1, r1, p2, r2))

                def st_a1(i):
                    cx = ctxs[i]
                    rr, C3, S3, N3, sx = cx["rr"], cx["C3"], cx["S3"], cx["N3"], cx["sx"]
                    p1, r1, p2, r2 = cx["p"]
                    t1, t2, t3, t4, zsr, zsi = cx["t"]
                    V("dve", lambda e: e.tensor_tensor(out=seg3(t1), in0=seg3(p1), in1=C3, op=ALU.mult), [r1, rr], ["t1" + sx])
                    V("dve", lambda e: e.tensor_tensor(out=seg3(t2), in0=seg3(p2), in1=S3, op=ALU.mult), [r2, rr], ["t2" + sx])
                    V("dve", lambda e: e.tensor_tensor(out=seg3(t3), in0=seg3(p2), in1=C3, op=ALU.mult), [r2, rr], ["t3" + sx])
                    V("dve", lambda e: e.tensor_tensor(out=seg3(t4), in0=seg3(p1), in1=N3, op=ALU.mult), [r1, rr], ["t4" + sx])
                    identf = csts[:, 0:128]
                    bz1, pz1, rz1 = bank()
                    bz2, pz2, rz2 = bank()
                    V("pe", lambda e: e.matmul(pz1, lhsT=identf, rhs=t1, start=True, stop=False), ["csts", "t1" + sx], [rz1])
                    V("pe", lambda e: e.matmul(pz1, lhsT=identf, rhs=t2, start=False, stop=True), ["csts", "t2" + sx], [rz1])
                    V("pe", lambda e: e.matmul(pz2, lhsT=identf, rhs=t3, start=True, stop=False), ["csts", "t3" + sx], [rz2])
                    V("pe", lambda e: e.matmul(pz2, lhsT=identf, rhs=t4, start=False, stop=True), ["csts", "t4" + sx], [rz2])
                    cx["z"] = (pz1, rz1, pz2, rz2)

                def st_b(i):
                    cx = ctxs[i]
                    c, dc, rr, C3, S3, sx = cx["c"], cx["dc"], cx["rr"], cx["C3"], cx["S3"], cx["sx"]
                    t1, t2, t3, t4, zsr, zsi = cx["t"]
                    rbc = rmag[:, dc:dc + 1].to_broadcast([128, ln])
                    for s_ in range(nseg):
                        sl = slice(s_ * ln, (s_ + 1) * ln)
                        pz1, rz1, pz2, rz2 = cx["z"]
                        for (zo, zin, rix, zres, ores) in ((zsr, pz1, 0, rz1, "zsr" + sx), (zsi, pz2, 1, rz2, "zsi" + sx)):
                            o_ = zo[:, sl]; i_ = zin[:, sl]
                            if rev:
                                o_ = o_[:, ::-1]; i_ = i_[:, ::-1]
                            init = carry[:, dirn, c, rix:rix + 1] if use_carry else 0.0
                            V("dve", lambda e, o_=o_, i_=i_, init=init: e.tensor_tensor_scan(out=o_, data0=rbc, data1=i_, initial=init, op0=ALU.mult, op1=ALU.add),
                              [zres, "carry", "sm"], [ores])
                    pos = 0 if rev else ln - 1
                    V("act", lambda e: e.activation(out=zl4[:, c, :, 0], in_=seg3(zsr)[:, :, pos], func=AF.Copy), ["zsr" + sx], ["zl"])
                    V("act", lambda e: e.activation(out=zl4[:, c, :, 1], in_=seg3(zsi)[:, :, pos], func=AF.Copy), ["zsi" + sx], ["zl"])
                    if mode != "N":
                        N3 = cx["N3"]
                        par_, j_ = cx["uc"] % 2, cx["j"]
                        pr_ = ("prod", par_)
                        pd = prod[par_]
                        V("pool", lambda e: e.tensor_tensor(out=seg3(pd[:, j_, 0, :]), in0=seg3(zsr), in1=C3, op=ALU.mult), ["zsr" + sx, rr], [pr_])
                        V("pool", lambda e: e.tensor_tensor(out=seg3(pd[:, j_, 1, :]), in0=seg3(zsi), in1=N3, op=ALU.mult), ["zsi" + sx, rr], [pr_])
                        V("pool", lambda e: e.tensor_tensor(out=seg3(pd[:, j_, 2, :]), in0=seg3(zsr), in1=S3, op=ALU.mult), ["zsr" + sx, rr], [pr_])
                        V("pool", lambda e: e.tensor_tensor(out=seg3(pd[:, j_, 3, :]), in0=seg3(zsi), in1=C3, op=ALU.mult), ["zsi" + sx, rr], [pr_])

                def st_c(i):
                    cx = ctxs.pop(i)
                    uc, j = cx["uc"], cx["j"]
                    par = uc % 2
                    if mode == "N" or j != 3:
                        return
                    pr_ = ("prod", par)
                    pd = prod[par]
                    by, py, ry = bank()
                    n = 0
                    for jj in range(4):
                        dcc = dirn * 32 + 4 * uc + jj
                        for q in range(4):
                            tab = ctr_ if q < 2 else cti
                            V("pe", lambda e, dcc=dcc, jj=jj, q=q, tab=tab, n=n: e.matmul(py, lhsT=tab[:, dcc, :], rhs=pd[:, jj, q, :], start=(n == 0), stop=(n == 15)), ["bbt", pr_], [ry])
                            n += 1
                    ysl = uc % 2
                    if mode == "A":
                        V("act", lambda e: e.activation(out=ybf[ysl], in_=py, func=AF.Copy), [ry], [("ybf", ysl)])
                        P.dma("sp", lambda e: e.dma_start(out=yasv[:, uc, yoff:yoff + T], in_=ybf[ysl]), reads=[("ybf", ysl)], writes=["yas"], key=("ybf", ysl))
                    else:
                        g1, g2, g3 = gt
                        V("dve", lambda e: e.tensor_tensor(out=g1, in0=py, in1=ybf[ysl], op=ALU.add), [ry, ("ybf", ysl)], ["g1"])
                        V("dve", lambda e: e.scalar_tensor_tensor(out=g1, in0=ut[:, uc, :], scalar=dsks[:, uc:uc + 1], in1=g1, op0=ALU.mult, op1=ALU.add), ["ut", "g1", "dsks"], ["g1"])
                        V("act", lambda e: e.activation(out=g2, in_=g1, func=AF.Square), ["g1"], ["g2"])
                        V("dve", lambda e: e.tensor_scalar(out=g2, in0=g2, scalar1=0.044715, scalar2=1.0, op0=ALU.mult, op1=ALU.add), ["g2"], ["g2"])
                        V("pool", lambda e: e.tensor_tensor(out=g2, in0=g2, in1=g1, op=ALU.mult), ["g2", "g1"], ["g2"])
                        V("act", lambda e: e.activation(out=g3, in_=g2, func=AF.Sigmoid, scale=1.5957691216057308), ["g2"], ["g3"])
                        V("pool", lambda e: e.tensor_tensor(out=ybb[ysl], in0=g1, in1=g3, op=ALU.mult), ["g1", "g3"], [("ybb", ysl)])
                        P.dma("sp", lambda e: e.dma_start(out=ybsv[:, uc, yoff:yoff + T], in_=ybb[ysl]), reads=[("ybb", ysl)], writes=["ybs"], key=("ybb", ysl))

                n_it = len(items)
                for i in range(n_it + 3):
                    if i < n_it:
                        st_a0(i)
                    if 0 <= i - 1 < n_it:
                        st_a1(i - 1)
                    if 0 <= i - 2 < n_it:
                        st_b(i - 2)
                    if 0 <= i - 3 < n_it:
                        st_c(i - 3)
                u1 = gt[0][:, 0:64].rearrange("p (c s) -> p c s", c=32)[:, :, 0:nseg]
                u2 = gt[1][:, 0:64].rearrange("p (c s) -> p c s", c=32)[:, :, 0:nseg]
                if use_carry:
                    cc = CL[:, 9, dirn * 32:(dirn + 1) * 32]; ss = SL[:, 9, dirn * 32:(dirn + 1) * 32]
                    outs = (carry[:, dirn, :, 0], carry[:, dirn, :, 1])
                    zre_, zim_ = zl4[:, :, 0, 0], zl4[:, :, 0, 1]
                    u1_, u2_ = gt[0][:, 0:32], gt[1][:, 0:32]
                else:
                    cc = c255[:, dirn * 32:(dirn + 1) * 32].unsqueeze(2).to_broadcast([128, 32, nseg])
                    ss = s255[:, dirn * 32:(dirn + 1) * 32].unsqueeze(2).to_broadcast([128, 32, nseg])
                    fv = fin[:, fin_seq:fin_seq + nseg, dirn]
                    outs = (fv[:, :, 0, :].rearrange("p s c -> p c s"), fv[:, :, 1, :].rearrange("p s c -> p c s"))
                    zre_, zim_ = zl4[:, :, :, 0], zl4[:, :, :, 1]
                    u1_, u2_ = u1, u2
                V("dve", lambda e: e.tensor_tensor(out=u1_, in0=zre_, in1=cc, op=ALU.mult), ["zl", "sm"], ["g1"])
                V("dve", lambda e: e.tensor_tensor(out=u2_, in0=zim_, in1=ss, op=ALU.mult), ["zl", "sm"], ["g2"])
                V("dve", lambda e: e.tensor_tensor(out=outs[0], in0=u1_, in1=u2_, op=ALU.subtract), ["g1", "g2"], ["carry", "fin"])
                V("dve", lambda e: e.tensor_tensor(out=u1_, in0=zre_, in1=ss, op=ALU.mult), ["zl", "sm"], ["g1"])
                V("dve", lambda e: e.tensor_tensor(out=u2_, in0=zim_, in1=cc, op=ALU.mult), ["zl", "sm"], ["g2"])
                V("dve", lambda e: e.tensor_tensor(out=outs[1], in0=u1_, in1=u2_, op=ALU.add), ["g1", "g2"], ["carry", "fin"])

            for pt in range(2):
                s5_tile(pt * T, 0, False, 2, False, "A", yoff=pt * T, fin_seq=2 * pt)
                s5_tile(pt * T, 1, True, 2, False, "B", yoff=pt * T, fin_seq=2 * pt)
            for i in range(4):
                s5_tile(1024 + i * T, 0, False, 1, True, "A", yoff=1024 + i * T)
            for k in (3, 2, 1, 0):
                s5_tile(3072 + k * T, 1, True, 1, True, "N")
            for i in (3, 2, 1, 0):
                s5_tile(1024 + i * T, 1, True, 1, True, "B", yoff=1024 + i * T)
            P.dma("sp", lambda e: e.dma_start(out=ns, in_=fin.rearrange("p s d r c -> p (s d r c)")), reads=["fin"], writes=["ns"], key="nsst")

            chk(6)
            def attention(ti):
                g0 = ti * T
                prompt = ti < 2
                A.reset(PERS)
                Qt = A.bf(8 * T).rearrange("p (k t) -> p k t", k=8)
                nkw = T if prompt else 1024
                Kw = A.bf(8 * nkw).rearrange("p (k t) -> p k t", k=8)
                nvb = nkw // 128
                Vw = A.bf(nvb * 1040).rearrange("p (b h e) -> p b h e", b=nvb, h=16)
                npart = 128 if prompt else 64
                Osb = A.bf(8 * 1024).rearrange("p (b f) -> p b f", b=8)
                pTs = [A.bf(4 * T).rearrange("p (b t) -> p b t", b=4) for _ in range(2)]
                rcs = [A.f32(4) for _ in range(4)]
                rcc = [0]

                def tmp():
                    i = rcc[0] % 4
                    rcc[0] += 1
                    return rcs[i], ("rc", i)
                qv = qs.rearrange("(c p) t -> p c t", p=128)
                kv = ks.rearrange("(c p) t -> p c t", p=128)
                vv = vs.rearrange("(b p) e -> p b e", p=128)
                P.dma("sp", lambda e: e.dma_start(out=Qt, in_=qv[:, :, g0:g0 + T]), writes=["Qt"], key="Qt")
                if prompt:
                    k0 = g0
                else:
                    r0 = 8 * (ti - 2)
                    wr0 = max(r0 - 4, 0)
                    k0 = 1024 + wr0 * 64
                P.dma("sp", lambda e: e.dma_start(out=Kw, in_=kv[:, :, k0:k0 + nkw]), writes=["Kw"], key="Kw")
                P.dma("sp", lambda e: e.dma_start(out=Vw.rearrange("p b h e -> p b (h e)"), in_=vv[:, k0 // 128:k0 // 128 + nvb, :]), writes=["Vw"], key="Vw")
                if not prompt:
                    cK = A.bf(8 * 512).rearrange("p (k t) -> p k t", k=8)
                    cV = A.bf(4 * 1040).rearrange("p (b h e) -> p b h e", b=4, h=16)
                    _CACHE.setdefault("offs", {})[("Tb", ti)] = A.off
                    Tb = A.bf(16 * NSLOT * 64).rearrange("p (h s q) -> p h s q", h=16, s=NSLOT)
                    pTl = [A.bf(320) for _ in range(2)]
                    P.dma("pool", lambda e: e.dma_start(out=cK.rearrange("p k t -> p (k t)").rearrange("p (a b) -> p a b", a=4), in_=ckT.rearrange("p (a b) -> p a b", a=4)), writes=["cK"], key="cK")
                    P.dma("pool", lambda e: e.dma_start(out=Tb[0:64].rearrange("p h s q -> p h (s q)"), in_=rpbx.rearrange("p (h x) -> p h x", h=16)), writes=["Tb"], key="Tb")
                    P.dma("pool", lambda e: e.dma_start(out=Tb[64:128].rearrange("p h s q -> p h (s q)"), in_=rpbx.rearrange("p (h x) -> p h x", h=16)), writes=["Tb"], key="Tb")
                    V("dve", lambda e: e.memset(cV[:, :, :, 64:65], 1.0), [], ["cV"])
                    P.dma("pool", lambda e: e.dma_start(out=cV[:, :, :, 0:64], in_=cvt.rearrange("p (b h d) -> p b h d", b=4, h=16)), writes=["cV"], key="cV")
                if ti == 2:
                    chk(8.01)
                bfv = lambda ps: ps.bitcast(BF16)
                pc = [0]
                if prompt:
                    for hg in range(4):
                        for s_ in range(2):
                            pts = []
                            for hh in range(4):
                                h = hg * 4 + hh
                                ch, pb = h // 2, 64 * (h % 2)
                                b, ps, pr = bank()
                                for kb in range(2):
                                    V("pe", lambda e, ps=ps, kb=kb, ch=ch, pb=pb, s_=s_: e.matmul(ps[:, kb * 256:(kb + 1) * 256], lhsT=Kw[pb:pb + 64, ch, s_ * 256 + kb * 128:s_ * 256 + (kb + 1) * 128],
                                                                                                 rhs=Qt[pb:pb + 64, ch, s_ * 256:(s_ + 1) * 256], start=True, stop=True), ["Kw", "Qt"], [pr])
                                pi = pc[0] % 8
                                pc[0] += 1
                                pT = pTs[pi // 4][:, pi % 4, :]
                                V("act", lambda e, pT=pT, ps=ps: e.activation(out=pT, in_=ps, func=AF.Exp), [pr], [("pT", pi)])
                                pts.append((pT, ("pT", pi)))
                            for qb in range(2):
                                b, ps, pr = bank()
                                o4 = ps.rearrange("p (h e) -> p h e", h=4)
                                for hh in range(4):
                                    h = hg * 4 + hh
                                    pT, ptr = pts[hh]
                                    for kb in range(2):
                                        V("pe", lambda e, o4=o4, hh=hh, pT=pT, kb=kb, qb=qb, h=h, s_=s_: e.matmul(o4[:, hh, 0:65], lhsT=pT[:, kb * 256 + qb * 128:kb * 256 + (qb + 1) * 128],
                                                                                                                  rhs=Vw[:, s_ * 2 + kb, h, :], start=(kb == 0), stop=(kb == 1)), [ptr, "Vw"], [pr])
                                rc, rcr = tmp()
                                rc4 = rc[:, 0:4].unsqueeze(2)
                                V("dve", lambda e, rc4=rc4, o4=o4: e.reciprocal(out=rc4, in_=o4[:, :, 64:65]), [pr], [rcr])
                                V("dve", lambda e, rc4=rc4, o4=o4, s_=s_, qb=qb, hg=hg: e.tensor_tensor(out=Osb[:, s_ * 2 + qb, hg * 256:(hg + 1) * 256].rearrange("p (h d) -> p h d", h=4), in0=o4[:, :, 0:64],
                                                                                                   in1=rc4.to_broadcast([128, 4, 64]), op=ALU.mult), [pr, rcr], ["Osb"])
                    for fc in range(8):
                        b, ps, pr = bank()
                        pbv = bfv(ps)
                        for blk in range(4):
                            V("pe", lambda e, pbv=pbv, blk=blk, fc=fc: e.transpose(out=pbv[:, blk * 128:(blk + 1) * 128], in_=Osb[:, blk, fc * 128:(fc + 1) * 128], identity=identb), ["Osb", "identb"], [pr])
                        V("act", lambda e, pbv=pbv, fc=fc: e.activation(out=ATv[:, fc, :], in_=pbv[:, 0:T], func=AF.Copy), [pr], ["AT"])
                else:
                    sbc = [0]
                    obc = [0]

                    def sbank():
                        bi = sbc[0] % 6
                        sbc[0] += 1
                        return bi, psb[bi][:], ("ps", bi)

                    def obank():
                        bi = 6 + obc[0] % 2
                        obc[0] += 1
                        return bi, psb[bi][:], ("ps", bi)

                    jobs = [(h, half, r4) for h in range(16) for half in range(2) for r4 in range(4)]
                    jctx = {}

                    def S1(n):
                        h, half, r4 = jobs[n]
                        ch, pb = h // 2, 64 * (h % 2)
                        pTc = pTs[h % 2]
                        if half == 0 and r4 == 0:
                            for blk in range(4):
                                b_, ps_, pr_ = sbank()
                                V("pe", lambda e, ps_=ps_, blk=blk: e.matmul(ps_, lhsT=cK[pb:pb + 64, ch, blk * 128:(blk + 1) * 128], rhs=Qt[pb:pb + 64, ch, :], start=True, stop=True), ["cK", "Qt"], [pr_])
                                V("act", lambda e, ps_=ps_, blk=blk: e.activation(out=pTc[:, blk, :], in_=ps_, func=AF.Exp), [pr_], [("pTc", h % 2)])
                        rr_ = half * 4 + r4
                        r_p = r0 + rr_
                        if r_p < 4:
                            sr, npair, edge = 0, 4, True
                        else:
                            sr, npair, edge = (r_p - 4) & ~1, 5, False
                        b, ps, pr = sbank()
                        for pp in range(npair):
                            kt = (sr - wr0 + 2 * pp) * 64
                            V("pe", lambda e, pp=pp, kt=kt: e.matmul(ps[:, pp * 64:(pp + 1) * 64], lhsT=Kw[pb:pb + 64, ch, kt:kt + 128], rhs=Qt[pb:pb + 64, ch, rr_ * 64:(rr_ + 1) * 64],
                                                                     start=True, stop=False), ["Kw", "Qt"], [pr])
                            for jj in range(2):
                                off = sr + 2 * pp + jj - r_p
                                slot = (11 + off + 3) if edge else (off + 5)
                                assert 0 <= slot < NSLOT
                                V("pe", lambda e, pp=pp, jj=jj, slot=slot: e.matmul(ps[:, pp * 64:(pp + 1) * 64], lhsT=selb[pb:pb + 64, jj * 128:(jj + 1) * 128], rhs=Tb[pb:pb + 64, h, slot, :],
                                                                                 start=False, stop=(jj == 1)), ["selb", "Tb"], [pr])
                        li = pc[0] % 2
                        pc[0] += 1
                        pl = pTl[li]
                        V("act", lambda e: e.activation(out=pl[:, 0:npair * 64], in_=ps[:, 0:npair * 64], func=AF.Exp), [pr], [("pTl", li)])
                        jctx[n] = (pl, li, sr, npair, rr_, pTc)

                    ocur = [None]

                    def S2(n):
                        h, half, r4 = jobs[n]
                        pl, li, sr, npair, rr_, pTc = jctx.pop(n)
                        if r4 == 0:
                            bo, pso, pro = obank()
                            ocur[0] = (pso.rearrange("p (r e) -> p r e", r=4), pro)
                        o4, pro = ocur[0]
                        for pp in range(npair):
                            V("pe", lambda e, pp=pp: e.matmul(o4[0:64, r4, 0:65], lhsT=pl[:, pp * 64:(pp + 1) * 64], rhs=Vw[:, (sr - wr0) // 2 + pp, h, :], start=(pp == 0), stop=False),
                              [("pTl", li), "Vw"], [pro])
                        for blk in range(4):
                            V("pe", lambda e, blk=blk: e.matmul(o4[0:64, r4, 0:65], lhsT=pTc[:, blk, rr_ * 64:(rr_ + 1) * 64], rhs=cV[:, blk, h, :], start=False, stop=(blk == 3)),
                              [("pTc", h % 2), "cV"], [pro])
                        if r4 == 3:
                            rc, rcr = tmp()
                            rc4 = rc[0:64, 0:4].unsqueeze(2)
                            V("dve", lambda e: e.reciprocal(out=rc4, in_=o4[0:64, :, 64:65]), [pro], [rcr])
                            V("dve", lambda e: e.tensor_tensor(out=Osb[0:64, half * 4:(half + 1) * 4, h * 64:(h + 1) * 64], in0=o4[0:64, :, 0:64], in1=rc4.to_broadcast([64, 4, 64]), op=ALU.mult),
                              [pro, rcr], ["Osb"])

                    S1(0)
                    for n in range(len(jobs)):
                        if n + 1 < len(jobs):
                            S1(n + 1)
                        S2(n)
                    if ti == 2:
                        chk(8.04)
                    for fc in range(8):
                        b, ps, pr = bank()
                        pbv = bfv(ps)
                        for rr_ in range(8):
                            V("pe", lambda e, pbv=pbv, rr_=rr_, fc=fc: e.transpose(out=pbv[:, rr_ * 64:(rr_ + 1) * 64], in_=Osb[0:64, rr_, fc * 128:(fc + 1) * 128], identity=identb[0:64, 0:64]), ["Osb", "identb"], [pr])
                        V("act", lambda e, pbv=pbv, fc=fc: e.activation(out=ATv[:, fc, :], in_=pbv[:, 0:T], func=AF.Copy), [pr], ["AT"])

            def phaseB(ti):
                g0 = ti * T
                ci = 0 if ti < 2 else 1
                P.barrier()
                attention(ti)
                if ti == 0:
                    chk(6.1)
                if ti == 2:
                    chk(8.1)
                P.barrier()
                xv, hv, rs = alloc_common(5)
                yb = A.bf(8 * T).rearrange("p (k t) -> p k t", k=8)
                yb2 = A.bf(8 * T).rearrange("p (k t) -> p k t", k=8)
                mg_ = A.bf(16 * T).rearrange("p (k t) -> p k t", k=16)
                P.dma("sp", lambda e: e.dma_start(out=yb, in_=ybs.rearrange("(c p) t -> p c t", p=128)[:, :, g0:g0 + T]), writes=["yb"], key="yb")
                P.dma("sp", lambda e: e.dma_start(out=xv, in_=x1s.rearrange("(k p) t -> p k t", p=128)[:, :, g0:g0 + T]), writes=XR, key="xld")
                wv, wr = wslot()
                wv3 = wv.rearrange("p (k n) -> p k n", k=8)
                wdma(wv3, w_glu.rearrange("(k p) n -> p k n", p=128), wr)
                for m in range(8):
                    b, ps, pr = bank()
                    for k in range(8):
                        V("pe", lambda e, ps=ps, k=k, m=m, wv3=wv3: e.matmul(ps, lhsT=wv3[:, k, m * 128:(m + 1) * 128], rhs=yb[:, k, :], start=(k == 0), stop=(k == 7)), [wr, "yb"], [pr])
                    sg, sr_ = tmp()
                    V("act", lambda e, sg=sg, ps=ps: e.activation(out=sg, in_=ps, func=AF.Sigmoid), [pr], [sr_])
                    V("dve", lambda e, sg=sg, m=m: e.tensor_tensor(out=yb2[:, m, :], in0=sg, in1=yb[:, m, :], op=ALU.mult), [sr_, "yb"], ["yb2"])
                norm_mod(xv, hv, rs, 1, ci)
                if ti == 0:
                    chk(6.3)
                upa = w_up_a.rearrange("(k p) n -> p k n", p=128)
                upb = w_up_b.rearrange("(k p) n -> p k n", p=128)
                for mg in range(8):
                    wg, wrg = wslot()
                    wg4 = wg.rearrange("p (g k n) -> p g k n", g=2, k=16)
                    wdma(wg4[:, 0], win_v[:, :, 4096 + mg * 256:4096 + (mg + 1) * 256], wrg)
                    wdma(wg4[:, 1], win_v[:, :, 6144 + mg * 256:6144 + (mg + 1) * 256], wrg)
                    wu, wru = wslot()
                    wu4 = wu[:, 0:4096].rearrange("p (g k n) -> p g k n", g=2, k=8)
                    wdma(wu4[:, 0], upa[:, :, mg * 256:(mg + 1) * 256], wru)
                    wdma(wu4[:, 1], upb[:, :, mg * 256:(mg + 1) * 256], wru)
                    for mm in range(2):
                        m = mg * 2 + mm
                        b1, p1, r1 = bank(); b2, p2, r2 = bank(); b3, p3, r3 = bank(); b4, p4, r4 = bank()
                        for k in range(16):
                            V("pe", lambda e, p1=p1, k=k, mm=mm, wg4=wg4: e.matmul(p1, lhsT=wg4[:, 0, k, mm * 128:(mm + 1) * 128], rhs=hv[:, k, :], start=(k == 0), stop=(k == 15)), [wrg, ("h", k)], [r1])
                        for k in range(8):
                            V("pe", lambda e, p2=p2, k=k, mm=mm, wu4=wu4: e.matmul(p2, lhsT=wu4[:, 0, k, mm * 128:(mm + 1) * 128], rhs=ATv[:, k, :], start=(k == 0), stop=(k == 7)), [wru, "AT"], [r2])
                        for k in range(16):
                            V("pe", lambda e, p3=p3, k=k, mm=mm, wg4=wg4: e.matmul(p3, lhsT=wg4[:, 1, k, mm * 128:(mm + 1) * 128], rhs=hv[:, k, :], start=(k == 0), stop=(k == 15)), [wrg, ("h", k)], [r3])
                        for k in range(8):
                            V("pe", lambda e, p4=p4, k=k, mm=mm, wu4=wu4: e.matmul(p4, lhsT=wu4[:, 1, k, mm * 128:(mm + 1) * 128], rhs=yb2[:, k, :], start=(k == 0), stop=(k == 7)), [wru, "yb2"], [r4])
                        s1, s1r = tmp(); s2, s2r = tmp()
                        V("act", lambda e, s1=s1, p1=p1: e.activation(out=s1, in_=p1, func=AF.Sigmoid), [r1], [s1r])
                        V("dve", lambda e, s1=s1, p2=p2: e.tensor_tensor(out=s1, in0=s1, in1=p2, op=ALU.mult), [s1r, r2], [s1r])
                        V("act", lambda e, s2=s2, p3=p3: e.activation(out=s2, in_=p3, func=AF.Sigmoid), [r3], [s2r])
                        V("dve", lambda e, s2=s2, p4=p4: e.tensor_tensor(out=s2, in0=s2, in1=p4, op=ALU.mult), [s2r, r4], [s2r])
                        V("pool", lambda e, s1=s1, s2=s2, m=m: e.tensor_tensor(out=mg_[:, m, :], in0=s1, in1=s2, op=ALU.add), [s1r, s2r], [("mg", m)])

                if ti == 0:
                    chk(6.5)

                def ev_o(c, ps, pr):
                    V("dve", lambda e: e.scalar_tensor_tensor(out=xv[:, c, :], in0=ps, scalar=mods[:, 80 + c, ci:ci + 1], in1=xv[:, c, :], op0=ALU.mult, op1=ALU.add), [pr, ("x", c), "mods"], [("x", c)])
                linear_fm(lambda k: mg_[:, k, :], (lambda k: [("mg", k)]), 16, lambda cb: w_out.rearrange("(k p) n -> p k n", p=128)[:, :, cb * 512:(cb + 1) * 512], 4, ev_o)
                P.barrier()
                if ti == 0:
                    chk(6.7)
                xv2, hv2, rs2 = alloc_common()
                actv2 = A.bf(44 * T).rearrange("p (j t) -> p j t", j=44)
                norm_mod(xv2, hv2, rs2, 2, ci)
                ffn(1, xv2, hv2, actv2, ci)
                rstd_of(xv2, rs2)
                for k in range(16):
                    V("dve", lambda e, k=k: e.scalar_tensor_tensor(out=xv2[:, k, :], in0=xv2[:, k, :], scalar=fgs[:, k:k + 1], in1=rs2, op0=ALU.mult, op1=ALU.mult), [("x", k), "rs", "fgs"], [("x", k)])
                dst = yp if ti < 2 else ys
                t0 = g0 if ti < 2 else g0 - 1024
                P.dma("sp", lambda e: e.dma_start(out=dst.rearrange("(k p) t -> p k t", p=128)[:, :, t0:t0 + T], in_=xv2), reads=XR, writes=[("y", ti)], key="xst")

            for ti in range(6):
                phaseB(ti)
                chk(7 + ti)
        except _Stop:
            pass
        P.emit()
    return nc, P.stats


def _bias_table(rpb, flip):
    tb = np.full((64, 16, NSLOT, 64), NEG, np.float32)
    kc = np.arange(64)[:, None]
    qc = np.arange(64)[None, :]
    if not flip:
        cok = (kc >= np.clip(qc - 8, 0, 48)) & (kc <= np.clip(qc - 8, 0, 48) + 15)
        dc = kc - qc + 15
    else:
        cok = (kc >= np.clip(qc - 7, 0, 48)) & (kc <= np.clip(qc - 7, 0, 48) + 15)
        dc = qc - kc + 15
    dcc = np.clip(dc, 0, 30)
    for slot in range(NSLOT):
        edge = slot >= 11
        off = (slot - 11 - 3) if edge else (slot - 5)
        dr = (7 - off) if flip else (off + 7)
        if edge:
            valid = 0 <= dr <= 14
        else:
            valid = (-3 <= off <= 4) if flip else (-4 <= off <= 3)
        if not valid:
            continue
        g = rpb[:, dr, :][:, dcc]
        g = np.where(cok[None], g, np.float32(NEG))
        tb[:, :, slot, :] = np.transpose(g, (1, 0, 2))
    return np.ascontiguousarray(tb.reshape(64, -1))


def _pad_bc(w, c_layout):
    out = np.zeros((2, 32, 2, 64, 4, 2, 16), np.float32)
    for c in range(32):
        for gj in range(2):
            g = 2 * c + gj
            if c_layout:
                blk = np.transpose(w[:, g], (0, 2, 1))
            else:
                blk = w[:, g]
            out[:, c, gj, :, c % 4, gj, :] = blk
    return np.ascontiguousarray(out.reshape(64, 128, 128))


_CACHE = {}


def kernel(x_prompt, x_sample, cache_k, cache_v, state_ssm, c, c_ctx, w_ada, b_ada, norm_g,
           ffn_in, ffn_out, w_in, rpb, s5_lam_re, s5_lam_im, s5_log_dt, s5_b_re, s5_b_im,
           s5_c_re, s5_c_im, s5_d, w_glu, w_up_a, w_up_b, w_out, final_g):
    f = np.float32
    A_ = lambda a: np.ascontiguousarray(np.asarray(a, dtype=f))
    x_prompt, x_sample = A_(x_prompt), A_(x_sample)
    if "nc" not in _CACHE:
        _CACHE["nc"] = build_program()
    nc, stats = _CACHE["nc"]
    print("program stats (ops, eng counts, nsems, max dma sem):", stats, flush=True)

    def chunked(v, nchunk):
        return np.ascontiguousarray(np.asarray(v, f).reshape(nchunk, 128).T)

    shared = dict(
        w_ada=A_(w_ada[0]), b_ada=chunked(b_ada[0], 144),
        ng=np.ascontiguousarray(np.concatenate([chunked(norm_g[0, n], 16) for n in range(3)], axis=1)),
        fg=chunked(final_g, 16), ffn_in=A_(ffn_in[0]), ffn_out=A_(ffn_out[0]), w_in=A_(w_in[0]),
        w_glu=A_(w_glu[0]), w_up_a=A_(w_up_a[0]), w_up_b=A_(w_up_b[0]), w_out=A_(w_out[0]),
        dsk=chunked(s5_d[0], 8),
    )
    cstm = np.zeros((128, 384), f)
    cstm[:, 0:128] = np.eye(128, dtype=f)
    cstm[0:64, 128:192] = np.eye(64, dtype=f)
    cstm[0:64, 256 + 64:256 + 128] = np.eye(64, dtype=f)
    cstm[64:128, 128:192] = np.eye(64, dtype=f)
    cstm[64:128, 256 + 64:256 + 128] = np.eye(64, dtype=f)
    shared["cst"] = cstm

    def chan(a):
        a = np.asarray(a, f).reshape(2, 32, 2, 64)
        return np.ascontiguousarray(np.transpose(a, (2, 3, 0, 1)).reshape(128, 64))

    per_orient = {}
    for flip in (False, True):
        sw = (lambda a: np.asarray(a, f)[::-1]) if flip else (lambda a: np.asarray(a, f))
        ldt_full = np.broadcast_to(np.asarray(s5_log_dt[0], f)[:, :, None], (2, 64, 64))
        per_orient[flip] = dict(
            lamre=chan(sw(s5_lam_re[0])), lamim=chan(sw(s5_lam_im[0])), ldt=chan(sw(ldt_full)),
            bpr=_pad_bc(sw(s5_b_re[0]), False), bpi=_pad_bc(sw(s5_b_im[0]), False),
            cpr=_pad_bc(sw(s5_c_re[0]), True), cpi=_pad_bc(sw(s5_c_im[0]), True),
            rpbx=_bias_table(np.asarray(rpb[0], f), flip),
        )

    in_maps = []
    for core in range(8):
        flip = (core % 2 == 1)
        b = core // 2
        xpT = x_prompt[4 * core:4 * core + 4]
        xsq = x_sample[b]
        if flip:
            xpT = xpT[:, ::-1]
            xsq = xsq[::-1]
        m = dict(shared)
        m.update(per_orient[flip])
        m["xp"] = np.ascontiguousarray(xpT.reshape(1024, D).T)
        m["xo"] = np.ascontiguousarray(xsq[0:2048].T)
        m["xh"] = np.ascontiguousarray(xsq[2048:4096].T)
        cc = np.stack([np.asarray(c_ctx, f), np.asarray(c[b], f)], axis=1)
        m["cond"] = np.ascontiguousarray(cc.reshape(16, 128, 2).transpose(1, 0, 2).reshape(128, 32))
        st_ = np.asarray(state_ssm[b, 0], f)
        if flip:
            st_ = st_[::-1]
        st_ = st_.reshape(2, 2, 32, 2, 64)
        m["h0"] = np.ascontiguousarray(np.transpose(st_, (3, 4, 0, 1, 2)).reshape(128, 128))
        ck = np.asarray(cache_k[b, 0], f)
        m["ckT"] = np.ascontiguousarray(np.transpose(ck.reshape(8, 2, 512, 64), (1, 3, 0, 2)).reshape(128, 4096))
        cv = np.asarray(cache_v[b, 0], f)
        m["cvt"] = np.ascontiguousarray(np.transpose(cv.reshape(16, 4, 128, 64), (2, 1, 0, 3)).reshape(128, 4096))
        in_maps.append(m)

    if _CACHE.get("sim_hook") is not None:
        R = _CACHE["sim_hook"](nc, in_maps)
    else:
        res = run_bass_kernel_spmd(nc, in_maps, core_ids=list(range(8)))
        R = res.results
    y_prompt = np.zeros((32, 256, D), f); y_sample = np.zeros((4, 4096, D), f)
    nck = np.zeros((32, 1, 16, 256, 64), f); ncv = np.zeros((32, 1, 16, 256, 64), f)
    nss = np.zeros((32, 1, 2, 2, 64, 64), f)
    for core in range(8):
        flip = (core % 2 == 1)
        b = core // 2
        r = R[core]
        ypc = np.asarray(r["yp"]).T.reshape(4, 256, D)
        ysc = np.asarray(r["ys"]).T
        kk = np.asarray(r["nk"]).reshape(16, 64, 4, 256)
        kk = np.transpose(kk, (2, 0, 3, 1))
        vv = np.asarray(r["nv"]).reshape(4, 256, 16, 64)
        vv = np.transpose(vv, (0, 2, 1, 3))
        s_ = np.asarray(r["ns"]).reshape(2, 64, 4, 2, 2, 32)
        s_ = np.transpose(s_, (2, 3, 4, 5, 0, 1)).reshape(4, 2, 2, 64, 64)
        if flip:
            ypc = ypc[:, ::-1]
            kk = kk[:, :, ::-1]
            vv = vv[:, :, ::-1]
            s_ = s_[:, ::-1]
            y_sample[b, 2048:4096] = ysc[::-1]
        else:
            y_sample[b, 0:2048] = ysc
        y_prompt[4 * core:4 * core + 4] = ypc
        nck[4 * core:4 * core + 4, 0] = kk
        ncv[4 * core:4 * core + 4, 0] = vv
        nss[4 * core:4 * core + 4, 0] = s_
    return (y_prompt, y_sample, nck, ncv, nss)
```
